# Optimizing a Trainium2 kernel written in Bass

```python
import jax, jax.numpy as jnp
from jax import lax
import numpy as np

D_MODEL = 1024
BATCH = 8
SEQ = 2048
DEPTH = 2

D_FF = 2816
NORM_EPS = 1e-6
CONV_WIDTH = 4

LRU_WIDTH = 256
LRU_BLOCKS = 4
LRU_BLOCK = LRU_WIDTH // LRU_BLOCKS
LRU_C = 8.0

RWKV_HEADS = 6
RWKV_HEAD_DIM = 64
RWKV_WIDTH = RWKV_HEADS * RWKV_HEAD_DIM
RWKV_DECAY_LORA = 64
RWKV_AAA_LORA = 64
RWKV_MV_LORA = 32
RWKV_GATE_LORA = 128
RWKV_GN_EPS = 64e-5
RWKV_IN = 3 * RWKV_WIDTH + RWKV_DECAY_LORA + RWKV_AAA_LORA + RWKV_GATE_LORA
RWKV_SPLITS = (RWKV_WIDTH, RWKV_WIDTH, RWKV_WIDTH, RWKV_DECAY_LORA, RWKV_AAA_LORA, RWKV_GATE_LORA)

GDN_HEADS = 6
GDN_HEAD_DIM = 64
GDN_WIDTH = GDN_HEADS * GDN_HEAD_DIM
GDN_CHUNK = 64

D_MIX = LRU_WIDTH + RWKV_WIDTH + GDN_WIDTH
IN_SPLITS = (LRU_WIDTH, LRU_WIDTH, RWKV_IN, 3 * GDN_WIDTH, GDN_WIDTH, GDN_HEADS, GDN_HEADS)
D_IN = 2 * LRU_WIDTH + RWKV_IN + 4 * GDN_WIDTH + 2 * GDN_HEADS

kernel_name = 'hymba_style_rglru_rwkv7_gdn_macaron'


def split_cols(t, sizes):
    idx = np.cumsum(sizes)[:-1].tolist()
    return jnp.split(t, idx, axis=-1)


def rms_norm(x, g, eps=NORM_EPS):
    xf = x.astype(jnp.float32)
    y = xf * lax.rsqrt(jnp.mean(xf * xf, axis=-1, keepdims=True) + eps)
    return (y * g.astype(jnp.float32)).astype(x.dtype)


def l2_normalize(t, eps=1e-6):
    return t * lax.rsqrt(jnp.sum(t * t, axis=-1, keepdims=True) + eps)


def swiglu_ffn(h, wi, wo):
    gate, up = jnp.split(h @ wi, 2, axis=-1)
    return (jax.nn.silu(gate) * up) @ wo


def token_shift(t):
    return jnp.pad(t, ((0, 0), (1, 0), (0, 0)))[:, :-1]


def causal_depthwise_conv(x, w):
    c = x.shape[-1]
    return lax.conv_general_dilated(
        x.astype(jnp.float32), w.astype(jnp.float32)[:, None, :],
        window_strides=(1,), padding=((CONV_WIDTH - 1, 0),),
        dimension_numbers=('NWC', 'WIO', 'NWC'), feature_group_count=c)


def _linear_combine(left, right):
    a1, b1 = left
    a2, b2 = right
    return a1 * a2, a2 * b1 + b2


def rglru_group(p_x, p_y, conv_w, conv_b, ga_w, ga_b, gx_w, gx_b, lam, out_g):
    bsz, T, _ = p_x.shape
    xc = causal_depthwise_conv(p_x, conv_w) + conv_b.astype(jnp.float32)
    xb = xc.reshape(bsz, T, LRU_BLOCKS, LRU_BLOCK)
    r = jax.nn.sigmoid(jnp.einsum('btnc,ncd->btnd', xb, ga_w.astype(jnp.float32)) + ga_b).reshape(bsz, T, LRU_WIDTH)
    i = jax.nn.sigmoid(jnp.einsum('btnc,ncd->btnd', xb, gx_w.astype(jnp.float32)) + gx_b).reshape(bsz, T, LRU_WIDTH)
    log_a = -LRU_C * r * jax.nn.softplus(-lam.astype(jnp.float32))
    a = jnp.exp(log_a)
    mult = jnp.sqrt(-jnp.expm1(2.0 * log_a))
    first = (jnp.arange(T) == 0)[None, :, None]
    mult = jnp.where(first, 1.0, mult)
    _, hseq = lax.associative_scan(_linear_combine, (a, mult * i * xc), axis=1)
    out = hseq * jax.nn.gelu(p_y.astype(jnp.float32))
    return rms_norm(out, out_g)


def rwkv7_group(p, mu, w_up, w_bias, a_up, a_bias, g_up, k_k, k_a, r_k, ln_g, ln_b, v_first, vres):
    p = p.astype(jnp.float32)
    bsz, T, _ = p.shape
    xm = p + (token_shift(p) - p) * mu
    r, k, v, xw, xa, xg = split_cols(xm, RWKV_SPLITS)
    w_log = -jax.nn.softplus(-(w_bias + jnp.tanh(xw) @ w_up)) - 0.5
    decay = jnp.exp(-jnp.exp(w_log))
    a = jax.nn.sigmoid(a_bias + xa @ a_up)
    g = jax.nn.sigmoid(xg) @ g_up
    if vres is None:
        v_first = v
    else:
        vw1, vw2, vb = vres
        v = v + (v_first - v) * jax.nn.sigmoid(vb + (v @ vw1) @ vw2)
    heads = lambda t: t.reshape(bsz, T, RWKV_HEADS, RWKV_HEAD_DIM)
    r, k, v, decay, a = heads(r), heads(k), heads(v), heads(decay), heads(a)
    kk = l2_normalize(k * k_k.reshape(RWKV_HEADS, RWKV_HEAD_DIM))
    k = k * (1.0 + (a - 1.0) * k_a.reshape(RWKV_HEADS, RWKV_HEAD_DIM))

    def step(S, inp):
        r_t, w_t, k_t, v_t, kk_t, b_t = inp
        sa = jnp.einsum('bhvk,bhk->bhv', S, -kk_t)
        S = S * w_t[:, :, None, :] + sa[..., None] * b_t[:, :, None, :] + v_t[..., None] * k_t[:, :, None, :]
        return S, jnp.einsum('bhvk,bhk->bhv', S, r_t)

    tm = lambda t: jnp.swapaxes(t, 0, 1)
    S0 = jnp.zeros((bsz, RWKV_HEADS, RWKV_HEAD_DIM, RWKV_HEAD_DIM), jnp.float32)
    _, y = lax.scan(step, S0, (tm(r), tm(decay), tm(k), tm(v), tm(kk), tm(kk * a)))
    y = tm(y)
    mean = jnp.mean(y, axis=-1, keepdims=True)
    var = jnp.mean(jnp.square(y - mean), axis=-1, keepdims=True)
    y = ((y - mean) * lax.rsqrt(var + RWKV_GN_EPS)).reshape(bsz, T, RWKV_WIDTH) * ln_g + ln_b
    bonus = jnp.sum(r * k * r_k, axis=-1, keepdims=True) * v
    y = (y + bonus.reshape(bsz, T, RWKV_WIDTH)) * g
    return y, v_first


def gated_delta_rule_chunked(q, k, v, g, beta):
    bsz, T, H, D = q.shape
    nc = T // GDN_CHUNK

    def to_chunks(t):
        return t.reshape(bsz, nc, GDN_CHUNK, H, -1).transpose(0, 3, 1, 2, 4)

    q, k, v = to_chunks(q), to_chunks(k), to_chunks(v)
    g = to_chunks(g[..., None])[..., 0]
    beta = to_chunks(beta[..., None])[..., 0]
    gc = jnp.cumsum(g, axis=-1)
    idx = jnp.arange(GDN_CHUNK)
    causal = idx[:, None] >= idx[None, :]
    strict = idx[:, None] > idx[None, :]
    diff = gc[..., :, None] - gc[..., None, :]
    decay = jnp.where(causal, jnp.exp(jnp.where(causal, diff, 0.0)), 0.0)
    k_beta = k * beta[..., None]
    lower = jnp.where(strict, jnp.einsum('bhncd,bhnsd->bhncs', k_beta, k) * decay, 0.0)
    eye = jnp.eye(GDN_CHUNK, dtype=lower.dtype)
    rhs = jnp.concatenate([v * beta[..., None], k_beta * jnp.exp(gc)[..., None]], axis=-1)
    sol = lax.linalg.triangular_solve(lower + eye, rhs, left_side=True, lower=True, unit_diagonal=True)
    u, w = sol[..., :D], sol[..., D:]
    qk = jnp.where(causal, jnp.einsum('bhncd,bhnsd->bhncs', q, k) * decay, 0.0)
    q_dec = q * jnp.exp(gc)[..., None]
    k_dec = k * jnp.exp(gc[..., -1:] - gc)[..., None]
    g_last = jnp.exp(gc[..., -1])

    def step(S, inp):
        q_i, k_i, u_i, w_i, qk_i, gl_i = inp
        v_new = u_i - jnp.einsum('bhcd,bhde->bhce', w_i, S)
        o = jnp.einsum('bhcd,bhde->bhce', q_i, S) + jnp.einsum('bhcs,bhse->bhce', qk_i, v_new)
        S = S * gl_i[..., None, None] + jnp.einsum('bhcd,bhce->bhde', k_i, v_new)
        return S, o

    mv = lambda t: jnp.moveaxis(t, 2, 0)
    S0 = jnp.zeros((bsz, H, D, v.shape[-1]), jnp.float32)
    _, o = lax.scan(step, S0, (mv(q_dec), mv(k_dec), mv(u), mv(w), mv(qk), mv(g_last)))
    return o.transpose(1, 0, 3, 2, 4).reshape(bsz, T, H, -1)


def gdn_group(p_qkv, p_z, p_alpha, p_beta, conv_w, a_log, dt_bias, norm_g):
    bsz, T, _ = p_qkv.shape
    qkv = jax.nn.silu(causal_depthwise_conv(p_qkv, conv_w))
    heads = lambda t: t.reshape(bsz, T, GDN_HEADS, GDN_HEAD_DIM)
    q, k, v = [heads(t) for t in jnp.split(qkv, 3, axis=-1)]
    q = l2_normalize(q) * (GDN_HEAD_DIM ** -0.5)
    k = l2_normalize(k)
    g = -jnp.exp(a_log.astype(jnp.float32)) * jax.nn.softplus(p_alpha.astype(jnp.float32) + dt_bias)
    beta = jax.nn.sigmoid(p_beta.astype(jnp.float32))
    o = gated_delta_rule_chunked(q, k, v, g, beta)
    o = rms_norm(o, norm_g) * jax.nn.silu(heads(p_z.astype(jnp.float32)))
    return o.reshape(bsz, T, GDN_WIDTH)


def setup_inputs(seed: int = 0) -> dict:
    key = jax.random.key(seed)
    keys = jax.random.split(key, 64)
    counter = [0]

    def nk():
        counter[0] += 1
        return keys[counter[0] - 1]

    def nrm(shape, scale):
        return jax.random.normal(nk(), shape, jnp.float32) * scale

    def gain(shape):
        return 1.0 + nrm(shape, 0.02)

    def unif(shape, lo, hi):
        return jax.random.uniform(nk(), shape, jnp.float32, lo, hi)

    L = DEPTH
    x = nrm((BATCH, SEQ, D_MODEL), 1.0)
    ffn1_norm = gain((L, D_MODEL))
    ffn1_wi = nrm((L, D_MODEL, 2 * D_FF), D_MODEL ** -0.5)
    ffn1_wo = nrm((L, D_FF, D_MODEL), D_FF ** -0.5)
    mix_norm = gain((L, D_MODEL))
    w_in = nrm((L, D_MODEL, D_IN), D_MODEL ** -0.5)
    w_out = nrm((L, D_MIX, D_MODEL), D_MIX ** -0.5)
    lru_conv_w = nrm((L, CONV_WIDTH, LRU_WIDTH), 0.5)
    lru_conv_b = nrm((L, LRU_WIDTH), 0.02)
    lru_gate_a_w = nrm((L, LRU_BLOCKS, LRU_BLOCK, LRU_BLOCK), LRU_BLOCK ** -0.5)
    lru_gate_a_b = nrm((L, LRU_BLOCKS, LRU_BLOCK), 0.02)
    lru_gate_x_w = nrm((L, LRU_BLOCKS, LRU_BLOCK, LRU_BLOCK), LRU_BLOCK ** -0.5)
    lru_gate_x_b = nrm((L, LRU_BLOCKS, LRU_BLOCK), 0.02)
    s = unif((L, LRU_WIDTH), 0.9, 0.999) ** (1.0 / LRU_C)
    lru_lambda = jnp.log(s) - jnp.log1p(-s)
    lru_out_norm = gain((L, LRU_WIDTH))
    rwkv_mu = unif((L, RWKV_IN), 0.0, 1.0)
    rwkv_w_up = nrm((L, RWKV_DECAY_LORA, RWKV_WIDTH), 0.1)
    rwkv_w_bias = unif((L, RWKV_WIDTH), -6.5, -1.5)
    rwkv_a_up = nrm((L, RWKV_AAA_LORA, RWKV_WIDTH), 0.5 * RWKV_AAA_LORA ** -0.5)
    rwkv_a_bias = nrm((L, RWKV_WIDTH), 0.1)
    rwkv_g_up = nrm((L, RWKV_GATE_LORA, RWKV_WIDTH), RWKV_GATE_LORA ** -0.5)
    rwkv_k_k = 0.85 + nrm((L, RWKV_WIDTH), 0.02)
    rwkv_k_a = gain((L, RWKV_WIDTH))
    rwkv_r_k = nrm((L, RWKV_HEADS, RWKV_HEAD_DIM), 0.1)
    rwkv_ln_g = gain((L, RWKV_WIDTH))
    rwkv_ln_b = nrm((L, RWKV_WIDTH), 0.02)
    rwkv_vres_w1 = nrm((L - 1, RWKV_WIDTH, RWKV_MV_LORA), RWKV_WIDTH ** -0.5)
    rwkv_vres_w2 = nrm((L - 1, RWKV_MV_LORA, RWKV_WIDTH), 0.5 * RWKV_MV_LORA ** -0.5)
    rwkv_vres_b = 1.0 + nrm((L - 1, RWKV_WIDTH), 0.1)
    gdn_conv_w = nrm((L, CONV_WIDTH, 3 * GDN_WIDTH), 0.5)
    gdn_a_log = jnp.log(unif((L, GDN_HEADS), 1.0, 16.0))
    dt = jnp.exp(unif((L, GDN_HEADS), float(np.log(1e-3)), float(np.log(1e-1))))
    gdn_dt_bias = dt + jnp.log(-jnp.expm1(-dt))
    gdn_norm = gain((L, GDN_HEAD_DIM))
    ffn2_norm = gain((L, D_MODEL))
    ffn2_wi = nrm((L, D_MODEL, 2 * D_FF), D_MODEL ** -0.5)
    ffn2_wo = nrm((L, D_FF, D_MODEL), D_FF ** -0.5)
    final_norm = gain((D_MODEL,))
    return {
        'x': x,
        'ffn1_norm': ffn1_norm, 'ffn1_wi': ffn1_wi, 'ffn1_wo': ffn1_wo,
        'mix_norm': mix_norm, 'w_in': w_in, 'w_out': w_out,
        'lru_conv_w': lru_conv_w, 'lru_conv_b': lru_conv_b,
        'lru_gate_a_w': lru_gate_a_w, 'lru_gate_a_b': lru_gate_a_b,
        'lru_gate_x_w': lru_gate_x_w, 'lru_gate_x_b': lru_gate_x_b,
        'lru_lambda': lru_lambda, 'lru_out_norm': lru_out_norm,
        'rwkv_mu': rwkv_mu, 'rwkv_w_up': rwkv_w_up, 'rwkv_w_bias': rwkv_w_bias,
        'rwkv_a_up': rwkv_a_up, 'rwkv_a_bias': rwkv_a_bias, 'rwkv_g_up': rwkv_g_up,
        'rwkv_k_k': rwkv_k_k, 'rwkv_k_a': rwkv_k_a, 'rwkv_r_k': rwkv_r_k,
        'rwkv_ln_g': rwkv_ln_g, 'rwkv_ln_b': rwkv_ln_b,
        'rwkv_vres_w1': rwkv_vres_w1, 'rwkv_vres_w2': rwkv_vres_w2, 'rwkv_vres_b': rwkv_vres_b,
        'gdn_conv_w': gdn_conv_w, 'gdn_a_log': gdn_a_log, 'gdn_dt_bias': gdn_dt_bias, 'gdn_norm': gdn_norm,
        'ffn2_norm': ffn2_norm, 'ffn2_wi': ffn2_wi, 'ffn2_wo': ffn2_wo,
        'final_norm': final_norm,
    }


def reference(x, ffn1_norm, ffn1_wi, ffn1_wo, mix_norm, w_in, w_out,
              lru_conv_w, lru_conv_b, lru_gate_a_w, lru_gate_a_b, lru_gate_x_w, lru_gate_x_b,
              lru_lambda, lru_out_norm,
              rwkv_mu, rwkv_w_up, rwkv_w_bias, rwkv_a_up, rwkv_a_bias, rwkv_g_up,
              rwkv_k_k, rwkv_k_a, rwkv_r_k, rwkv_ln_g, rwkv_ln_b,
              rwkv_vres_w1, rwkv_vres_w2, rwkv_vres_b,
              gdn_conv_w, gdn_a_log, gdn_dt_bias, gdn_norm,
              ffn2_norm, ffn2_wi, ffn2_wo, final_norm):
    dt = x.dtype
    v_first = None
    for l in range(DEPTH):
        x = x + 0.5 * swiglu_ffn(rms_norm(x, ffn1_norm[l]), ffn1_wi[l], ffn1_wo[l])
        h = rms_norm(x, mix_norm[l])
        p_lx, p_ly, p_rwkv, p_qkv, p_z, p_alpha, p_beta = split_cols(h @ w_in[l], IN_SPLITS)
        y_lru = rglru_group(p_lx, p_ly, lru_conv_w[l], lru_conv_b[l], lru_gate_a_w[l], lru_gate_a_b[l],
                            lru_gate_x_w[l], lru_gate_x_b[l], lru_lambda[l], lru_out_norm[l])
        vres = None if l == 0 else (rwkv_vres_w1[l - 1], rwkv_vres_w2[l - 1], rwkv_vres_b[l - 1])
        y_rwkv, v_first = rwkv7_group(p_rwkv, rwkv_mu[l], rwkv_w_up[l], rwkv_w_bias[l], rwkv_a_up[l],
                                      rwkv_a_bias[l], rwkv_g_up[l], rwkv_k_k[l], rwkv_k_a[l], rwkv_r_k[l],
                                      rwkv_ln_g[l], rwkv_ln_b[l], v_first, vres)
        y_gdn = gdn_group(p_qkv, p_z, p_alpha, p_beta, gdn_conv_w[l], gdn_a_log[l], gdn_dt_bias[l], gdn_norm[l])
        mixed = jnp.concatenate([y_lru, y_rwkv, y_gdn], axis=-1).astype(dt)
        x = x + mixed @ w_out[l]
        x = x + 0.5 * swiglu_ffn(rms_norm(x, ffn2_norm[l]), ffn2_wi[l], ffn2_wo[l])
    return rms_norm(x, final_norm)
```

```python
import os
import numpy as np
import concourse.bass as bass
import concourse.mybir as mybir
from concourse.bass_utils import run_bass_kernel_spmd

F32 = mybir.dt.float32
BF16 = mybir.dt.bfloat16
AF = mybir.ActivationFunctionType
ALU = mybir.AluOpType

D = 1024
T = 2048
L = 2
DFF = 2816
NFC = DFF // 128
KC = D // 128
TB = 512
NB = T // TB
EPS = 1e-6
DIN = 3468

WRITE_KEYS = ("out", "accum_out", "ap")


class Dep:
    __slots__ = ("w", "r", "excl")

    def __init__(self, excl=False):
        self.w = None
        self.r = {}
        self.excl = excl


class V:
    __slots__ = ("ap", "dep")

    def __init__(self, ap, dep=None):
        self.ap = ap
        self.dep = dep if dep is not None else Dep()

    def __getitem__(self, idx):
        return V(self.ap[idx], self.dep)

    def re(self, s, **kw):
        return V(self.ap.rearrange(s, **kw), self.dep)

    def bc(self, dt):
        return V(self.ap.bitcast(dt), self.dep)


class KB:
    def __init__(self, nc):
        self.nc = nc
        self.E = {"pe": nc.tensor, "act": nc.scalar, "dve": nc.vector, "pool": nc.gpsimd, "sp": nc.sync}
        self.sems = {}
        self.ecnt = {}
        self.waited = {e: {} for e in self.E}
        for e in self.E:
            self.sems[("e", e)] = nc.alloc_semaphore("s_" + e)
            self.ecnt[e] = 0
        self.dma_n = {"sp": 24, "pool": 12, "act": 4}
        self.dma_rr = {q: 0 for q in self.dma_n}
        self.dma_val = {}
        for q, n in self.dma_n.items():
            for i in range(n):
                self.sems[("d", q, i)] = nc.alloc_semaphore("d_%s%d" % (q, i))
                self.dma_val[("d", q, i)] = 0
        self.all_dma_toks = []
        self.sb_off = 0

    def _wait(self, eng, tok):
        if tok is None:
            return
        key, val = tok
        if val <= 0 or self.waited[eng].get(key, 0) >= val:
            return
        self.E[eng].wait_ge(self.sems[key], val)
        self.waited[eng][key] = val

    def _deps(self, eng, reads, writes):
        mykey = ("e", eng)
        skip_self = eng == "pe"
        for v in reads:
            w = v.dep.w
            if w is not None and not (skip_self and w[0] == mykey):
                self._wait(eng, w)
            if v.dep.excl:
                for key, val in v.dep.r.items():
                    if key != mykey:
                        self._wait(eng, (key, val))
        for v in writes:
            d = v.dep
            if d.w is not None and not (skip_self and d.w[0] == mykey):
                self._wait(eng, d.w)
            for key, val in d.r.items():
                if skip_self and key == mykey:
                    continue
                self._wait(eng, (key, val))

    def _mark(self, tok, reads, writes):
        key, val = tok
        for v in reads:
            if v.dep.r.get(key, 0) < val:
                v.dep.r[key] = val
        for v in writes:
            v.dep.w = tok
            v.dep.r = {}

    def I(self, eng, fname, **kw):
        reads, writes, real = [], [], {}
        for k, v in kw.items():
            if isinstance(v, V):
                (writes if k in WRITE_KEYS else reads).append(v)
                real[k] = v.ap
            else:
                real[k] = v
        self._deps(eng, reads, writes)
        ins = getattr(self.E[eng], fname)(**real)
        self.ecnt[eng] += 1
        ins.then_inc(self.sems[("e", eng)], 1)
        self._mark((("e", eng), self.ecnt[eng]), reads, writes)
        return ins

    def dma(self, q, out, in_, **kw):
        self._deps(q, [in_], [out])
        idx = self.dma_rr[q] % self.dma_n[q]
        self.dma_rr[q] += 1
        key = ("d", q, idx)
        prev = self.dma_val[key]
        self._wait(q, (key, prev))
        ins = self.E[q].dma_start(out=out.ap, in_=in_.ap, **kw)
        ins.then_inc(self.sems[key], 16)
        self.dma_val[key] = prev + 16
        tok = (key, prev + 16)
        self._mark(tok, [in_], [out])
        return tok

    def barrier(self, engines=("pe", "act", "dve", "pool", "sp")):
        toks = [(("e", e), self.ecnt[e]) for e in self.E]
        toks += [(k, v) for k, v in self.dma_val.items()]
        for e in engines:
            for t in toks:
                if t[0] == ("e", e):
                    continue
                self._wait(e, t)

    def mm(self, out, lhsT, rhs, start=True, stop=True):
        return self.I("pe", "matmul", out=out, lhsT=lhsT, rhs=rhs, start=start, stop=stop)

    def tr(self, out, in_, ident):
        return self.I("pe", "transpose", out=out, in_=in_, identity=ident)

    def act(self, out, in_, func, bias=None, scale=None, accum_out=None):
        kw = {}
        if bias is not None:
            kw["bias"] = bias
        if scale is not None:
            kw["scale"] = scale
        if accum_out is not None:
            kw["accum_out"] = accum_out
        return self.I("act", "activation", out=out, in_=in_, func=func, **kw)

    def tt(self, eng, out, in0, in1, op):
        return self.I(eng, "tensor_tensor", out=out, in0=in0, in1=in1, op=op)

    def ts(self, eng, out, in0, s1, op0, s2=None, op1=None, accum_out=None):
        kw = {}
        if op1 is not None:
            kw["op1"] = op1
        if accum_out is not None:
            kw["accum_out"] = accum_out
        return self.I(eng, "tensor_scalar", out=out, in0=in0, scalar1=s1, scalar2=s2, op0=op0, **kw)

    def stt(self, out, in0, scalar, in1, op0, op1):
        return self.I("dve", "scalar_tensor_tensor", out=out, in0=in0, scalar=scalar, in1=in1, op0=op0, op1=op1)

    def copy(self, eng, out, in_):
        if eng == "act":
            return self.I("act", "activation", out=out, in_=in_, func=AF.Copy)
        return self.I(eng, "tensor_copy", out=out, in_=in_)

    def memset(self, eng, ap, val):
        return self.I(eng, "memset", ap=ap, constant=val)

    def sb(self, name, shape, dtype):
        return V(self.nc.alloc_sbuf_tensor(name, list(shape), dtype)[:])


def _cols_layout():
    m = {}
    n = 0

    def add(name, k):
        nonlocal n
        m[name] = n
        n += k
    add("ffn1_norm", 8)
    add("mix_norm", 8)
    add("ffn2_norm", 8)
    add("final_norm", 8)
    add("lru_conv_w", 8)
    add("lru_conv_b", 2)
    add("lru_ga_b", 2)
    add("lru_gx_b", 2)
    add("lru_lam", 2)
    add("lru_out_g", 2)
    add("rw_mu", 11)
    add("rw_wbias", 3)
    add("rw_abias", 3)
    add("rw_kk", 3)
    add("rw_ka", 3)
    add("rw_rk", 3)
    add("rw_lng", 3)
    add("rw_lnb", 3)
    add("rw_vresb", 3)
    add("gd_conv_w", 36)
    m["_n"] = n
    return m


PCOL = _cols_layout()


C0 = float(np.exp(-0.5))
GELU_C = 0.7978845608028654
CST = {"ident": 0, "ones": 128, "mu_s": 256, "mu_i": 384, "ml_s": 512, "bd": 640, "rst": 768, "sel": 1280}
CST_N = 1280 + 384
PROW = {"gd_dt": 0, "gd_alog": 6, "gd_norm": 12}
PROW_N = 12 + 128


class Bump:
    def __init__(self, carve, base, end):
        self.carve, self.off, self.end = carve, base, end

    def _a(self, n, esz, dtype):
        nbytes = (n * esz + 31) // 32 * 32
        v = self.carve(self.off, nbytes, dtype)
        self.off += nbytes
        assert self.off <= self.end, ("arena overflow", self.off, self.end)
        return v[:, 0:n]

    def f32(self, n):
        return self._a(n, 4, F32)

    def b16(self, n):
        return self._a(n, 2, BF16)


def build(nc, dbg=None):
    dbg = dbg or {}
    stages = dbg.get("stages", "all")
    Tn = dbg.get("T", T)
    NBn = Tn // TB
    Ln = dbg.get("L", L)
    kb = KB(nc)

    def din(name, shape):
        return V(nc.dram_tensor(name, list(shape), F32, kind="ExternalInput").ap())

    x_d = din("x", [Tn, D])
    need_ffn = stages in ("all", "ffn1")
    need_mix = stages in ("all", "lru", "rwkv", "gdn", "mix")
    if need_ffn:
        wi_d = [din("ffn1_wi", [L, D, 2 * DFF]), din("ffn2_wi", [L, D, 2 * DFF])]
        wo_d = [din("ffn1_wo", [L, DFF, D]), din("ffn2_wo", [L, DFF, D])]
    if need_mix:
        w_in_d = din("w_in", [L, D, DIN])
        w_out_d = din("w_out", [L, D, D])
        lru_g_d = din("lru_g", [L, 128, 4, 128])
        rw_wa_d = din("rw_wa", [L, 128, 384])
        rw_gup_d = din("rwkv_g_up", [L, 128, 384])
        vw1_d = din("rwkv_vres_w1", [L - 1, 384, 32])
        vw2_d = din("rwkv_vres_w2", [L - 1, 32, 384])
        prow_d = din("prow", [L, 128, PROW_N])
    pcol_d = din("pcol", [L, 128, PCOL["_n"]])
    cst_d = din("cst", [128, CST_N])
    y_d = V(nc.dram_tensor("y", [Tn, D], F32, kind="ExternalOutput").ap())

    XT = [[kb.sb("xt%d_%d" % (c, b), [128, TB], F32) for b in range(NBn)] for c in range(KC)]
    HT = [kb.sb("ht%d" % b, [128, KC, TB], BF16) for b in range(NBn)]
    VF = [[kb.sb("vf%d_%d" % (hp, b), [128, TB], BF16) for b in range(NBn)] for hp in range(3)]
    cst = kb.sb("cst_sb", [128, CST_N], F32)
    ident = cst[:, 0:128]
    ones_f = cst[:, 128:256]
    ones_b = kb.sb("ones_b", [128, 128], BF16)
    ident_b = kb.sb("ident_b", [128, 128], BF16)
    bd_b = kb.sb("bd_b", [128, 128], BF16)
    pcol = [kb.sb("pcol%d" % l, [128, PCOL["_n"]], F32) for l in range(L)]
    if need_mix:
        prow = [kb.sb("prow%d" % l, [128, PROW_N], F32) for l in range(L)]
    ps = [V(nc.alloc_psum_tensor("ps%d" % i, [128, 512], F32)[:], Dep(excl=True)) for i in range(8)]

    ARENA = 88 * 1024
    arena_h = nc.alloc_sbuf_tensor("arena", [128, ARENA // 4], F32)

    def carve(off, nbytes, dtype, dep=None):
        assert off % 4 == 0 and nbytes % 4 == 0 and off + nbytes <= ARENA, (off, nbytes)
        v = V(arena_h[:][:, off // 4:(off + nbytes) // 4], dep)
        return v.bc(dtype) if dtype != F32 else v

    kb.dma("sp", cst, cst_d)
    kb.copy("dve", ones_b, ones_f)
    kb.copy("dve", ident_b, ident)
    kb.copy("dve", bd_b, cst[:, CST["bd"]:CST["bd"] + 128])
    for l in range(L):
        kb.dma("sp", pcol[l], pcol_d[l])
        if need_mix:
            kb.dma("sp", prow[l], prow_d[l])

    def pc(l, name, j=0):
        c = PCOL[name] + j
        return pcol[l][:, c:c + 1]

    NFMAX = 5
    off = 0
    WI = [carve(off + i * 20480, 20480, BF16).re("p (k n) -> p k n", k=KC) for i in range(2)]
    off += 2 * 20480
    WO = [carve(off + i * 10240, 10240, BF16).re("p (f n) -> p f n", f=NFMAX) for i in range(2)]
    off += 2 * 10240
    ATb = [carve(off + i * 5120, 5120, BF16).re("p (f n) -> p f n", f=NFMAX) for i in range(2)]
    off += 2 * 5120
    SG = [carve(off + i * 2048, 2048, F32) for i in range(2)]
    off += 2 * 2048
    SQ = carve(off, 8192, BF16).re("p (k n) -> p k n", k=KC)
    off += 8192
    RS = [carve(off + i * 2048, 2048, F32) for i in range(2)]
    off += 2 * 2048
    assert off <= ARENA
    UPO = 640
    XS = [carve(i * 4096, 4096, F32) for i in range(4)]

    psi = [0]
    ps_b16 = [p.bc(BF16) for p in ps]

    def next_ps():
        p = ps[psi[0] % 8]
        psi[0] += 1
        return p

    def next_ps_b():
        p = ps_b16[psi[0] % 8]
        psi[0] += 1
        return p

    for tt_ in range(Tn // 128):
        xs = XS[tt_ % 4]
        kb.dma("sp", xs, x_d[tt_ * 128:(tt_ + 1) * 128, :])
        b, o = divmod(tt_ * 128, TB)
        for half in range(2):
            p = next_ps()
            for j in range(4):
                c = half * 4 + j
                kb.tr(p[:, j * 128:(j + 1) * 128], xs[:, c * 128:(c + 1) * 128], ident)
            for j in range(4):
                c = half * 4 + j
                kb.copy("act" if half else "dve", XT[c][b][:, o:o + 128], p[:, j * 128:(j + 1) * 128])
    kb.barrier()

    def rmsnorm(l, gname, out_fn, SQ=SQ, RS=RS):
        for b in range(NBn):
            for c in range(KC):
                kb.act(SQ[:, c, :], XT[c][b], AF.Square)
            p = next_ps()
            for c in range(KC):
                kb.mm(p, ones_b, SQ[:, c, :], start=(c == 0), stop=(c == KC - 1))
            rs = RS[b % 2]
            kb.act(rs, p, AF.Sqrt, bias=EPS, scale=1.0 / D)
            kb.I("dve", "reciprocal", out=rs, in_=rs)
            for c in range(KC):
                out_fn(b, c, rs, pc(l, gname, c))

    def h_out(b, c, rs, g):
        kb.stt(HT[b][:, c, :], XT[c][b], g, rs, ALU.mult, ALU.mult)

    groups = [(0, 5), (5, 5), (10, 4), (14, 4), (18, 4)]

    def ffn_load(l, which, gi):
        f0, nf = groups[gi]
        buf = gi % 2
        wi = wi_d[which]
        wo = wo_d[which]
        for k in range(KC):
            kb.dma("pool", WI[buf][:, k, 0:nf * 128], wi[l, k * 128:(k + 1) * 128, f0 * 128:(f0 + nf) * 128])
            kb.dma("pool", WI[buf][:, k, UPO:UPO + nf * 128],
                   wi[l, k * 128:(k + 1) * 128, DFF + f0 * 128:DFF + (f0 + nf) * 128])
        for f in range(nf):
            kb.dma("pool", WO[buf][:, f, :], wo[l, (f0 + f) * 128:(f0 + f + 1) * 128, :])

    def ffn(l, which):
        ffn_load(l, which, 0)
        rmsnorm(l, "ffn1_norm" if which == 0 else "ffn2_norm", h_out)
        it = 0
        for gi, (f0, nf) in enumerate(groups):
            buf = gi % 2
            if gi + 1 < len(groups):
                ffn_load(l, which, gi + 1)
            for b in range(NBn):
                at = ATb[it % 2]
                for f in range(nf):
                    pg = next_ps()
                    pu = next_ps()
                    for k in range(KC):
                        kb.mm(pg, WI[buf][:, k, f * 128:(f + 1) * 128], HT[b][:, k, :], start=(k == 0), stop=(k == KC - 1))
                    for k in range(KC):
                        kb.mm(pu, WI[buf][:, k, UPO + f * 128:UPO + (f + 1) * 128], HT[b][:, k, :],
                              start=(k == 0), stop=(k == KC - 1))
                    sg = SG[f % 2]
                    kb.act(sg, pg, AF.Silu)
                    kb.tt("dve", at[:, f, :], sg, pu, ALU.mult)
                for c in range(KC):
                    po = next_ps()
                    for f in range(nf):
                        kb.mm(po, WO[buf][:, f, c * 128:(c + 1) * 128], at[:, f, :], start=(f == 0), stop=(f == nf - 1))
                    kb.stt(XT[c][b], po, 0.5, XT[c][b], ALU.mult, ALU.add)
                it += 1

    MIX_BASE = 0
    if need_mix:
        mo = 0
        WIN = [carve(mo + k * 3104, 3104, BF16) for k in range(KC)]
        mo += KC * 3104
        WOUT = carve(mo, 6144, BF16).re("p (c n) -> p c n", c=3)
        mo += 6144
        MIX_BASE = mo

    def proj(b, col0, ncols, out_ps):
        for k in range(KC):
            kb.mm(out_ps, WIN[k][:, col0:col0 + ncols], HT[b][:, k, :], start=(k == 0), stop=(k == KC - 1))

    def wout_apply(b, nchunks, ysrc):
        for d in range(KC):
            po = next_ps()
            for c in range(nchunks):
                kb.mm(po, WOUT[:, c, d * 128:(d + 1) * 128], ysrc(c), start=(c == 0), stop=(c == nchunks - 1))
            kb.tt("dve", XT[d][b], po, XT[d][b], ALU.add)

    def load_win(l, c0, n):
        for k in range(KC):
            kb.dma("pool", WIN[k][:, 0:n], w_in_d[l, k * 128:(k + 1) * 128, c0:c0 + n])

    def mixer_lru(l):
        al = Bump(carve, MIX_BASE, ARENA)
        LG = al.b16(512).re("p (g n) -> p g n", g=4)
        load_win(l, 0, 512)
        kb.dma("pool", LG, lru_g_d[l])
        for c in range(2):
            kb.dma("pool", WOUT[:, c, :], w_out_d[l, c * 128:(c + 1) * 128, :])
        cols = al.f32(8)
        PXB = [al.f32(516) for _ in range(2)]
        PY = [al.f32(512) for _ in range(2)]
        XC = [al.f32(512) for _ in range(2)]
        XCb = [al.b16(512) for _ in range(2)]
        Rg = [al.f32(512) for _ in range(2)]
        Ig = [al.f32(512) for _ in range(2)]
        Ag = [al.f32(512) for _ in range(2)]
        A2 = [al.f32(512) for _ in range(2)]
        Hh = [al.f32(512) for _ in range(2)]
        Ut = [al.f32(512) for _ in range(2)]
        SQl = [al.b16(512) for _ in range(2)]
        YL = [al.b16(512) for _ in range(2)]
        RSl = al.f32(512)
        for c in range(2):
            t = cols[:, c:c + 1]
            kb.act(t, pc(l, "lru_lam", c), AF.Exp, scale=-1.0)
            kb.act(t, t, AF.Ln, bias=1.0)
            kb.ts("dve", cols[:, 2 + c:3 + c], t, -16.0, ALU.mult)
            kb.ts("dve", t, t, -8.0, ALU.mult)
            kb.memset("dve", cols[:, 4 + c:5 + c], 0.0)
            kb.memset("dve", PXB[c][:, 0:3], 0.0)
        for b in range(NBn):
            for c in range(2):
                px = next_ps()
                proj(b, c * 128, 128, px)
                kb.copy("act", PXB[c][:, 3:515], px)
                py = next_ps()
                proj(b, 256 + c * 128, 128, py)
                kb.copy("act", PY[c], py)
                xc = XC[c]
                kb.ts("dve", xc, PXB[c][:, 3:515], pc(l, "lru_conv_w", 3 * 2 + c), ALU.mult,
                      s2=pc(l, "lru_conv_b", c), op1=ALU.add)
                for k in range(3):
                    kb.stt(xc, PXB[c][:, k:k + 512], pc(l, "lru_conv_w", k * 2 + c), xc, ALU.mult, ALU.add)
                kb.copy("dve", PXB[c][:, 0:3], PXB[c][:, 512:515])
                kb.copy("act", XCb[c], xc)
                pr = next_ps()
                kb.mm(pr, LG[:, c, :], XCb[c])
                kb.act(Rg[c], pr, AF.Sigmoid, bias=pc(l, "lru_ga_b", c))
                pi_ = next_ps()
                kb.mm(pi_, LG[:, 2 + c, :], XCb[c])
                kb.act(Ig[c], pi_, AF.Sigmoid, bias=pc(l, "lru_gx_b", c))
                kb.act(Ag[c], Rg[c], AF.Exp, scale=cols[:, c:c + 1])
                kb.act(A2[c], Rg[c], AF.Exp, scale=cols[:, 2 + c:3 + c])
                kb.act(A2[c], A2[c], AF.Sqrt, scale=-1.0, bias=1.0)
                if b == 0:
                    kb.memset("dve", A2[c][:, 0:1], 1.0)
                kb.tt("dve", Ig[c], Ig[c], xc, ALU.mult)
                kb.tt("dve", Ig[c], Ig[c], A2[c], ALU.mult)
                kb.I("dve", "tensor_tensor_scan", out=Hh[c], data0=Ag[c], data1=Ig[c],
                     initial=cols[:, 4 + c:5 + c], op0=ALU.mult, op1=ALU.add)
                kb.copy("act", cols[:, 4 + c:5 + c], Hh[c][:, 511:512])
                u = Ut[c]
                kb.act(u, PY[c], AF.Square)
                kb.ts("dve", u, u, 0.044715, ALU.mult, s2=1.0, op1=ALU.add)
                kb.tt("dve", u, u, PY[c], ALU.mult)
                kb.act(u, u, AF.Sigmoid, scale=2.0 * GELU_C)
                kb.tt("dve", u, u, PY[c], ALU.mult)
                kb.tt("dve", Hh[c], Hh[c], u, ALU.mult)
                kb.act(SQl[c], Hh[c], AF.Square)
            pss = next_ps()
            for c in range(2):
                kb.mm(pss, ones_b, SQl[c], start=(c == 0), stop=(c == 1))
            kb.act(RSl, pss, AF.Sqrt, bias=EPS, scale=1.0 / 256)
            kb.I("dve", "reciprocal", out=RSl, in_=RSl)
            for c in range(2):
                kb.stt(YL[c], Hh[c], pc(l, "lru_out_g", c), RSl, ALU.mult, ALU.mult)
            wout_apply(b, 2, lambda c: YL[c])


    def tchain(items, f32=False):
        for it in items:
            kb.tt("dve", it["Sf"], it["Y0"], ident, ALU.add)
            if not f32:
                kb.copy("act", it["Sb"], it["Sf"])
            it["Yk"], it["Ak"] = it["Y0"], it["A1"]
        for lev in range(6):
            for it in items:
                pa = next_ps()
                kb.mm(pa[:, 0:128], it["Yk"], it["Ak"])
                An = it["Ap"][lev % 2]
                kb.copy("act", An, pa[:, 0:128])
                if lev < 5:
                    py = next_ps()
                    kb.mm(py[:, 0:128], it["Ak"], it["Yk"])
                    Yn = it["Yp"][lev % 2]
                    kb.copy("dve", Yn, py[:, 0:128])
                else:
                    Yn = None
                it["Yk"], it["Ak"] = Yn, An
            for it in items:
                psn = next_ps()
                kb.mm(psn[:, 0:128], it["Ak"], it["Sf"] if f32 else it["Sb"])
                kb.tt("dve", it["Sf"], psn[:, 0:128], it["Sf"], ALU.add)
                if not f32:
                    kb.copy("act", it["Sb"], it["Sf"])

    def mixer_gdn(l):
        al = Bump(carve, MIX_BASE, ARENA)
        load_win(l, 1920, 1548)
        for c in range(3):
            kb.dma("pool", WOUT[:, c, :], w_out_d[l, 640 + c * 128:640 + (c + 1) * 128, :])
        NEA = al.f32(8)
        kb.act(NEA[:, 0:6], prow[l][:, 6:12], AF.Exp)
        kb.ts("dve", NEA[:, 0:6], NEA[:, 0:6], -1.0, ALU.mult)
        NMU = al.f32(128)
        kb.ts("dve", NMU, cst[:, CST["mu_s"]:CST["mu_s"] + 128], -1.0, ALU.mult)
        mu_i = cst[:, CST["mu_i"]:CST["mu_i"] + 128]
        ml_s = cst[:, CST["ml_s"]:CST["ml_s"] + 128]
        GNR = prow[l][:, 12:140]
        Mf = [al.f32(64) for _ in range(3)]
        Mb = [al.b16(64) for _ in range(3)]
        TQ = al.f32(27).re("p (j k) -> p j k", j=9)
        for hp in range(3):
            kb.memset("dve", Mf[hp], 0.0)
            kb.memset("dve", Mb[hp], 0.0)
        kb.memset("dve", TQ, 0.0)
        def t46():
            return al.f32(24)
        G4, BT4, GC4, GAM, NGB, DK, GCE, TA = [t46() for _ in range(8)]

        def nh(v, n, h0, h1=None):
            h1 = h0 + 1 if h1 is None else h1
            return v[:, n * 6 + h0:n * 6 + h1]
        BTT = al.f32(512)
        PQ = [al.f32(516) for _ in range(2)]
        QKV = [al.f32(512) for _ in range(3)]
        TMP = [al.f32(512)] * 2
        SQb = al.b16(512)
        KQ = al.b16(1024).re("p (n j t) -> p n j t", n=4, j=2)
        KNb = al.b16(512)
        Vb = al.b16(512)
        OF = al.f32(512)
        YG = [al.b16(512) for _ in range(3)]
        KD = [al.b16(128) for _ in range(4)]
        VB = [al.b16(128) for _ in range(4)]
        QKT = [[al.b16(128) for _ in range(2)] for _ in range(4)]
        TTf = [[al.f32(128) for _ in range(2)] for _ in range(4)]
        SL = [[dict(Y0=al.f32(128), A1=al.f32(128), Yp=[al.f32(128), al.f32(128)], Ap=[al.f32(128), al.f32(128)])
               for _ in range(2)] for _ in range(2)]
        GMh = [al.f32(128) for _ in range(2)]
        EXD = [al.f32(128) for _ in range(2)]
        EM = [al.f32(256) for _ in range(2)]
        EL = [al.f32(128) for _ in range(2)]
        Rb = [al.f32(64) for _ in range(2)]
        VN = [al.b16(64) for _ in range(2)]
        O1 = [al.f32(64) for _ in range(2)]
        Ot = [al.f32(128) for _ in range(4)]
        SS = al.f32(8)
        qi = [0]

        for b in range(NBn):
            pab = next_ps()
            for n in range(4):
                for k in range(KC):
                    kb.mm(pab[:, n * 12:(n + 1) * 12], HT[b][:, k, n * 128:(n + 1) * 128], WIN[k][:, 1536:1548],
                          start=(k == 0), stop=(k == KC - 1))
            for n in range(4):
                kb.tt("dve", nh(TA, n, 0, 6), pab[:, n * 12:n * 12 + 6], prow[l][:, 0:6], ALU.add)
                kb.act(nh(BT4, n, 0, 6), pab[:, n * 12 + 6:n * 12 + 12], AF.Sigmoid)
            kb.act(TA, TA, AF.Exp)
            kb.act(TA, TA, AF.Ln, bias=1.0)
            for n in range(4):
                kb.tt("dve", nh(G4, n, 0, 6), nh(TA, n, 0, 6), NEA[:, 0:6], ALU.mult)
            pgc = next_ps()
            for n in range(4):
                kb.mm(pgc[:, n * 6:(n + 1) * 6], mu_i, nh(G4, n, 0, 6))
            pgl = next_ps()
            for n in range(4):
                kb.mm(pgl[:, n * 6:(n + 1) * 6], ones_f, nh(G4, n, 0, 6))
            kb.copy("act", GC4, pgc[:, 0:24])
            kb.act(GAM, pgc[:, 0:24], AF.Exp)
            kb.stt(NGB, GAM, -1.0, BT4, ALU.mult, ALU.mult)
            kb.tt("dve", DK, pgl[:, 0:24], GC4, ALU.subtract)
            kb.act(DK, DK, AF.Exp)
            kb.act(GCE, pgl[:, 0:24], AF.Exp)
            pbt = next_ps()
            for k in range(KC):
                kb.mm(pbt[0:6, :], WIN[k][:, 1542:1548], HT[b][:, k, :], start=(k == 0), stop=(k == KC - 1))
            kb.act(BTT[0:6, :], pbt[0:6, :], AF.Sigmoid)
            if os.environ.get("GDN_CUT") == "1":
                return

            for hp in range(3):
                for wi_, j in enumerate((hp, 3 + hp, 6 + hp)):
                    pq = PQ[qi[0] % 2]
                    qi[0] += 1
                    pp = next_ps()
                    proj(b, j * 128, 128, pp)
                    kb.copy("act", pq[:, 3:515], pp)
                    kb.copy("dve", pq[:, 0:3], TQ[:, j, :])
                    t = QKV[wi_]
                    kb.ts("dve", t, pq[:, 3:515], pc(l, "gd_conv_w", 3 * 9 + j), ALU.mult)
                    for k in range(3):
                        kb.stt(t, pq[:, k:k + 512], pc(l, "gd_conv_w", k * 9 + j), t, ALU.mult, ALU.add)
                    kb.copy("dve", TQ[:, j, :], pq[:, 512:515])
                    kb.act(t, t, AF.Silu)
                q, k_, v = QKV
                kb.act(SQb, q, AF.Square)
                pss = next_ps()
                kb.mm(pss, bd_b, SQb)
                kb.act(TMP[0], pss, AF.Sqrt, bias=1e-6)
                kb.I("dve", "reciprocal", out=TMP[0], in_=TMP[0])
                kb.stt(KQ[:, :, 1, :], q.re("p (n t) -> p n t", n=4), 0.125, TMP[0].re("p (n t) -> p n t", n=4), ALU.mult, ALU.mult)
                kb.act(SQb, k_, AF.Square)
                pss = next_ps()
                kb.mm(pss, bd_b, SQb)
                kb.act(TMP[1], pss, AF.Sqrt, bias=1e-6)
                kb.I("dve", "reciprocal", out=TMP[1], in_=TMP[1])
                kb.tt("dve", k_, k_, TMP[1], ALU.mult)
                kb.copy("act", KNb, k_)
                pbb = next_ps()
                kb.mm(pbb, cst[0:6, CST["sel"] + hp * 128:CST["sel"] + (hp + 1) * 128], BTT[0:6, :])
                kb.tt("dve", KQ[:, :, 0, :], k_.re("p (n t) -> p n t", n=4), pbb.re("p (n t) -> p n t", n=4), ALU.mult)
                kb.copy("act", Vb, v)
                if os.environ.get("GDN_CUT") == "2":
                    return
                for n2 in range(2):
                    items = []
                    for n in (2 * n2, 2 * n2 + 1):
                        cs = slice(n * 128, (n + 1) * 128)
                        pk = next_ps_b()
                        kb.tr(pk[:, 0:128], KNb[:, cs], ident_b)
                        pv = next_ps_b()
                        kb.tr(pv[:, 0:128], Vb[:, cs], ident_b)
                        for h in range(2):
                            hh = 2 * hp + h
                            fs = slice(h * 64, (h + 1) * 64)
                            kb.ts("dve", KD[n][:, fs], pk[:, fs], nh(DK, n, hh), ALU.mult)
                            kb.ts("dve", VB[n][:, fs], pv[:, fs], nh(BT4, n, hh), ALU.mult)
                        for h in range(2):
                            hh = 2 * hp + h
                            hs = slice(h * 64, (h + 1) * 64)
                            gm, exd, em, el = GMh[h], EXD[h], EM[h], EL[h]
                            sl = SL[n % 2][h]
                            kb.ts("dve", gm, ml_s, nh(G4, n, hh), ALU.mult)
                            pD = next_ps()
                            kb.mm(pD[:, 0:128], gm, mu_i)
                            kb.act(exd, pD[:, 0:128], AF.Exp)
                            kb.tt("dve", em[:, 0:128], exd, NMU, ALU.mult)
                            kb.tt("dve", em[:, 128:256], exd, mu_i, ALU.mult)
                            pEL = next_ps()
                            kb.tr(pEL[:, 0:128], em[:, 0:128], ident)
                            kb.copy("act", el, pEL[:, 0:128])
                            pGR = next_ps()
                            kb.mm(pGR[:, 0:256], KNb[hs, cs], KQ[hs, n, :, :].re("p j t -> p (j t)"))
                            kb.tt("dve", sl["Y0"], pGR[:, 0:128], em[:, 0:128], ALU.mult)
                            kb.tt("dve", QKT[n][h], pGR[:, 128:256], em[:, 128:256], ALU.mult)
                            pGL = next_ps()
                            kb.mm(pGL[:, 0:128], KQ[hs, n, 0, :], KNb[hs, cs])
                            kb.tt("dve", sl["A1"], pGL[:, 0:128], el, ALU.mult)
                            sl["Sf"] = TTf[n][h]
                            items.append(sl)
                    tchain(items, f32=True)
                for n in range(int(os.environ.get("GDN_N", "4"))):
                    cs = slice(n * 128, (n + 1) * 128)
                    if os.environ.get("GDN_CUT") == "4b" and n == int(os.environ.get("GDN_CUTN", "0")):
                        return
                    pKMh = [next_ps(), next_ps()]
                    pO1h = [next_ps(), next_ps()]
                    for h in range(2):
                        hs = slice(h * 64, (h + 1) * 64)
                        kb.mm(pKMh[h][:, 0:64], KNb[hs, cs], Mb[hp][hs, :])
                        kb.mm(pO1h[h][:, 0:64], KQ[hs, n, 1, :], Mb[hp][hs, :])
                    if os.environ.get("GDN_CUT") == "4c" and n == int(os.environ.get("GDN_CUTN", "0")):
                        return
                    pVN = next_ps()
                    for h in range(2):
                        hh = 2 * hp + h
                        fs = slice(h * 64, (h + 1) * 64)
                        kb.stt(Rb[h], pKMh[h][:, 0:64], nh(NGB, n, hh), VB[n][:, fs], ALU.mult, ALU.add)
                        if os.environ.get("GDN_CUT") == "4d" and n == int(os.environ.get("GDN_CUTN", "0")):
                            return
                        kb.mm(pVN[:, fs], TTf[n][h], Rb[h])
                        if os.environ.get("GDN_CUT") == "4e" and n == int(os.environ.get("GDN_CUTN", "0")):
                            return
                        kb.act(O1[h], pO1h[h][:, 0:64], AF.Identity, scale=nh(GAM, n, hh))
                    if os.environ.get("GDN_CUT") == "5" and n == int(os.environ.get("GDN_CUTN", "0")):
                        return
                    pO2 = next_ps()
                    pM = next_ps()
                    for h in range(2):
                        fs = slice(h * 64, (h + 1) * 64)
                        kb.copy("act", VN[h], pVN[:, fs])
                        kb.mm(pO2[:, fs], QKT[n][h], VN[h])
                        kb.mm(pM[fs, 0:64], KD[n][:, fs], VN[h])
                    if os.environ.get("GDN_CUT") == "6" and n == int(os.environ.get("GDN_CUTN", "0")):
                        return
                    ot = Ot[n]
                    for h in range(2):
                        hh = 2 * hp + h
                        fs = slice(h * 64, (h + 1) * 64)
                        kb.stt(Mf[hp][fs, :], Mf[hp][fs, :], nh(GCE, n, hh)[fs, :], pM[fs, 0:64], ALU.mult, ALU.add)
                        kb.tt("dve", ot[:, fs], O1[h], pO2[:, fs], ALU.add)
                    if not os.environ.get("GDN_NOMB"):
                        kb.copy("act", Mb[hp], Mf[hp])
                    if os.environ.get("GDN_CUT") == "7" and n == int(os.environ.get("GDN_CUTN", "0")):
                        return
                    if os.environ.get("GDN_NONORM"):
                        continue
                for n in range(4):
                    cs = slice(n * 128, (n + 1) * 128)
                    ot = Ot[n]
                    for h in range(2):
                        fs = slice(h * 64, (h + 1) * 64)
                        kb.act(O1[h], ot[:, fs], AF.Square, accum_out=SS[:, h:h + 1])
                    kb.act(SS[:, 0:2], SS[:, 0:2], AF.Sqrt, bias=EPS, scale=1.0 / 64)
                    kb.I("dve", "reciprocal", out=SS[:, 0:2], in_=SS[:, 0:2])
                    for h in range(2):
                        fs = slice(h * 64, (h + 1) * 64)
                        kb.stt(ot[:, fs], ot[:, fs], SS[:, h:h + 1], GNR[:, fs], ALU.mult, ALU.mult)
                    pT = next_ps()
                    kb.tr(pT[:, 0:128], ot, ident)
                    kb.copy("act", OF[:, cs], pT[:, 0:128])
                pz = next_ps()
                proj(b, 1152 + hp * 128, 128, pz)
                kb.act(TMP[0], pz, AF.Silu)
                kb.tt("dve", YG[hp], OF, TMP[0], ALU.mult)
                if os.environ.get("GDN_CUT") == "10":
                    return
            wout_apply(b, 3, lambda c: YG[c])


    def mixer_rwkv(l):
        SW = 256
        NCH = SW // 128
        al = Bump(carve, MIX_BASE, ARENA)
        load_win(l, 512, 1408)
        for c in range(3):
            kb.dma("pool", WOUT[:, c, :], w_out_d[l, 256 + c * 128:256 + (c + 1) * 128, :])
        WA = al.b16(384)
        GUP = al.b16(384)
        kb.dma("pool", WA, rw_wa_d[l])
        kb.dma("pool", GUP, rw_gup_d[l])
        if l >= 1:
            VW1 = al.b16(96).re("p (c n) -> p c n", c=3)
            VW2 = al.b16(384)
            kb.dma("pool", VW1, vw1_d[l - 1].re("(c p) n -> p c n", p=128))
            kb.dma("pool", VW2[0:32, :], vw2_d[l - 1])
        mu_s = cst[:, CST["mu_s"]:CST["mu_s"] + 128]
        mu_i = cst[:, CST["mu_i"]:CST["mu_i"] + 128]
        ml_s = cst[:, CST["ml_s"]:CST["ml_s"] + 128]
        bd_f = cst[:, CST["bd"]:CST["bd"] + 128]
        rst = cst[:, CST["rst"]:CST["rst"] + SW]
        MSK1 = al.f32(256)
        MSK2 = al.f32(256)
        NML = al.f32(128)
        kb.ts("dve", MSK1[:, 0:128], mu_s, -1.0, ALU.mult)
        kb.copy("dve", MSK1[:, 128:256], mu_i)
        kb.copy("dve", MSK2[:, 0:128], mu_s)
        kb.copy("dve", MSK2[:, 128:256], mu_i)
        kb.ts("dve", NML, ml_s, -1.0, ALU.mult)
        OMM = al.f32(11)
        OMKA = al.f32(3)
        kb.ts("dve", OMM, pcol[l][:, PCOL["rw_mu"]:PCOL["rw_mu"] + 11], -1.0, ALU.mult, s2=1.0, op1=ALU.add)
        kb.ts("dve", OMKA, pcol[l][:, PCOL["rw_ka"]:PCOL["rw_ka"] + 3], -1.0, ALU.mult, s2=1.0, op1=ALU.add)
        TR = al.f32(11)
        kb.memset("dve", TR, 0.0)
        Mf = [al.f32(64) for _ in range(3)]
        Mb = [al.b16(64) for _ in range(3)]
        for hp in range(3):
            kb.memset("dve", Mf[hp], 0.0)
            kb.memset("dve", Mb[hp], 0.0)
        PR = [al.f32(SW + 4) for _ in range(2)]
        XWAb = al.b16(SW)
        SXGb = al.b16(SW)
        V3 = [al.f32(SW) for _ in range(3)]
        V3b = [al.b16(SW) for _ in range(3)]
        V1b = al.b16(SW)
        Rf = al.f32(SW)
        Kf = al.f32(SW)
        SGW, GS, E1, Aa, KKf, KPf, BON, Gg, TMP = [al.f32(SW) for _ in range(9)]
        SQb = al.b16(SW)
        KR = al.b16(2 * SW).re("p (n j t) -> p n j t", n=NCH, j=2)
        BIGb = al.b16(SW)
        KIGb = al.b16(SW)
        Vb = al.b16(SW)
        YNF = al.f32(SW)
        YR = [al.b16(SW) for _ in range(3)]
        BT = [al.b16(128) for _ in range(NCH)]
        KT = [al.b16(128) for _ in range(NCH)]
        VT = [al.b16(128) for _ in range(NCH)]
        ARm = [[al.b16(256) for _ in range(2)] for _ in range(NCH)]
        BRm = [[al.b16(256) for _ in range(2)] for _ in range(NCH)]
        A1 = [[al.b16(128) for _ in range(2)] for _ in range(NCH)]
        CH = [[dict(Yp=[al.b16(128), al.b16(128)], Ap=[al.b16(128), al.b16(128)], Sf=al.f32(128), Sb=al.b16(128))
               for _ in range(2)] for _ in range(NCH)]
        Zn = [al.b16(64) for _ in range(2)]
        Ub = [al.b16(64) for _ in range(2)]
        YS = [al.f32(128) for _ in range(2)]
        SCR = al.f32(64)
        ST = al.f32(8)
        pri = [0]

        def lerp(b, t0, j, out):
            pr = PR[pri[0] % 2]
            pri[0] += 1
            pp = next_ps()
            for k in range(KC):
                kb.mm(pp[:, 0:SW], WIN[k][:, j * 128:(j + 1) * 128], HT[b][:, k, t0:t0 + SW], start=(k == 0), stop=(k == KC - 1))
            kb.copy("act", pr[:, 1:SW + 1], pp[:, 0:SW])
            kb.copy("dve", pr[:, 0:1], TR[:, j:j + 1])
            kb.ts("dve", out, pr[:, 1:SW + 1], OMM[:, j:j + 1], ALU.mult)
            kb.stt(out, pr[:, 0:SW], pc(l, "rw_mu", j), out, ALU.mult, ALU.add)
            kb.copy("dve", TR[:, j:j + 1], pr[:, SW:SW + 1])

        for sb in range(Tn // SW):
            b, t0 = divmod(sb * SW, TB)
            ts_ = slice(t0, t0 + SW)
            lerp(b, t0, 9, TMP)
            kb.act(XWAb[0:64, :], TMP[0:64, :], AF.Tanh)
            kb.copy("act", XWAb[64:128, :], TMP[64:128, :])
            lerp(b, t0, 10, TMP)
            kb.act(SXGb, TMP, AF.Sigmoid)
            for hp in range(3):
                lerp(b, t0, 6 + hp, V3[hp])
                if l == 0:
                    kb.copy("act", VF[hp][b][:, ts_], V3[hp])
                else:
                    kb.copy("act", V3b[hp], V3[hp])
            if l >= 1:
                pv1 = next_ps()
                for hp in range(3):
                    kb.mm(pv1[0:32, 0:SW], VW1[:, hp, :], V3b[hp], start=(hp == 0), stop=(hp == 2))
                kb.copy("act", V1b[0:32, :], pv1[0:32, 0:SW])
            for hp in range(3):
                v = V3[hp]
                hc = slice(hp * 128, (hp + 1) * 128)
                if l >= 1:
                    pv2 = next_ps()
                    kb.mm(pv2[:, 0:SW], VW2[0:32, hc], V1b[0:32, :])
                    kb.act(TMP, pv2[:, 0:SW], AF.Sigmoid, bias=pc(l, "rw_vresb", hp))
                    kb.tt("dve", Gg, VF[hp][b][:, ts_], v, ALU.subtract)
                    kb.tt("dve", Gg, Gg, TMP, ALU.mult)
                    kb.tt("dve", v, v, Gg, ALU.add)
                lerp(b, t0, hp, Rf)
                lerp(b, t0, 3 + hp, Kf)
                pw = next_ps()
                kb.mm(pw[:, 0:SW], WA[0:64, hc], XWAb[0:64, :])
                kb.act(SGW, pw[:, 0:SW], AF.Sigmoid, bias=pc(l, "rw_wbias", hp))
                pa = next_ps()
                kb.mm(pa[:, 0:SW], WA[64:128, hc], XWAb[64:128, :])
                kb.act(Aa, pa[:, 0:SW], AF.Sigmoid, bias=pc(l, "rw_abias", hp))
                pg = next_ps()
                kb.mm(pg[:, 0:SW], GUP[:, hc], SXGb)
                kb.copy("act", Gg, pg[:, 0:SW])
                kb.ts("dve", KKf, Kf, pc(l, "rw_kk", hp), ALU.mult)
                kb.act(SQb, KKf, AF.Square)
                pss = next_ps()
                kb.mm(pss[:, 0:SW], bd_b, SQb)
                kb.act(TMP, pss[:, 0:SW], AF.Sqrt, bias=1e-6)
                kb.I("dve", "reciprocal", out=TMP, in_=TMP)
                kb.tt("dve", KKf, KKf, TMP, ALU.mult)
                kb.ts("dve", TMP, Aa, pc(l, "rw_ka", hp), ALU.mult, s2=OMKA[:, hp:hp + 1], op1=ALU.add)
                kb.tt("dve", KPf, Kf, TMP, ALU.mult)
                kb.stt(BON, Rf, pc(l, "rw_rk", hp), KPf, ALU.mult, ALU.mult)
                pbs = next_ps()
                kb.mm(pbs[:, 0:SW], bd_f, BON)
                kb.tt("dve", BON, pbs[:, 0:SW], v, ALU.mult)
                kb.tt("dve", Aa, KKf, Aa, ALU.mult)
                kb.I("dve", "tensor_tensor_scan", out=GS, data0=rst, data1=SGW, initial=0.0, op0=ALU.mult, op1=ALU.add)
                kb.tt("dve", SGW, GS, SGW, ALU.subtract)
                kb.act(SGW, SGW, AF.Exp, scale=-C0)
                kb.act(E1, GS, AF.Exp, scale=-C0)
                kb.act(GS, GS, AF.Exp, scale=C0)
                kb.tt("dve", KR[:, :, 0, :], KKf.re("p (n t) -> p n t", n=NCH), SGW.re("p (n t) -> p n t", n=NCH), ALU.mult)
                kb.tt("dve", KR[:, :, 1, :], Rf.re("p (n t) -> p n t", n=NCH), E1.re("p (n t) -> p n t", n=NCH), ALU.mult)
                kb.tt("dve", BIGb, Aa, GS, ALU.mult)
                kb.tt("dve", KIGb, KPf, GS, ALU.mult)
                kb.copy("act", Vb, v)
                for n in range(NCH):
                    cs = slice(n * 128, (n + 1) * 128)
                    for src, dst in ((BIGb, BT[n]), (KIGb, KT[n]), (Vb, VT[n])):
                        pt = next_ps_b()
                        kb.tr(pt[:, 0:128], src[:, cs], ident_b)
                        kb.copy("act", dst, pt[:, 0:128])
                    for h in range(2):
                        hs = slice(h * 64, (h + 1) * 64)
                        krh = KR[hs, n, :, :].re("p j t -> p (j t)")
                        pAR = next_ps()
                        kb.mm(pAR[:, 0:256], BIGb[hs, cs], krh)
                        kb.tt("dve", ARm[n][h], pAR[:, 0:256], MSK1, ALU.mult)
                        pBR = next_ps()
                        kb.mm(pBR[:, 0:256], KIGb[hs, cs], krh)
                        kb.tt("dve", BRm[n][h], pBR[:, 0:256], MSK2, ALU.mult)
                        pA = next_ps()
                        kb.mm(pA[:, 0:128], KR[hs, n, 0, :], BIGb[hs, cs])
                        kb.tt("dve", A1[n][h], pA[:, 0:128], NML, ALU.mult)
                        CH[n][h]["Y0"] = ARm[n][h][:, 0:128]
                        CH[n][h]["A1"] = A1[n][h]
                tchain([CH[n][h] for n in range(NCH) for h in range(2)])
                for n in range(NCH):
                    cs = slice(n * 128, (n + 1) * 128)
                    pZh = [next_ps(), next_ps()]
                    for h in range(2):
                        hs = slice(h * 64, (h + 1) * 64)
                        kb.mm(pZh[h][:, 0:64], KR[hs, n, 0, :], Mb[hp][hs, :], start=True, stop=False)
                        kb.mm(pZh[h][:, 0:64], BRm[n][h][:, 0:128], VT[n][:, hs], start=False, stop=True)
                    pU = next_ps()
                    for h in range(2):
                        hs = slice(h * 64, (h + 1) * 64)
                        kb.act(Zn[h], pZh[h][:, 0:64], AF.Copy, scale=-1.0)
                        kb.mm(pU[:, hs], CH[n][h]["Sb"], Zn[h])
                    pYh = [next_ps(), next_ps()]
                    pM = next_ps()
                    for h in range(2):
                        hs = slice(h * 64, (h + 1) * 64)
                        kb.copy("act" if h else "dve", Ub[h], pU[:, hs])
                    for h in range(2):
                        hs = slice(h * 64, (h + 1) * 64)
                        kb.mm(pYh[h][:, 0:64], KR[hs, n, 1, :], Mb[hp][hs, :], start=True, stop=False)
                        kb.mm(pYh[h][:, 0:64], ARm[n][h][:, 128:256], Ub[h], start=False, stop=False)
                        kb.mm(pYh[h][:, 0:64], BRm[n][h][:, 128:256], VT[n][:, hs], start=False, stop=True)
                    for h in range(2):
                        hs = slice(h * 64, (h + 1) * 64)
                        kb.mm(pM[hs, 0:64], BT[n][:, hs], Ub[h], start=True, stop=False)
                        kb.mm(pM[hs, 0:64], KT[n][:, hs], VT[n][:, hs], start=False, stop=True)
                    kb.tt("dve", Mf[hp], Mf[hp], pM[:, 0:64], ALU.add)
                    kb.ts("dve", Mf[hp], Mf[hp], E1[:, n * 128 + 127:n * 128 + 128], ALU.mult)
                    kb.copy("act", Mb[hp], Mf[hp])
                    ys = YS[n % 2]
                    for h in range(2):
                        hs = slice(h * 64, (h + 1) * 64)
                        kb.act(ys[:, hs], pYh[h][:, 0:64], AF.Copy, accum_out=ST[:, h:h + 1])
                    kb.ts("dve", ST[:, 2:4], ST[:, 0:2], -1.0 / 64, ALU.mult)
                    for h in range(2):
                        hs = slice(h * 64, (h + 1) * 64)
                        kb.ts("dve", ys[:, hs], ys[:, hs], ST[:, 2 + h:3 + h], ALU.add)
                        kb.act(SCR, ys[:, hs], AF.Square, accum_out=ST[:, 4 + h:5 + h])
                    kb.act(ST[:, 4:6], ST[:, 4:6], AF.Sqrt, bias=64e-5, scale=1.0 / 64)
                    kb.I("dve", "reciprocal", out=ST[:, 4:6], in_=ST[:, 4:6])
                    for h in range(2):
                        hs = slice(h * 64, (h + 1) * 64)
                        kb.ts("dve", ys[:, hs], ys[:, hs], ST[:, 4 + h:5 + h], ALU.mult)
                    pT = next_ps()
                    kb.tr(pT[:, 0:128], ys, ident)
                    kb.act(YNF[:, cs], pT[:, 0:128], AF.Identity, scale=pc(l, "rw_lng", hp), bias=pc(l, "rw_lnb", hp))
                kb.tt("dve", YNF, YNF, BON, ALU.add)
                kb.tt("dve", YR[hp], YNF, Gg, ALU.mult)
            for d in range(KC):
                po = next_ps()
                for c in range(3):
                    kb.mm(po[:, 0:SW], WOUT[:, c, d * 128:(d + 1) * 128], YR[c], start=(c == 0), stop=(c == 2))
                kb.tt("dve", XT[d][b][:, ts_], po[:, 0:SW], XT[d][b][:, ts_], ALU.add)

    def mixer(l):
        rmsnorm(l, "mix_norm", h_out, SQ=MSQ, RS=MRS)
        kb.barrier()
        sel = os.environ.get("MIXERS", "lru,rwkv,gdn").split(",")
        if stages in ("all", "mix", "lru") and (stages == "lru" or "lru" in sel):
            mixer_lru(l)
            kb.barrier()
        if stages in ("all", "mix", "rwkv") and (stages == "rwkv" or "rwkv" in sel):
            mixer_rwkv(l)
            kb.barrier()
        if stages in ("all", "mix", "gdn") and (stages == "gdn" or "gdn" in sel):
            mixer_gdn(l)
            kb.barrier()

    if need_mix:
        MSQ = carve(ARENA - 8192 - 4096, 8192, BF16).re("p (k n) -> p k n", k=KC)
        MRS = [carve(ARENA - 4096 + i * 2048, 2048, F32) for i in range(2)]

    for l in range(Ln):
        if stages in ("io", "io0"):
            break
        if stages == "norm":
            rmsnorm(l, "ffn1_norm", h_out)
            break
        if need_ffn:
            ffn(l, 0)
            if stages == "ffn1":
                break
            kb.barrier()
        if need_mix:
            mixer(l)
            kb.barrier()
        if need_ffn:
            ffn(l, 1)
            kb.barrier()

    kb.barrier()
    YT = [carve(16384 + c * 2048 * NBn, 2048 * NBn, F32).re("p (b n) -> p b n", b=NBn) for c in range(KC)]
    OS = [carve(i * 4096, 4096, F32) for i in range(4)]
    FSQ = carve(16384 + 8 * 2048 * NBn, 8192, BF16).re("p (k n) -> p k n", k=KC) if NBn < 4 else None
    if NBn == 4:
        FSQ = carve(16384 + 65536, 8192, BF16).re("p (k n) -> p k n", k=KC)
    FRS = [carve(3 * 4096 + i * 2048, 2048, F32) for i in range(2)]

    def y_out(b, c, rs, g):
        kb.stt(YT[c][:, b, :], XT[c][b], g, rs, ALU.mult, ALU.mult)

    if stages == "io0":
        for b in range(NBn):
            for c in range(KC):
                kb.copy("dve", YT[c][:, b, :], XT[c][b])
    else:
        rmsnorm(Ln - 1, "final_norm", y_out, SQ=FSQ, RS=FRS)
    kb.barrier()
    out_toks = []
    for tt_ in range(Tn // 128):
        b, o = divmod(tt_ * 128, TB)
        osb = OS[tt_ % 4]
        for half in range(2):
            p = next_ps()
            for j in range(4):
                c = half * 4 + j
                kb.tr(p[:, j * 128:(j + 1) * 128], YT[c][:, b, o:o + 128], ident)
            kb.copy("act" if half else "dve", osb[:, half * 512:(half + 1) * 512], p)
        out_toks.append(kb.dma("sp", y_d[tt_ * 128:(tt_ + 1) * 128, :], osb))
    for t in out_toks:
        kb._wait("sp", t)
    return nc


def _prep_shared(inp, need_ffn=True, need_mix=True):
    f = lambda a: np.asarray(a, dtype=np.float32)
    pc = np.zeros((L, 128, PCOL["_n"]), np.float32)

    def put(l, name, vec, n):
        pc[l, :, PCOL[name]:PCOL[name] + n] = f(vec).reshape(n, 128).T

    for l in range(L):
        put(l, "ffn1_norm", inp["ffn1_norm"][l], 8)
        put(l, "mix_norm", inp["mix_norm"][l], 8)
        put(l, "ffn2_norm", inp["ffn2_norm"][l], 8)
        put(l, "final_norm", inp["final_norm"], 8)
        cw = f(inp["lru_conv_w"][l])
        for k in range(4):
            pc[l, :, PCOL["lru_conv_w"] + 2 * k:PCOL["lru_conv_w"] + 2 * k + 2] = cw[k].reshape(2, 128).T
        put(l, "lru_conv_b", inp["lru_conv_b"][l], 2)
        put(l, "lru_ga_b", f(inp["lru_gate_a_b"][l]).reshape(-1), 2)
        put(l, "lru_gx_b", f(inp["lru_gate_x_b"][l]).reshape(-1), 2)
        put(l, "lru_lam", inp["lru_lambda"][l], 2)
        put(l, "lru_out_g", inp["lru_out_norm"][l], 2)
        put(l, "rw_mu", inp["rwkv_mu"][l], 11)
        put(l, "rw_wbias", inp["rwkv_w_bias"][l], 3)
        put(l, "rw_abias", inp["rwkv_a_bias"][l], 3)
        put(l, "rw_kk", inp["rwkv_k_k"][l], 3)
        put(l, "rw_ka", inp["rwkv_k_a"][l], 3)
        put(l, "rw_rk", f(inp["rwkv_r_k"][l]).reshape(-1), 3)
        put(l, "rw_lng", inp["rwkv_ln_g"][l], 3)
        put(l, "rw_lnb", inp["rwkv_ln_b"][l], 3)
        if l >= 1:
            put(l, "rw_vresb", inp["rwkv_vres_b"][l - 1], 3)
        gw = f(inp["gdn_conv_w"][l])
        for k in range(4):
            pc[l, :, PCOL["gd_conv_w"] + 9 * k:PCOL["gd_conv_w"] + 9 * k + 9] = gw[k].reshape(9, 128).T
    cst = np.zeros((128, CST_N), np.float32)
    i = np.arange(128)
    cst[:, 0:128] = np.eye(128)
    cst[:, 128:256] = 1.0
    cst[:, 256:384] = (i[:, None] < i[None, :])
    cst[:, 384:512] = (i[:, None] <= i[None, :])
    cst[:, 512:640] = (i[:, None] > i[None, :])
    cst[:, 640:768] = (i[:, None] // 64 == i[None, :] // 64)
    cst[:, 768:1280] = (np.arange(512) % 128 != 0)[None, :]
    for h in range(6):
        cst[h, 1280 + h * 64:1280 + (h + 1) * 64] = 1.0
    sh = {"pcol": pc, "cst": cst}
    if need_ffn:
        for k in ("ffn1_wi", "ffn2_wi", "ffn1_wo", "ffn2_wo"):
            sh[k] = np.ascontiguousarray(f(inp[k]))
    if need_mix:
        pr = np.zeros((L, 128, PROW_N), np.float32)
        lg = np.zeros((L, 128, 4, 128), np.float32)
        wa = np.zeros((L, 128, 384), np.float32)
        for l in range(L):
            pr[l, :, 0:6] = f(inp["gdn_dt_bias"][l])[None, :]
            pr[l, :, 6:12] = f(inp["gdn_a_log"][l])[None, :]
            pr[l, :, 12:76] = f(inp["gdn_norm"][l])[None, :]
            pr[l, :, 76:140] = f(inp["gdn_norm"][l])[None, :]
            for gi, nm in enumerate(("lru_gate_a_w", "lru_gate_x_w")):
                w = f(inp[nm][l])
                for c in range(2):
                    for j in range(2):
                        lg[l, j * 64:(j + 1) * 64, gi * 2 + c, j * 64:(j + 1) * 64] = w[2 * c + j]
            wa[l, 0:64] = f(inp["rwkv_w_up"][l])
            wa[l, 64:128] = f(inp["rwkv_a_up"][l])
        sh.update({"prow": pr, "lru_g": lg, "rw_wa": wa})
        for k in ("w_in", "w_out", "rwkv_g_up", "rwkv_vres_w1", "rwkv_vres_w2"):
            sh[k] = np.ascontiguousarray(f(inp[k]))
    return sh


def kernel(**inputs):
    x = np.ascontiguousarray(inputs["x"], dtype=np.float32)
    B = x.shape[0]
    sh = _prep_shared(inputs)
    nc = bass.Bass("TRN2", target_bir_lowering=False)
    build(nc)
    in_maps = []
    for b in range(B):
        m = dict(sh)
        m["x"] = x[b]
        in_maps.append(m)
    res = run_bass_kernel_spmd(nc, in_maps, core_ids=list(range(B)))
    return np.stack([np.asarray(r["y"]) for r in res.results], axis=0).astype(np.float32)
```

```python
import os
import numpy as np
import concourse.bass as bass
import concourse.mybir as mybir
from concourse.bass_utils import run_bass_kernel_spmd

F32 = mybir.dt.float32
BF16 = mybir.dt.bfloat16
AF = mybir.ActivationFunctionType
ALU = mybir.AluOpType

D = 1024
T = 2048
L = 2
DFF = 2816
NFC = DFF // 128
KC = D // 128
TB = 512
NB = T // TB
EPS = 1e-6
DIN = 3468

WRITE_KEYS = ("out", "accum_out", "ap")
ATTACH_WAITS = not os.environ.get("NO_ATTACH")


class Dep:
    __slots__ = ("w", "r", "excl")

    def __init__(self, excl=False):
        self.w = None
        self.r = {}
        self.excl = excl


class V:
    __slots__ = ("ap", "dep")

    def __init__(self, ap, dep=None):
        self.ap = ap
        self.dep = dep if dep is not None else Dep()

    def __getitem__(self, idx):
        return V(self.ap[idx], self.dep)

    def re(self, s, **kw):
        return V(self.ap.rearrange(s, **kw), self.dep)

    def bc(self, dt):
        return V(self.ap.bitcast(dt), self.dep)


class KB:
    def __init__(self, nc):
        self.nc = nc
        self.E = {"pe": nc.tensor, "act": nc.scalar, "dve": nc.vector, "pool": nc.gpsimd, "sp": nc.sync}
        self.sems = {}
        self.ecnt = {}
        self.waited = {e: {} for e in self.E}
        for e in self.E:
            self.sems[("e", e)] = nc.alloc_semaphore("s_" + e)
            self.ecnt[e] = 0
        self.dma_n = {"sp": 24, "pool": 12, "act": 4}
        self.dma_rr = {q: 0 for q in self.dma_n}
        self.dma_val = {}
        for q, n in self.dma_n.items():
            for i in range(n):
                self.sems[("d", q, i)] = nc.alloc_semaphore("d_%s%d" % (q, i))
                self.dma_val[("d", q, i)] = 0
        self.all_dma_toks = []
        self.sb_off = 0

    def _wait(self, eng, tok):
        if tok is None:
            return
        key, val = tok
        if val <= 0 or self.waited[eng].get(key, 0) >= val:
            return
        self.E[eng].wait_ge(self.sems[key], val)
        self.waited[eng][key] = val

    def _need(self, eng, reads, writes):
        mykey = ("e", eng)
        skip_self = eng == "pe"
        need = {}

        def add(tok):
            key, val = tok
            if val <= 0 or self.waited[eng].get(key, 0) >= val:
                return
            if need.get(key, 0) < val:
                need[key] = val
        for v in reads:
            w = v.dep.w
            if w is not None and not (skip_self and w[0] == mykey):
                add(w)
            if v.dep.excl:
                for key, val in v.dep.r.items():
                    if key != mykey:
                        add((key, val))
        for v in writes:
            d = v.dep
            if d.w is not None and not (skip_self and d.w[0] == mykey):
                add(d.w)
            for key, val in d.r.items():
                if skip_self and key == mykey:
                    continue
                add((key, val))
        return need

    def _deps(self, eng, reads, writes):
        for key, val in self._need(eng, reads, writes).items():
            self._wait(eng, (key, val))

    def _mark(self, tok, reads, writes):
        key, val = tok
        for v in reads:
            if v.dep.r.get(key, 0) < val:
                v.dep.r[key] = val
        for v in writes:
            v.dep.w = tok
            v.dep.r = {}

    def I(self, eng, fname, **kw):
        reads, writes, real = [], [], {}
        for k, v in kw.items():
            if isinstance(v, V):
                (writes if k in WRITE_KEYS else reads).append(v)
                real[k] = v.ap
            else:
                real[k] = v
        need = list(self._need(eng, reads, writes).items())
        fuse = None
        if need and ATTACH_WAITS and "accum_out" not in kw:
            fuse = need.pop()
        for key, val in need:
            self._wait(eng, (key, val))
        ins = getattr(self.E[eng], fname)(**real)
        if fuse is not None:
            ins._wait_ge(self.sems[fuse[0]], fuse[1])
            self.waited[eng][fuse[0]] = fuse[1]
        self.ecnt[eng] += 1
        ins.then_inc(self.sems[("e", eng)], 1)
        self._mark((("e", eng), self.ecnt[eng]), reads, writes)
        return ins

    def dma(self, q, out, in_, **kw):
        self._deps(q, [in_], [out])
        idx = self.dma_rr[q] % self.dma_n[q]
        self.dma_rr[q] += 1
        key = ("d", q, idx)
        prev = self.dma_val[key]
        self._wait(q, (key, prev))
        ins = self.E[q].dma_start(out=out.ap, in_=in_.ap, **kw)
        ins.then_inc(self.sems[key], 16)
        self.dma_val[key] = prev + 16
        tok = (key, prev + 16)
        self._mark(tok, [in_], [out])
        return tok

    def barrier(self, engines=("pe", "act", "dve", "pool", "sp")):
        toks = [(("e", e), self.ecnt[e]) for e in self.E]
        toks += [(k, v) for k, v in self.dma_val.items()]
        for e in engines:
            for t in toks:
                if t[0] == ("e", e):
                    continue
                self._wait(e, t)

    def mm(self, out, lhsT, rhs, start=True, stop=True):
        return self.I("pe", "matmul", out=out, lhsT=lhsT, rhs=rhs, start=start, stop=stop)

    def tr(self, out, in_, ident):
        return self.I("pe", "transpose", out=out, in_=in_, identity=ident)

    def act(self, out, in_, func, bias=None, scale=None, accum_out=None):
        kw = {}
        if bias is not None:
            kw["bias"] = bias
        if scale is not None:
            kw["scale"] = scale
        if accum_out is not None:
            kw["accum_out"] = accum_out
        return self.I("act", "activation", out=out, in_=in_, func=func, **kw)

    def tt(self, eng, out, in0, in1, op):
        return self.I(eng, "tensor_tensor", out=out, in0=in0, in1=in1, op=op)

    def ts(self, eng, out, in0, s1, op0, s2=None, op1=None, accum_out=None):
        kw = {}
        if op1 is not None:
            kw["op1"] = op1
        if accum_out is not None:
            kw["accum_out"] = accum_out
        return self.I(eng, "tensor_scalar", out=out, in0=in0, scalar1=s1, scalar2=s2, op0=op0, **kw)

    def stt(self, out, in0, scalar, in1, op0, op1):
        return self.I("dve", "scalar_tensor_tensor", out=out, in0=in0, scalar=scalar, in1=in1, op0=op0, op1=op1)

    def copy(self, eng, out, in_):
        if eng == "act":
            return self.I("act", "activation", out=out, in_=in_, func=AF.Copy)
        return self.I(eng, "tensor_copy", out=out, in_=in_)

    def memset(self, eng, ap, val):
        return self.I(eng, "memset", ap=ap, constant=val)

    def sb(self, name, shape, dtype):
        return V(self.nc.alloc_sbuf_tensor(name, list(shape), dtype)[:])


def _cols_layout():
    m = {}
    n = 0

    def add(name, k):
        nonlocal n
        m[name] = n
        n += k
    add("ffn1_norm", 8)
    add("mix_norm", 8)
    add("ffn2_norm", 8)
    add("final_norm", 8)
    add("lru_conv_w", 8)
    add("lru_conv_b", 2)
    add("lru_ga_b", 2)
    add("lru_gx_b", 2)
    add("lru_lam", 2)
    add("lru_out_g", 2)
    add("rw_mu", 11)
    add("rw_wbias", 3)
    add("rw_abias", 3)
    add("rw_kk", 3)
    add("rw_ka", 3)
    add("rw_rk", 3)
    add("rw_lng", 3)
    add("rw_lnb", 3)
    add("rw_vresb", 3)
    add("gd_conv_w", 36)
    m["_n"] = n
    return m


PCOL = _cols_layout()


C0 = float(np.exp(-0.5))
GELU_C = 0.7978845608028654
CST = {"ident": 0, "ones": 128, "mu_s": 256, "mu_i": 384, "ml_s": 512, "bd": 640, "rst": 768, "sel": 1280}
CST_N = 1280 + 384
PROW = {"gd_dt": 0, "gd_alog": 6, "gd_norm": 12}
PROW_N = 12 + 128


class Bump:
    def __init__(self, carve, base, end):
        self.carve, self.off, self.end = carve, base, end

    def _a(self, n, esz, dtype):
        nbytes = (n * esz + 31) // 32 * 32
        v = self.carve(self.off, nbytes, dtype)
        self.off += nbytes
        assert self.off <= self.end, ("arena overflow", self.off, self.end)
        return v[:, 0:n]

    def f32(self, n):
        return self._a(n, 4, F32)

    def b16(self, n):
        return self._a(n, 2, BF16)


def build(nc, dbg=None):
    dbg = dbg or {}
    stages = dbg.get("stages", "all")
    Tn = dbg.get("T", T)
    NBn = Tn // TB
    Ln = dbg.get("L", L)
    kb = KB(nc)

    def din(name, shape):
        return V(nc.dram_tensor(name, list(shape), F32, kind="ExternalInput").ap())

    x_d = din("x", [Tn, D])
    need_ffn = stages in ("all", "ffn1")
    need_mix = stages in ("all", "lru", "rwkv", "gdn", "mix")
    if need_ffn:
        wi_d = [din("ffn1_wi", [L, D, 2 * DFF]), din("ffn2_wi", [L, D, 2 * DFF])]
        wo_d = [din("ffn1_wo", [L, DFF, D]), din("ffn2_wo", [L, DFF, D])]
    if need_mix:
        w_in_d = din("w_in", [L, D, DIN])
        w_out_d = din("w_out", [L, D, D])
        lru_g_d = din("lru_g", [L, 128, 4, 128])
        rw_wa_d = din("rw_wa", [L, 128, 384])
        rw_gup_d = din("rwkv_g_up", [L, 128, 384])
        vw1_d = din("rwkv_vres_w1", [L - 1, 384, 32])
        vw2_d = din("rwkv_vres_w2", [L - 1, 32, 384])
        prow_d = din("prow", [L, 128, PROW_N])
    pcol_d = din("pcol", [L, 128, PCOL["_n"]])
    cst_d = din("cst", [128, CST_N])
    y_d = V(nc.dram_tensor("y", [Tn, D], F32, kind="ExternalOutput").ap())

    XT = [[kb.sb("xt%d_%d" % (c, b), [128, TB], F32) for b in range(NBn)] for c in range(KC)]
    HT = [kb.sb("ht%d" % b, [128, KC, TB], BF16) for b in range(NBn)]
    VF = [[kb.sb("vf%d_%d" % (hp, b), [128, TB], BF16) for b in range(NBn)] for hp in range(3)]
    cst = kb.sb("cst_sb", [128, CST_N], F32)
    ident = cst[:, 0:128]
    ones_f = cst[:, 128:256]
    ones_b = kb.sb("ones_b", [128, 128], BF16)
    ident_b = kb.sb("ident_b", [128, 128], BF16)
    bd_b = kb.sb("bd_b", [128, 128], BF16)
    pcol = [kb.sb("pcol%d" % l, [128, PCOL["_n"]], F32) for l in range(L)]
    if need_mix:
        prow = [kb.sb("prow%d" % l, [128, PROW_N], F32) for l in range(L)]
    ps = [V(nc.alloc_psum_tensor("ps%d" % i, [128, 512], F32)[:], Dep(excl=True)) for i in range(8)]

    ARENA = 88 * 1024
    arena_h = nc.alloc_sbuf_tensor("arena", [128, ARENA // 4], F32)

    def carve(off, nbytes, dtype, dep=None):
        assert off % 4 == 0 and nbytes % 4 == 0 and off + nbytes <= ARENA, (off, nbytes)
        v = V(arena_h[:][:, off // 4:(off + nbytes) // 4], dep)
        return v.bc(dtype) if dtype != F32 else v

    kb.dma("sp", cst, cst_d)
    kb.copy("dve", ones_b, ones_f)
    kb.copy("dve", ident_b, ident)
    kb.copy("dve", bd_b, cst[:, CST["bd"]:CST["bd"] + 128])
    for l in range(L):
        kb.dma("sp", pcol[l], pcol_d[l])
        if need_mix:
            kb.dma("sp", prow[l], prow_d[l])

    def pc(l, name, j=0):
        c = PCOL[name] + j
        return pcol[l][:, c:c + 1]

    NFMAX = 5
    off = 0
    WI = [carve(off + i * 20480, 20480, BF16).re("p (k n) -> p k n", k=KC) for i in range(2)]
    off += 2 * 20480
    WO = [carve(off + i * 10240, 10240, BF16).re("p (f n) -> p f n", f=NFMAX) for i in range(2)]
    off += 2 * 10240
    ATb = [carve(off + i * 5120, 5120, BF16).re("p (f n) -> p f n", f=NFMAX) for i in range(2)]
    off += 2 * 5120
    SG = [carve(off + i * 2048, 2048, F32) for i in range(2)]
    off += 2 * 2048
    SQ = carve(off, 8192, BF16).re("p (k n) -> p k n", k=KC)
    off += 8192
    RS = [carve(off + i * 2048, 2048, F32) for i in range(2)]
    off += 2 * 2048
    assert off <= ARENA
    UPO = 640
    XS = [carve(i * 4096, 4096, F32) for i in range(4)]

    psi = [0]
    ps_b16 = [p.bc(BF16) for p in ps]

    pool_sel = ["all"]
    pool_ctr = {"all": 0, "S": 0, "P": 0}

    def _ps_index():
        p = pool_sel[0]
        if p == "all":
            i = psi[0] % 8
            psi[0] += 1
            return i
        i = pool_ctr[p] % 4 + (0 if p == "S" else 4)
        pool_ctr[p] += 1
        return i

    def next_ps():
        return ps[_ps_index()]

    def next_ps_b():
        return ps_b16[_ps_index()]

    def drain(gen, pool="all"):
        pool_sel[0] = pool
        for _ in gen:
            pass
        pool_sel[0] = "all"

    def interleave(gS, gP, ratio=2):
        doneS = gS is None
        doneP = gP is None
        while not (doneS and doneP):
            if not doneS:
                pool_sel[0] = "S"
                try:
                    next(gS)
                except StopIteration:
                    doneS = True
            for _ in range(ratio):
                if not doneP:
                    pool_sel[0] = "P"
                    try:
                        next(gP)
                    except StopIteration:
                        doneP = True
        pool_sel[0] = "all"

    for tt_ in range(Tn // 128):
        xs = XS[tt_ % 4]
        kb.dma("sp", xs, x_d[tt_ * 128:(tt_ + 1) * 128, :])
        b, o = divmod(tt_ * 128, TB)
        for half in range(2):
            p = next_ps()
            for j in range(4):
                c = half * 4 + j
                kb.tr(p[:, j * 128:(j + 1) * 128], xs[:, c * 128:(c + 1) * 128], ident)
            for j in range(4):
                c = half * 4 + j
                kb.copy("act" if half else "dve", XT[c][b][:, o:o + 128], p[:, j * 128:(j + 1) * 128])
    kb.barrier()

    def rmsnorm(l, gname, out_fn, SQ=SQ, RS=RS):
        for b in range(NBn):
            for c in range(KC):
                kb.act(SQ[:, c, :], XT[c][b], AF.Square)
            p = next_ps()
            for c in range(KC):
                kb.mm(p, ones_b, SQ[:, c, :], start=(c == 0), stop=(c == KC - 1))
            rs = RS[b % 2]
            kb.act(rs, p, AF.Sqrt, bias=EPS, scale=1.0 / D)
            kb.I("dve", "reciprocal", out=rs, in_=rs)
            for c in range(KC):
                out_fn(b, c, rs, pc(l, gname, c))

    def h_out(b, c, rs, g):
        kb.stt(HT[b][:, c, :], XT[c][b], g, rs, ALU.mult, ALU.mult)

    groups = [(0, 5), (5, 5), (10, 4), (14, 4), (18, 4)]

    def ffn_load(l, which, gi):
        f0, nf = groups[gi]
        buf = gi % 2
        wi = wi_d[which]
        wo = wo_d[which]
        for k in range(KC):
            kb.dma("pool", WI[buf][:, k, 0:nf * 128], wi[l, k * 128:(k + 1) * 128, f0 * 128:(f0 + nf) * 128])
            kb.dma("pool", WI[buf][:, k, UPO:UPO + nf * 128],
                   wi[l, k * 128:(k + 1) * 128, DFF + f0 * 128:DFF + (f0 + nf) * 128])
        for f in range(nf):
            kb.dma("pool", WO[buf][:, f, :], wo[l, (f0 + f) * 128:(f0 + f + 1) * 128, :])

    def ffn(l, which):
        ffn_load(l, which, 0)
        rmsnorm(l, "ffn1_norm" if which == 0 else "ffn2_norm", h_out)
        it = 0
        for gi, (f0, nf) in enumerate(groups):
            buf = gi % 2
            if gi + 1 < len(groups):
                ffn_load(l, which, gi + 1)
            for b in range(NBn):
                at = ATb[it % 2]
                for f in range(nf):
                    pg = next_ps()
                    pu = next_ps()
                    for k in range(KC):
                        kb.mm(pg, WI[buf][:, k, f * 128:(f + 1) * 128], HT[b][:, k, :], start=(k == 0), stop=(k == KC - 1))
                    for k in range(KC):
                        kb.mm(pu, WI[buf][:, k, UPO + f * 128:UPO + (f + 1) * 128], HT[b][:, k, :],
                              start=(k == 0), stop=(k == KC - 1))
                    sg = SG[f % 2]
                    kb.act(sg, pg, AF.Silu)
                    kb.tt("dve", at[:, f, :], sg, pu, ALU.mult)
                for c in range(KC):
                    po = next_ps()
                    for f in range(nf):
                        kb.mm(po, WO[buf][:, f, c * 128:(c + 1) * 128], at[:, f, :], start=(f == 0), stop=(f == nf - 1))
                    kb.stt(XT[c][b], po, 0.5, XT[c][b], ALU.mult, ALU.add)
                it += 1

    MIX_BASE = 0
    if need_mix:
        mo = 0
        WIN = [carve(mo + k * 3104, 3104, BF16) for k in range(KC)]
        mo += KC * 3104
        WOUT = carve(mo, 6144, BF16).re("p (c n) -> p c n", c=3)
        mo += 6144
        MIX_BASE = mo

    def proj(b, col0, ncols, out_ps):
        for k in range(KC):
            kb.mm(out_ps, WIN[k][:, col0:col0 + ncols], HT[b][:, k, :], start=(k == 0), stop=(k == KC - 1))

    def wout_apply(b, nchunks, ysrc):
        for d in range(KC):
            po = next_ps()
            for c in range(nchunks):
                kb.mm(po, WOUT[:, c, d * 128:(d + 1) * 128], ysrc(c), start=(c == 0), stop=(c == nchunks - 1))
            kb.tt("dve", XT[d][b], po, XT[d][b], ALU.add)

    def load_win(l, c0, n):
        for k in range(KC):
            kb.dma("pool", WIN[k][:, 0:n], w_in_d[l, k * 128:(k + 1) * 128, c0:c0 + n])

    def mixer_lru(l):
        al = Bump(carve, MIX_BASE, ARENA)
        LG = al.b16(512).re("p (g n) -> p g n", g=4)
        load_win(l, 0, 512)
        kb.dma("pool", LG, lru_g_d[l])
        for c in range(2):
            kb.dma("pool", WOUT[:, c, :], w_out_d[l, c * 128:(c + 1) * 128, :])
        cols = al.f32(8)
        PXB = [al.f32(516) for _ in range(2)]
        PY = [al.f32(512) for _ in range(2)]
        XC = [al.f32(512) for _ in range(2)]
        XCb = [al.b16(512) for _ in range(2)]
        Rg = [al.f32(512) for _ in range(2)]
        Ig = [al.f32(512) for _ in range(2)]
        Ag = [al.f32(512) for _ in range(2)]
        A2 = [al.f32(512) for _ in range(2)]
        Hh = [al.f32(512) for _ in range(2)]
        Ut = [al.f32(512) for _ in range(2)]
        SQl = [al.b16(512) for _ in range(2)]
        YL = [al.b16(512) for _ in range(2)]
        RSl = al.f32(512)
        if dbg.get('mem'):
            print('LRU arena used', al.off, 'of', ARENA)
        for c in range(2):
            t = cols[:, c:c + 1]
            kb.act(t, pc(l, "lru_lam", c), AF.Exp, scale=-1.0)
            kb.act(t, t, AF.Ln, bias=1.0)
            kb.ts("dve", cols[:, 2 + c:3 + c], t, -16.0, ALU.mult)
            kb.ts("dve", t, t, -8.0, ALU.mult)
            kb.memset("dve", cols[:, 4 + c:5 + c], 0.0)
            kb.memset("dve", PXB[c][:, 0:3], 0.0)
        for b in range(NBn):
            for c in range(2):
                px = next_ps()
                proj(b, c * 128, 128, px)
                kb.copy("act", PXB[c][:, 3:515], px)
                py = next_ps()
                proj(b, 256 + c * 128, 128, py)
                kb.copy("act", PY[c], py)
                xc = XC[c]
                kb.ts("dve", xc, PXB[c][:, 3:515], pc(l, "lru_conv_w", 3 * 2 + c), ALU.mult,
                      s2=pc(l, "lru_conv_b", c), op1=ALU.add)
                for k in range(3):
                    kb.stt(xc, PXB[c][:, k:k + 512], pc(l, "lru_conv_w", k * 2 + c), xc, ALU.mult, ALU.add)
                kb.copy("dve", PXB[c][:, 0:3], PXB[c][:, 512:515])
                kb.copy("act", XCb[c], xc)
                pr = next_ps()
                kb.mm(pr, LG[:, c, :], XCb[c])
                kb.act(Rg[c], pr, AF.Sigmoid, bias=pc(l, "lru_ga_b", c))
                pi_ = next_ps()
                kb.mm(pi_, LG[:, 2 + c, :], XCb[c])
                kb.act(Ig[c], pi_, AF.Sigmoid, bias=pc(l, "lru_gx_b", c))
                kb.act(Ag[c], Rg[c], AF.Exp, scale=cols[:, c:c + 1])
                kb.act(A2[c], Rg[c], AF.Exp, scale=cols[:, 2 + c:3 + c])
                kb.act(A2[c], A2[c], AF.Sqrt, scale=-1.0, bias=1.0)
                if b == 0:
                    kb.memset("dve", A2[c][:, 0:1], 1.0)
                kb.tt("dve", Ig[c], Ig[c], xc, ALU.mult)
                kb.tt("dve", Ig[c], Ig[c], A2[c], ALU.mult)
                kb.I("dve", "tensor_tensor_scan", out=Hh[c], data0=Ag[c], data1=Ig[c],
                     initial=cols[:, 4 + c:5 + c], op0=ALU.mult, op1=ALU.add)
                kb.copy("act", cols[:, 4 + c:5 + c], Hh[c][:, 511:512])
                u = Ut[c]
                kb.act(u, PY[c], AF.Square)
                kb.ts("dve", u, u, 0.044715, ALU.mult, s2=1.0, op1=ALU.add)
                kb.tt("dve", u, u, PY[c], ALU.mult)
                kb.act(u, u, AF.Sigmoid, scale=2.0 * GELU_C)
                kb.tt("dve", u, u, PY[c], ALU.mult)
                kb.tt("dve", Hh[c], Hh[c], u, ALU.mult)
                kb.act(SQl[c], Hh[c], AF.Square)
            pss = next_ps()
            for c in range(2):
                kb.mm(pss, ones_b, SQl[c], start=(c == 0), stop=(c == 1))
            kb.act(RSl, pss, AF.Sqrt, bias=EPS, scale=1.0 / 256)
            kb.I("dve", "reciprocal", out=RSl, in_=RSl)
            for c in range(2):
                kb.stt(YL[c], Hh[c], pc(l, "lru_out_g", c), RSl, ALU.mult, ALU.mult)
            wout_apply(b, 2, lambda c: YL[c])


    def tchain_gen(items, f32=False):
        for it in items:
            kb.tt("dve", it["Sf"], it["Y0"], ident, ALU.add)
            if not f32:
                kb.copy("act", it["Sb"], it["Sf"])
            it["Yk"], it["Ak"] = it["Y0"], it["A1"]
        yield
        for lev in range(6):
            for it in items:
                pa = next_ps()
                kb.mm(pa[:, 0:128], it["Yk"], it["Ak"])
                An = it["Ap"][lev % 2]
                kb.copy("act", An, pa[:, 0:128])
                if lev < 5:
                    py = next_ps()
                    kb.mm(py[:, 0:128], it["Ak"], it["Yk"])
                    Yn = it["Yp"][lev % 2]
                    kb.copy("dve", Yn, py[:, 0:128])
                else:
                    Yn = None
                it["Yk"], it["Ak"] = Yn, An
                yield
            for it in items:
                psn = next_ps()
                kb.mm(psn[:, 0:128], it["Ak"], it["Sf"] if f32 else it["Sb"])
                kb.tt("dve", it["Sf"], psn[:, 0:128], it["Sf"], ALU.add)
                if not f32:
                    kb.copy("act", it["Sb"], it["Sf"])
                yield

    def tchain(items, f32=False):
        for _ in tchain_gen(items, f32):
            pass

    def mixer_gdn(l):
        al = Bump(carve, MIX_BASE, ARENA)
        load_win(l, 1920, 1548)
        for c in range(3):
            kb.dma("pool", WOUT[:, c, :], w_out_d[l, 640 + c * 128:640 + (c + 1) * 128, :])
        NEA = al.f32(8)
        kb.act(NEA[:, 0:6], prow[l][:, 6:12], AF.Exp)
        kb.ts("dve", NEA[:, 0:6], NEA[:, 0:6], -1.0, ALU.mult)
        NMU = al.f32(128)
        kb.ts("dve", NMU, cst[:, CST["mu_s"]:CST["mu_s"] + 128], -1.0, ALU.mult)
        mu_i = cst[:, CST["mu_i"]:CST["mu_i"] + 128]
        ml_s = cst[:, CST["ml_s"]:CST["ml_s"] + 128]
        GNR = prow[l][:, 12:140]
        Mf = [al.f32(64) for _ in range(3)]
        Mb = [al.b16(64) for _ in range(3)]
        TQ = al.f32(27).re("p (j k) -> p j k", j=9)
        for hp in range(3):
            kb.memset("dve", Mf[hp], 0.0)
            kb.memset("dve", Mb[hp], 0.0)
        kb.memset("dve", TQ, 0.0)
        def t46():
            return al.f32(24)
        G4, BT4, GC4, GAM, NGB, DK, GCE, TA = [t46() for _ in range(8)]

        def nh(v, n, h0, h1=None):
            h1 = h0 + 1 if h1 is None else h1
            return v[:, n * 6 + h0:n * 6 + h1]
        BTT = al.f32(512)
        PQ = [al.f32(516) for _ in range(2)]
        QKV = [al.f32(512) for _ in range(3)]
        TMP = [al.f32(512)] * 2
        SQb = al.b16(512)
        KQ = al.b16(1024).re("p (n j t) -> p n j t", n=4, j=2)
        KNb = al.b16(512)
        Vb = al.b16(512)
        OF = al.f32(512)
        YG = [al.b16(512) for _ in range(3)]
        KD = [al.b16(128) for _ in range(4)]
        VB = [al.b16(128) for _ in range(4)]
        QKT = [[al.b16(128) for _ in range(2)] for _ in range(4)]
        TTf = [[al.f32(128) for _ in range(2)] for _ in range(4)]
        SL = [[dict(Y0=al.f32(128), A1=al.f32(128), Yp=[al.f32(128), al.f32(128)], Ap=[al.f32(128), al.f32(128)])
               for _ in range(2)] for _ in range(2)]
        GMh = [al.f32(128) for _ in range(2)]
        EXD = [al.f32(128) for _ in range(2)]
        EM = [al.f32(256) for _ in range(2)]
        EL = [al.f32(128) for _ in range(2)]
        Rb = [al.f32(64) for _ in range(2)]
        VN = [al.b16(64) for _ in range(2)]
        O1 = [al.f32(64) for _ in range(2)]
        Ot = [al.f32(128) for _ in range(4)]
        SS = al.f32(8)
        qi = [0]
        if dbg.get('mem'):
            print('GDN arena used', al.off, 'of', ARENA)

        for b in range(NBn):
            pab = next_ps()
            for n in range(4):
                for k in range(KC):
                    kb.mm(pab[:, n * 12:(n + 1) * 12], HT[b][:, k, n * 128:(n + 1) * 128], WIN[k][:, 1536:1548],
                          start=(k == 0), stop=(k == KC - 1))
            for n in range(4):
                kb.tt("dve", nh(TA, n, 0, 6), pab[:, n * 12:n * 12 + 6], prow[l][:, 0:6], ALU.add)
                kb.act(nh(BT4, n, 0, 6), pab[:, n * 12 + 6:n * 12 + 12], AF.Sigmoid)
            kb.act(TA, TA, AF.Exp)
            kb.act(TA, TA, AF.Ln, bias=1.0)
            for n in range(4):
                kb.tt("dve", nh(G4, n, 0, 6), nh(TA, n, 0, 6), NEA[:, 0:6], ALU.mult)
            pgc = next_ps()
            for n in range(4):
                kb.mm(pgc[:, n * 6:(n + 1) * 6], mu_i, nh(G4, n, 0, 6))
            pgl = next_ps()
            for n in range(4):
                kb.mm(pgl[:, n * 6:(n + 1) * 6], ones_f, nh(G4, n, 0, 6))
            kb.copy("act", GC4, pgc[:, 0:24])
            kb.act(GAM, pgc[:, 0:24], AF.Exp)
            kb.stt(NGB, GAM, -1.0, BT4, ALU.mult, ALU.mult)
            kb.tt("dve", DK, pgl[:, 0:24], GC4, ALU.subtract)
            kb.act(DK, DK, AF.Exp)
            kb.act(GCE, pgl[:, 0:24], AF.Exp)
            pbt = next_ps()
            for k in range(KC):
                kb.mm(pbt[0:6, :], WIN[k][:, 1542:1548], HT[b][:, k, :], start=(k == 0), stop=(k == KC - 1))
            kb.act(BTT[0:6, :], pbt[0:6, :], AF.Sigmoid)
            if os.environ.get("GDN_CUT") == "1":
                return

            for hp in range(3):
                for wi_, j in enumerate((hp, 3 + hp, 6 + hp)):
                    pq = PQ[qi[0] % 2]
                    qi[0] += 1
                    pp = next_ps()
                    proj(b, j * 128, 128, pp)
                    kb.copy("act", pq[:, 3:515], pp)
                    kb.copy("dve", pq[:, 0:3], TQ[:, j, :])
                    t = QKV[wi_]
                    kb.ts("dve", t, pq[:, 3:515], pc(l, "gd_conv_w", 3 * 9 + j), ALU.mult)
                    for k in range(3):
                        kb.stt(t, pq[:, k:k + 512], pc(l, "gd_conv_w", k * 9 + j), t, ALU.mult, ALU.add)
                    kb.copy("dve", TQ[:, j, :], pq[:, 512:515])
                    kb.act(t, t, AF.Silu)
                q, k_, v = QKV
                kb.act(SQb, q, AF.Square)
                pss = next_ps()
                kb.mm(pss, bd_b, SQb)
                kb.act(TMP[0], pss, AF.Sqrt, bias=1e-6)
                kb.I("dve", "reciprocal", out=TMP[0], in_=TMP[0])
                kb.stt(KQ[:, :, 1, :], q.re("p (n t) -> p n t", n=4), 0.125, TMP[0].re("p (n t) -> p n t", n=4), ALU.mult, ALU.mult)
                kb.act(SQb, k_, AF.Square)
                pss = next_ps()
                kb.mm(pss, bd_b, SQb)
                kb.act(TMP[1], pss, AF.Sqrt, bias=1e-6)
                kb.I("dve", "reciprocal", out=TMP[1], in_=TMP[1])
                kb.tt("dve", k_, k_, TMP[1], ALU.mult)
                kb.copy("act", KNb, k_)
                pbb = next_ps()
                kb.mm(pbb, cst[0:6, CST["sel"] + hp * 128:CST["sel"] + (hp + 1) * 128], BTT[0:6, :])
                kb.tt("dve", KQ[:, :, 0, :], k_.re("p (n t) -> p n t", n=4), pbb.re("p (n t) -> p n t", n=4), ALU.mult)
                kb.copy("act", Vb, v)
                if os.environ.get("GDN_CUT") == "2":
                    return
                for n2 in range(2):
                    items = []
                    for n in (2 * n2, 2 * n2 + 1):
                        cs = slice(n * 128, (n + 1) * 128)
                        pk = next_ps_b()
                        kb.tr(pk[:, 0:128], KNb[:, cs], ident_b)
                        pv = next_ps_b()
                        kb.tr(pv[:, 0:128], Vb[:, cs], ident_b)
                        for h in range(2):
                            hh = 2 * hp + h
                            fs = slice(h * 64, (h + 1) * 64)
                            kb.ts("dve", KD[n][:, fs], pk[:, fs], nh(DK, n, hh), ALU.mult)
                            kb.ts("dve", VB[n][:, fs], pv[:, fs], nh(BT4, n, hh), ALU.mult)
                        for h in range(2):
                            hh = 2 * hp + h
                            hs = slice(h * 64, (h + 1) * 64)
                            gm, exd, em, el = GMh[h], EXD[h], EM[h], EL[h]
                            sl = SL[n % 2][h]
                            kb.ts("dve", gm, ml_s, nh(G4, n, hh), ALU.mult)
                            pD = next_ps()
                            kb.mm(pD[:, 0:128], gm, mu_i)
                            kb.act(exd, pD[:, 0:128], AF.Exp)
                            kb.tt("dve", em[:, 0:128], exd, NMU, ALU.mult)
                            kb.tt("dve", em[:, 128:256], exd, mu_i, ALU.mult)
                            pEL = next_ps()
                            kb.tr(pEL[:, 0:128], em[:, 0:128], ident)
                            kb.copy("act", el, pEL[:, 0:128])
                            pGR = next_ps()
                            kb.mm(pGR[:, 0:256], KNb[hs, cs], KQ[hs, n, :, :].re("p j t -> p (j t)"))
                            kb.tt("dve", sl["Y0"], pGR[:, 0:128], em[:, 0:128], ALU.mult)
                            kb.tt("dve", QKT[n][h], pGR[:, 128:256], em[:, 128:256], ALU.mult)
                            pGL = next_ps()
                            kb.mm(pGL[:, 0:128], KQ[hs, n, 0, :], KNb[hs, cs])
                            kb.tt("dve", sl["A1"], pGL[:, 0:128], el, ALU.mult)
                            sl["Sf"] = TTf[n][h]
                            items.append(sl)
                    tchain(items, f32=True)
                for n in range(int(os.environ.get("GDN_N", "4"))):
                    cs = slice(n * 128, (n + 1) * 128)
                    if os.environ.get("GDN_CUT") == "4b" and n == int(os.environ.get("GDN_CUTN", "0")):
                        return
                    pKMh = [next_ps(), next_ps()]
                    pO1h = [next_ps(), next_ps()]
                    for h in range(2):
                        hs = slice(h * 64, (h + 1) * 64)
                        kb.mm(pKMh[h][:, 0:64], KNb[hs, cs], Mb[hp][hs, :])
                        kb.mm(pO1h[h][:, 0:64], KQ[hs, n, 1, :], Mb[hp][hs, :])
                    if os.environ.get("GDN_CUT") == "4c" and n == int(os.environ.get("GDN_CUTN", "0")):
                        return
                    pVN = next_ps()
                    for h in range(2):
                        hh = 2 * hp + h
                        fs = slice(h * 64, (h + 1) * 64)
                        kb.stt(Rb[h], pKMh[h][:, 0:64], nh(NGB, n, hh), VB[n][:, fs], ALU.mult, ALU.add)
                        if os.environ.get("GDN_CUT") == "4d" and n == int(os.environ.get("GDN_CUTN", "0")):
                            return
                        kb.mm(pVN[:, fs], TTf[n][h], Rb[h])
                        if os.environ.get("GDN_CUT") == "4e" and n == int(os.environ.get("GDN_CUTN", "0")):
                            return
                        kb.act(O1[h], pO1h[h][:, 0:64], AF.Identity, scale=nh(GAM, n, hh))
                    if os.environ.get("GDN_CUT") == "5" and n == int(os.environ.get("GDN_CUTN", "0")):
                        return
                    pO2 = next_ps()
                    pM = next_ps()
                    for h in range(2):
                        fs = slice(h * 64, (h + 1) * 64)
                        kb.copy("act", VN[h], pVN[:, fs])
                        kb.mm(pO2[:, fs], QKT[n][h], VN[h])
                        kb.mm(pM[fs, 0:64], KD[n][:, fs], VN[h])
                    if os.environ.get("GDN_CUT") == "6" and n == int(os.environ.get("GDN_CUTN", "0")):
                        return
                    ot = Ot[n]
                    for h in range(2):
                        hh = 2 * hp + h
                        fs = slice(h * 64, (h + 1) * 64)
                        kb.stt(Mf[hp][fs, :], Mf[hp][fs, :], nh(GCE, n, hh)[fs, :], pM[fs, 0:64], ALU.mult, ALU.add)
                        kb.tt("dve", ot[:, fs], O1[h], pO2[:, fs], ALU.add)
                    if not os.environ.get("GDN_NOMB"):
                        kb.copy("act", Mb[hp], Mf[hp])
                    if os.environ.get("GDN_CUT") == "7" and n == int(os.environ.get("GDN_CUTN", "0")):
                        return
                    if os.environ.get("GDN_NONORM"):
                        continue
                for n in range(4):
                    cs = slice(n * 128, (n + 1) * 128)
                    ot = Ot[n]
                    for h in range(2):
                        fs = slice(h * 64, (h + 1) * 64)
                        kb.act(O1[h], ot[:, fs], AF.Square, accum_out=SS[:, h:h + 1])
                    kb.act(SS[:, 0:2], SS[:, 0:2], AF.Sqrt, bias=EPS, scale=1.0 / 64)
                    kb.I("dve", "reciprocal", out=SS[:, 0:2], in_=SS[:, 0:2])
                    for h in range(2):
                        fs = slice(h * 64, (h + 1) * 64)
                        kb.stt(ot[:, fs], ot[:, fs], SS[:, h:h + 1], GNR[:, fs], ALU.mult, ALU.mult)
                    pT = next_ps()
                    kb.tr(pT[:, 0:128], ot, ident)
                    kb.copy("act", OF[:, cs], pT[:, 0:128])
                pz = next_ps()
                proj(b, 1152 + hp * 128, 128, pz)
                kb.act(TMP[0], pz, AF.Silu)
                kb.tt("dve", YG[hp], OF, TMP[0], ALU.mult)
                if os.environ.get("GDN_CUT") == "10":
                    return
            wout_apply(b, 3, lambda c: YG[c])


    def mixer_rwkv(l):
        SW = 256
        NCH = SW // 128
        al = Bump(carve, MIX_BASE, ARENA)
        load_win(l, 512, 1408)
        for c in range(3):
            kb.dma("pool", WOUT[:, c, :], w_out_d[l, 256 + c * 128:256 + (c + 1) * 128, :])
        WA = al.b16(384)
        GUP = al.b16(384)
        kb.dma("pool", WA, rw_wa_d[l])
        kb.dma("pool", GUP, rw_gup_d[l])
        if l >= 1:
            VW1 = al.b16(96).re("p (c n) -> p c n", c=3)
            VW2 = al.b16(384)
            kb.dma("pool", VW1, vw1_d[l - 1].re("(c p) n -> p c n", p=128))
            kb.dma("pool", VW2[0:32, :], vw2_d[l - 1])
        mu_s = cst[:, CST["mu_s"]:CST["mu_s"] + 128]
        mu_i = cst[:, CST["mu_i"]:CST["mu_i"] + 128]
        ml_s = cst[:, CST["ml_s"]:CST["ml_s"] + 128]
        bd_f = cst[:, CST["bd"]:CST["bd"] + 128]
        rst = cst[:, CST["rst"]:CST["rst"] + SW]
        MSK1 = al.f32(256)
        MSK2 = al.f32(256)
        NML = al.f32(128)
        kb.ts("dve", MSK1[:, 0:128], mu_s, -1.0, ALU.mult)
        kb.copy("dve", MSK1[:, 128:256], mu_i)
        kb.copy("dve", MSK2[:, 0:128], mu_s)
        kb.copy("dve", MSK2[:, 128:256], mu_i)
        kb.ts("dve", NML, ml_s, -1.0, ALU.mult)
        OMM = al.f32(11)
        OMKA = al.f32(3)
        kb.ts("dve", OMM, pcol[l][:, PCOL["rw_mu"]:PCOL["rw_mu"] + 11], -1.0, ALU.mult, s2=1.0, op1=ALU.add)
        kb.ts("dve", OMKA, pcol[l][:, PCOL["rw_ka"]:PCOL["rw_ka"] + 3], -1.0, ALU.mult, s2=1.0, op1=ALU.add)
        TR = al.f32(11)
        kb.memset("dve", TR, 0.0)
        Mf = [al.f32(64) for _ in range(3)]
        Mb = [al.b16(64) for _ in range(3)]
        for hp in range(3):
            kb.memset("dve", Mf[hp], 0.0)
            kb.memset("dve", Mb[hp], 0.0)
        PR = [al.f32(SW + 4) for _ in range(2)]
        XWAb = al.b16(SW)
        SXGb = al.b16(SW)
        V3 = [al.f32(SW) for _ in range(3)]
        V3b = [al.b16(SW) for _ in range(3)]
        V1b = al.b16(SW)
        Rf = al.f32(SW)
        Kf = al.f32(SW)
        SGW, GS, E1, Aa, KKf, KPf, TMP = [al.f32(SW) for _ in range(7)]
        SQb = al.b16(SW)
        BIGb = al.b16(SW)
        KIGb = al.b16(SW)
        Vb = al.b16(SW)
        YNF = al.f32(SW)
        YR = [al.b16(SW) for _ in range(3)]
        A1 = [[al.b16(128) for _ in range(2)] for _ in range(NCH)]
        CHS = [[dict(Yp=[al.b16(128), al.b16(128)], Ap=[al.b16(128), al.b16(128)], Sf=al.f32(128))
                for _ in range(2)] for _ in range(NCH)]
        HO = []
        for _ in range(2):
            HO.append(dict(
                KR=al.b16(2 * SW).re("p (n j t) -> p n j t", n=NCH, j=2),
                BT=[al.b16(128) for _ in range(NCH)], KT=[al.b16(128) for _ in range(NCH)],
                VT=[al.b16(128) for _ in range(NCH)],
                ARm=[[al.b16(256) for _ in range(2)] for _ in range(NCH)],
                BRm=[[al.b16(256) for _ in range(2)] for _ in range(NCH)],
                Sb=[[al.b16(128) for _ in range(2)] for _ in range(NCH)],
                BON=al.f32(SW), Gg=al.f32(SW), GCOL=al.f32(NCH)))
        Zn = [al.b16(64) for _ in range(2)]
        Ub = [al.b16(64) for _ in range(2)]
        YS = [al.f32(128) for _ in range(2)]
        SCR = al.f32(64)
        ST = al.f32(8)
        if dbg.get('mem'):
            print('RWKV arena used', al.off, 'of', ARENA)
        pri = [0]

        def lerp(b, t0, j, out):
            pr = PR[pri[0] % 2]
            pri[0] += 1
            pp = next_ps()
            for k in range(KC):
                kb.mm(pp[:, 0:SW], WIN[k][:, j * 128:(j + 1) * 128], HT[b][:, k, t0:t0 + SW], start=(k == 0), stop=(k == KC - 1))
            kb.copy("act", pr[:, 1:SW + 1], pp[:, 0:SW])
            kb.copy("dve", pr[:, 0:1], TR[:, j:j + 1])
            kb.ts("dve", out, pr[:, 1:SW + 1], OMM[:, j:j + 1], ALU.mult)
            kb.stt(out, pr[:, 0:SW], pc(l, "rw_mu", j), out, ALU.mult, ALU.add)
            kb.copy("dve", TR[:, j:j + 1], pr[:, SW:SW + 1])

        def P(sb, hp, ho):
            b, t0 = divmod(sb * SW, TB)
            ts_ = slice(t0, t0 + SW)
            KR, BON, Gg = ho["KR"], ho["BON"], ho["Gg"]
            if hp == 0:
                lerp(b, t0, 9, TMP)
                kb.act(XWAb[0:64, :], TMP[0:64, :], AF.Tanh)
                kb.copy("act", XWAb[64:128, :], TMP[64:128, :])
                yield
                lerp(b, t0, 10, TMP)
                kb.act(SXGb, TMP, AF.Sigmoid)
                yield
                for h3 in range(3):
                    lerp(b, t0, 6 + h3, V3[h3])
                    if l == 0:
                        kb.copy("act", VF[h3][b][:, ts_], V3[h3])
                    else:
                        kb.copy("act", V3b[h3], V3[h3])
                    yield
                if l >= 1:
                    pv1 = next_ps()
                    for h3 in range(3):
                        kb.mm(pv1[0:32, 0:SW], VW1[:, h3, :], V3b[h3], start=(h3 == 0), stop=(h3 == 2))
                    kb.copy("act", V1b[0:32, :], pv1[0:32, 0:SW])
                    yield
            v = V3[hp]
            hc = slice(hp * 128, (hp + 1) * 128)
            if l >= 1:
                pv2 = next_ps()
                kb.mm(pv2[:, 0:SW], VW2[0:32, hc], V1b[0:32, :])
                kb.act(TMP, pv2[:, 0:SW], AF.Sigmoid, bias=pc(l, "rw_vresb", hp))
                kb.tt("dve", KPf, VF[hp][b][:, ts_], v, ALU.subtract)
                kb.tt("dve", KPf, KPf, TMP, ALU.mult)
                kb.tt("dve", v, v, KPf, ALU.add)
                yield
            lerp(b, t0, hp, Rf)
            yield
            lerp(b, t0, 3 + hp, Kf)
            yield
            pw = next_ps()
            kb.mm(pw[:, 0:SW], WA[0:64, hc], XWAb[0:64, :])
            kb.act(SGW, pw[:, 0:SW], AF.Sigmoid, bias=pc(l, "rw_wbias", hp))
            pa = next_ps()
            kb.mm(pa[:, 0:SW], WA[64:128, hc], XWAb[64:128, :])
            kb.act(Aa, pa[:, 0:SW], AF.Sigmoid, bias=pc(l, "rw_abias", hp))
            pg = next_ps()
            kb.mm(pg[:, 0:SW], GUP[:, hc], SXGb)
            kb.copy("act", Gg, pg[:, 0:SW])
            yield
            kb.ts("dve", KKf, Kf, pc(l, "rw_kk", hp), ALU.mult)
            kb.act(SQb, KKf, AF.Square)
            pss = next_ps()
            kb.mm(pss[:, 0:SW], bd_b, SQb)
            kb.act(TMP, pss[:, 0:SW], AF.Sqrt, bias=1e-6)
            kb.I("dve", "reciprocal", out=TMP, in_=TMP)
            kb.tt("dve", KKf, KKf, TMP, ALU.mult)
            yield
            kb.ts("dve", TMP, Aa, pc(l, "rw_ka", hp), ALU.mult, s2=OMKA[:, hp:hp + 1], op1=ALU.add)
            kb.tt("dve", KPf, Kf, TMP, ALU.mult)
            kb.stt(BON, Rf, pc(l, "rw_rk", hp), KPf, ALU.mult, ALU.mult)
            pbs = next_ps()
            kb.mm(pbs[:, 0:SW], bd_f, BON)
            kb.tt("dve", BON, pbs[:, 0:SW], v, ALU.mult)
            yield
            kb.tt("dve", Aa, KKf, Aa, ALU.mult)
            kb.I("dve", "tensor_tensor_scan", out=GS, data0=rst, data1=SGW, initial=0.0, op0=ALU.mult, op1=ALU.add)
            kb.tt("dve", SGW, GS, SGW, ALU.subtract)
            kb.act(SGW, SGW, AF.Exp, scale=-C0)
            kb.act(E1, GS, AF.Exp, scale=-C0)
            kb.act(GS, GS, AF.Exp, scale=C0)
            yield
            kb.tt("dve", KR[:, :, 0, :], KKf.re("p (n t) -> p n t", n=NCH), SGW.re("p (n t) -> p n t", n=NCH), ALU.mult)
            kb.tt("dve", KR[:, :, 1, :], Rf.re("p (n t) -> p n t", n=NCH), E1.re("p (n t) -> p n t", n=NCH), ALU.mult)
            for n in range(NCH):
                kb.copy("act", ho["GCOL"][:, n:n + 1], E1[:, n * 128 + 127:n * 128 + 128])
            yield
            kb.tt("dve", BIGb, Aa, GS, ALU.mult)
            kb.tt("dve", KIGb, KPf, GS, ALU.mult)
            kb.copy("act", Vb, v)
            yield
            items = []
            for n in range(NCH):
                cs = slice(n * 128, (n + 1) * 128)
                for src, dst in ((BIGb, ho["BT"][n]), (KIGb, ho["KT"][n]), (Vb, ho["VT"][n])):
                    pt = next_ps_b()
                    kb.tr(pt[:, 0:128], src[:, cs], ident_b)
                    kb.copy("act", dst, pt[:, 0:128])
                yield
                for h in range(2):
                    hs = slice(h * 64, (h + 1) * 64)
                    krh = KR[hs, n, :, :].re("p j t -> p (j t)")
                    pAR = next_ps()
                    kb.mm(pAR[:, 0:256], BIGb[hs, cs], krh)
                    kb.tt("dve", ho["ARm"][n][h], pAR[:, 0:256], MSK1, ALU.mult)
                    pBR = next_ps()
                    kb.mm(pBR[:, 0:256], KIGb[hs, cs], krh)
                    kb.tt("dve", ho["BRm"][n][h], pBR[:, 0:256], MSK2, ALU.mult)
                    pA = next_ps()
                    kb.mm(pA[:, 0:128], KR[hs, n, 0, :], BIGb[hs, cs])
                    kb.tt("dve", A1[n][h], pA[:, 0:128], NML, ALU.mult)
                    it = CHS[n][h]
                    it["Y0"] = ho["ARm"][n][h][:, 0:128]
                    it["A1"] = A1[n][h]
                    it["Sb"] = ho["Sb"][n][h]
                    items.append(it)
                    yield
            yield from tchain_gen(items)

        def S(sb, hp, ho):
            b, t0 = divmod(sb * SW, TB)
            ts_ = slice(t0, t0 + SW)
            KR = ho["KR"]
            for n in range(NCH):
                cs = slice(n * 128, (n + 1) * 128)
                pZh = [next_ps(), next_ps()]
                for h in range(2):
                    hs = slice(h * 64, (h + 1) * 64)
                    kb.mm(pZh[h][:, 0:64], KR[hs, n, 0, :], Mb[hp][hs, :], start=True, stop=False)
                    kb.mm(pZh[h][:, 0:64], ho["BRm"][n][h][:, 0:128], ho["VT"][n][:, hs], start=False, stop=True)
                yield
                pU = next_ps()
                for h in range(2):
                    hs = slice(h * 64, (h + 1) * 64)
                    kb.act(Zn[h], pZh[h][:, 0:64], AF.Copy, scale=-1.0)
                    kb.mm(pU[:, hs], ho["Sb"][n][h], Zn[h])
                yield
                pYh = [next_ps(), next_ps()]
                pM = next_ps()
                for h in range(2):
                    hs = slice(h * 64, (h + 1) * 64)
                    kb.copy("act" if h else "dve", Ub[h], pU[:, hs])
                yield
                for h in range(2):
                    hs = slice(h * 64, (h + 1) * 64)
                    kb.mm(pYh[h][:, 0:64], KR[hs, n, 1, :], Mb[hp][hs, :], start=True, stop=False)
                    kb.mm(pYh[h][:, 0:64], ho["ARm"][n][h][:, 128:256], Ub[h], start=False, stop=False)
                    kb.mm(pYh[h][:, 0:64], ho["BRm"][n][h][:, 128:256], ho["VT"][n][:, hs], start=False, stop=True)
                for h in range(2):
                    hs = slice(h * 64, (h + 1) * 64)
                    kb.mm(pM[hs, 0:64], ho["BT"][n][:, hs], Ub[h], start=True, stop=False)
                    kb.mm(pM[hs, 0:64], ho["KT"][n][:, hs], ho["VT"][n][:, hs], start=False, stop=True)
                yield
                kb.tt("dve", Mf[hp], Mf[hp], pM[:, 0:64], ALU.add)
                kb.ts("dve", Mf[hp], Mf[hp], ho["GCOL"][:, n:n + 1], ALU.mult)
                kb.copy("act", Mb[hp], Mf[hp])
                yield
                ys = YS[n % 2]
                for h in range(2):
                    hs = slice(h * 64, (h + 1) * 64)
                    kb.act(ys[:, hs], pYh[h][:, 0:64], AF.Copy, accum_out=ST[:, h:h + 1])
                kb.ts("dve", ST[:, 2:4], ST[:, 0:2], -1.0 / 64, ALU.mult)
                yield
                for h in range(2):
                    hs = slice(h * 64, (h + 1) * 64)
                    kb.ts("dve", ys[:, hs], ys[:, hs], ST[:, 2 + h:3 + h], ALU.add)
                    kb.act(SCR, ys[:, hs], AF.Square, accum_out=ST[:, 4 + h:5 + h])
                yield
                kb.act(ST[:, 4:6], ST[:, 4:6], AF.Sqrt, bias=64e-5, scale=1.0 / 64)
                kb.I("dve", "reciprocal", out=ST[:, 4:6], in_=ST[:, 4:6])
                for h in range(2):
                    hs = slice(h * 64, (h + 1) * 64)
                    kb.ts("dve", ys[:, hs], ys[:, hs], ST[:, 4 + h:5 + h], ALU.mult)
                yield
                pT = next_ps()
                kb.tr(pT[:, 0:128], ys, ident)
                kb.act(YNF[:, cs], pT[:, 0:128], AF.Identity, scale=pc(l, "rw_lng", hp), bias=pc(l, "rw_lnb", hp))
                yield
            kb.tt("dve", YNF, YNF, ho["BON"], ALU.add)
            kb.tt("dve", YR[hp], YNF, ho["Gg"], ALU.mult)
            yield
            if hp == 2:
                for d in range(KC):
                    po = next_ps()
                    for c in range(3):
                        kb.mm(po[:, 0:SW], WOUT[:, c, d * 128:(d + 1) * 128], YR[c], start=(c == 0), stop=(c == 2))
                    kb.tt("dve", XT[d][b][:, ts_], po[:, 0:SW], XT[d][b][:, ts_], ALU.add)
                    yield

        iters = [(sb, hp) for sb in range(Tn // SW) for hp in range(3)]
        drain(P(iters[0][0], iters[0][1], HO[0]))
        for i, (sb, hp) in enumerate(iters):
            gS = S(sb, hp, HO[i % 2])
            gP = P(iters[i + 1][0], iters[i + 1][1], HO[(i + 1) % 2]) if i + 1 < len(iters) else None
            interleave(gS, gP)

    def mixer(l):
        rmsnorm(l, "mix_norm", h_out, SQ=MSQ, RS=MRS)
        kb.barrier()
        sel = os.environ.get("MIXERS", "lru,rwkv,gdn").split(",")
        if stages in ("all", "mix", "lru") and (stages == "lru" or "lru" in sel):
            mixer_lru(l)
            kb.barrier()
        if stages in ("all", "mix", "rwkv") and (stages == "rwkv" or "rwkv" in sel):
            mixer_rwkv(l)
            kb.barrier()
        if stages in ("all", "mix", "gdn") and (stages == "gdn" or "gdn" in sel):
            mixer_gdn(l)
            kb.barrier()

    if need_mix:
        MSQ = carve(ARENA - 8192 - 4096, 8192, BF16).re("p (k n) -> p k n", k=KC)
        MRS = [carve(ARENA - 4096 + i * 2048, 2048, F32) for i in range(2)]

    for l in range(Ln):
        if stages in ("io", "io0"):
            break
        if stages == "norm":
            rmsnorm(l, "ffn1_norm", h_out)
            break
        if need_ffn:
            ffn(l, 0)
            if stages == "ffn1":
                break
            kb.barrier()
        if need_mix:
            mixer(l)
            kb.barrier()
        if need_ffn:
            ffn(l, 1)
            kb.barrier()

    kb.barrier()
    YT = [carve(16384 + c * 2048 * NBn, 2048 * NBn, F32).re("p (b n) -> p b n", b=NBn) for c in range(KC)]
    OS = [carve(i * 4096, 4096, F32) for i in range(4)]
    FSQ = carve(16384 + 8 * 2048 * NBn, 8192, BF16).re("p (k n) -> p k n", k=KC) if NBn < 4 else None
    if NBn == 4:
        FSQ = carve(16384 + 65536, 8192, BF16).re("p (k n) -> p k n", k=KC)
    FRS = [carve(3 * 4096 + i * 2048, 2048, F32) for i in range(2)]

    def y_out(b, c, rs, g):
        kb.stt(YT[c][:, b, :], XT[c][b], g, rs, ALU.mult, ALU.mult)

    if stages == "io0":
        for b in range(NBn):
            for c in range(KC):
                kb.copy("dve", YT[c][:, b, :], XT[c][b])
    else:
        rmsnorm(Ln - 1, "final_norm", y_out, SQ=FSQ, RS=FRS)
    kb.barrier()
    out_toks = []
    for tt_ in range(Tn // 128):
        b, o = divmod(tt_ * 128, TB)
        osb = OS[tt_ % 4]
        for half in range(2):
            p = next_ps()
            for j in range(4):
                c = half * 4 + j
                kb.tr(p[:, j * 128:(j + 1) * 128], YT[c][:, b, o:o + 128], ident)
            kb.copy("act" if half else "dve", osb[:, half * 512:(half + 1) * 512], p)
        out_toks.append(kb.dma("sp", y_d[tt_ * 128:(tt_ + 1) * 128, :], osb))
    for t in out_toks:
        kb._wait("sp", t)
    return nc


def _prep_shared(inp, need_ffn=True, need_mix=True):
    f = lambda a: np.asarray(a, dtype=np.float32)
    pc = np.zeros((L, 128, PCOL["_n"]), np.float32)

    def put(l, name, vec, n):
        pc[l, :, PCOL[name]:PCOL[name] + n] = f(vec).reshape(n, 128).T

    for l in range(L):
        put(l, "ffn1_norm", inp["ffn1_norm"][l], 8)
        put(l, "mix_norm", inp["mix_norm"][l], 8)
        put(l, "ffn2_norm", inp["ffn2_norm"][l], 8)
        put(l, "final_norm", inp["final_norm"], 8)
        cw = f(inp["lru_conv_w"][l])
        for k in range(4):
            pc[l, :, PCOL["lru_conv_w"] + 2 * k:PCOL["lru_conv_w"] + 2 * k + 2] = cw[k].reshape(2, 128).T
        put(l, "lru_conv_b", inp["lru_conv_b"][l], 2)
        put(l, "lru_ga_b", f(inp["lru_gate_a_b"][l]).reshape(-1), 2)
        put(l, "lru_gx_b", f(inp["lru_gate_x_b"][l]).reshape(-1), 2)
        put(l, "lru_lam", inp["lru_lambda"][l], 2)
        put(l, "lru_out_g", inp["lru_out_norm"][l], 2)
        put(l, "rw_mu", inp["rwkv_mu"][l], 11)
        put(l, "rw_wbias", inp["rwkv_w_bias"][l], 3)
        put(l, "rw_abias", inp["rwkv_a_bias"][l], 3)
        put(l, "rw_kk", inp["rwkv_k_k"][l], 3)
        put(l, "rw_ka", inp["rwkv_k_a"][l], 3)
        put(l, "rw_rk", f(inp["rwkv_r_k"][l]).reshape(-1), 3)
        put(l, "rw_lng", inp["rwkv_ln_g"][l], 3)
        put(l, "rw_lnb", inp["rwkv_ln_b"][l], 3)
        if l >= 1:
            put(l, "rw_vresb", inp["rwkv_vres_b"][l - 1], 3)
        gw = f(inp["gdn_conv_w"][l])
        for k in range(4):
            pc[l, :, PCOL["gd_conv_w"] + 9 * k:PCOL["gd_conv_w"] + 9 * k + 9] = gw[k].reshape(9, 128).T
    cst = np.zeros((128, CST_N), np.float32)
    i = np.arange(128)
    cst[:, 0:128] = np.eye(128)
    cst[:, 128:256] = 1.0
    cst[:, 256:384] = (i[:, None] < i[None, :])
    cst[:, 384:512] = (i[:, None] <= i[None, :])
    cst[:, 512:640] = (i[:, None] > i[None, :])
    cst[:, 640:768] = (i[:, None] // 64 == i[None, :] // 64)
    cst[:, 768:1280] = (np.arange(512) % 128 != 0)[None, :]
    for h in range(6):
        cst[h, 1280 + h * 64:1280 + (h + 1) * 64] = 1.0
    sh = {"pcol": pc, "cst": cst}
    if need_ffn:
        for k in ("ffn1_wi", "ffn2_wi", "ffn1_wo", "ffn2_wo"):
            sh[k] = np.ascontiguousarray(f(inp[k]))
    if need_mix:
        pr = np.zeros((L, 128, PROW_N), np.float32)
        lg = np.zeros((L, 128, 4, 128), np.float32)
        wa = np.zeros((L, 128, 384), np.float32)
        for l in range(L):
            pr[l, :, 0:6] = f(inp["gdn_dt_bias"][l])[None, :]
            pr[l, :, 6:12] = f(inp["gdn_a_log"][l])[None, :]
            pr[l, :, 12:76] = f(inp["gdn_norm"][l])[None, :]
            pr[l, :, 76:140] = f(inp["gdn_norm"][l])[None, :]
            for gi, nm in enumerate(("lru_gate_a_w", "lru_gate_x_w")):
                w = f(inp[nm][l])
                for c in range(2):
                    for j in range(2):
                        lg[l, j * 64:(j + 1) * 64, gi * 2 + c, j * 64:(j + 1) * 64] = w[2 * c + j]
            wa[l, 0:64] = f(inp["rwkv_w_up"][l])
            wa[l, 64:128] = f(inp["rwkv_a_up"][l])
        sh.update({"prow": pr, "lru_g": lg, "rw_wa": wa})
        for k in ("w_in", "w_out", "rwkv_g_up", "rwkv_vres_w1", "rwkv_vres_w2"):
            sh[k] = np.ascontiguousarray(f(inp[k]))
    return sh


def kernel(**inputs):
    x = np.ascontiguousarray(inputs["x"], dtype=np.float32)
    B = x.shape[0]
    sh = _prep_shared(inputs)
    nc = bass.Bass("TRN2", target_bir_lowering=False)
    build(nc)
    in_maps = []
    for b in range(B):
        m = dict(sh)
        m["x"] = x[b]
        in_maps.append(m)
    res = run_bass_kernel_spmd(nc, in_maps, core_ids=list(range(B)))
    return np.stack([np.asarray(r["y"]) for r in res.results], axis=0).astype(np.float32)
```

```python
import os
import numpy as np
import concourse.bass as bass
import concourse.mybir as mybir
from concourse.bass_utils import run_bass_kernel_spmd

F32 = mybir.dt.float32
BF16 = mybir.dt.bfloat16
F32R = mybir.dt.float32r
USE_F32R = bool(os.environ.get('USE_F32R'))
AF = mybir.ActivationFunctionType
ALU = mybir.AluOpType

D = 1024
T = 2048
L = 2
DFF = 2816
NFC = DFF // 128
KC = D // 128
TB = 512
NB = T // TB
EPS = 1e-6
DIN = 3468

WRITE_KEYS = ("out", "accum_out", "ap")
ATTACH_WAITS = not os.environ.get("NO_ATTACH")


class Dep:
    __slots__ = ("w", "r", "excl")

    def __init__(self, excl=False):
        self.w = None
        self.r = {}
        self.excl = excl


class V:
    __slots__ = ("ap", "dep")

    def __init__(self, ap, dep=None):
        self.ap = ap
        self.dep = dep if dep is not None else Dep()

    def __getitem__(self, idx):
        return V(self.ap[idx], self.dep)

    def re(self, s, **kw):
        return V(self.ap.rearrange(s, **kw), self.dep)

    def bc(self, dt):
        return V(self.ap.bitcast(dt), self.dep)


class KB:
    def __init__(self, nc):
        self.nc = nc
        self.E = {"pe": nc.tensor, "act": nc.scalar, "dve": nc.vector, "pool": nc.gpsimd, "sp": nc.sync}
        self.sems = {}
        self.ecnt = {}
        self.waited = {e: {} for e in self.E}
        for e in self.E:
            self.sems[("e", e)] = nc.alloc_semaphore("s_" + e)
            self.ecnt[e] = 0
        self.dma_n = {"sp": 24, "pool": 12, "act": 4}
        self.dma_rr = {q: 0 for q in self.dma_n}
        self.dma_val = {}
        for q, n in self.dma_n.items():
            for i in range(n):
                self.sems[("d", q, i)] = nc.alloc_semaphore("d_%s%d" % (q, i))
                self.dma_val[("d", q, i)] = 0
        self.all_dma_toks = []
        self.sb_off = 0

    def _wait(self, eng, tok):
        if tok is None:
            return
        key, val = tok
        if val <= 0 or self.waited[eng].get(key, 0) >= val:
            return
        self.E[eng].wait_ge(self.sems[key], val)
        self.waited[eng][key] = val

    def _need(self, eng, reads, writes):
        mykey = ("e", eng)
        skip_self = eng == "pe"
        need = {}

        def add(tok):
            key, val = tok
            if val <= 0 or self.waited[eng].get(key, 0) >= val:
                return
            if need.get(key, 0) < val:
                need[key] = val
        for v in reads:
            w = v.dep.w
            if w is not None and not (skip_self and w[0] == mykey):
                add(w)
            if v.dep.excl:
                for key, val in v.dep.r.items():
                    if key != mykey:
                        add((key, val))
        for v in writes:
            d = v.dep
            if d.w is not None and not (skip_self and d.w[0] == mykey):
                add(d.w)
            for key, val in d.r.items():
                if skip_self and key == mykey:
                    continue
                add((key, val))
        return need

    def _deps(self, eng, reads, writes):
        for key, val in self._need(eng, reads, writes).items():
            self._wait(eng, (key, val))

    def _mark(self, tok, reads, writes):
        key, val = tok
        for v in reads:
            if v.dep.r.get(key, 0) < val:
                v.dep.r[key] = val
        for v in writes:
            v.dep.w = tok
            v.dep.r = {}

    def I(self, eng, fname, **kw):
        reads, writes, real = [], [], {}
        for k, v in kw.items():
            if isinstance(v, V):
                (writes if k in WRITE_KEYS else reads).append(v)
                real[k] = v.ap
            else:
                real[k] = v
        need = list(self._need(eng, reads, writes).items())
        fuse = None
        if need and ATTACH_WAITS and "accum_out" not in kw:
            fuse = need.pop()
        for key, val in need:
            self._wait(eng, (key, val))
        ins = getattr(self.E[eng], fname)(**real)
        if fuse is not None:
            ins._wait_ge(self.sems[fuse[0]], fuse[1])
            self.waited[eng][fuse[0]] = fuse[1]
        self.ecnt[eng] += 1
        ins.then_inc(self.sems[("e", eng)], 1)
        self._mark((("e", eng), self.ecnt[eng]), reads, writes)
        return ins

    def dma(self, q, out, in_, **kw):
        self._deps(q, [in_], [out])
        idx = self.dma_rr[q] % self.dma_n[q]
        self.dma_rr[q] += 1
        key = ("d", q, idx)
        prev = self.dma_val[key]
        self._wait(q, (key, prev))
        ins = self.E[q].dma_start(out=out.ap, in_=in_.ap, **kw)
        ins.then_inc(self.sems[key], 16)
        self.dma_val[key] = prev + 16
        tok = (key, prev + 16)
        self._mark(tok, [in_], [out])
        return tok

    def barrier(self, engines=("pe", "act", "dve", "pool", "sp")):
        toks = [(("e", e), self.ecnt[e]) for e in self.E]
        toks += [(k, v) for k, v in self.dma_val.items()]
        for e in engines:
            for t in toks:
                if t[0] == ("e", e):
                    continue
                self._wait(e, t)

    def mm(self, out, lhsT, rhs, start=True, stop=True):
        return self.I("pe", "matmul", out=out, lhsT=lhsT, rhs=rhs, start=start, stop=stop)

    def tr(self, out, in_, ident):
        return self.I("pe", "transpose", out=out, in_=in_, identity=ident)

    def act(self, out, in_, func, bias=None, scale=None, accum_out=None):
        kw = {}
        if bias is not None:
            kw["bias"] = bias
        if scale is not None:
            kw["scale"] = scale
        if accum_out is not None:
            kw["accum_out"] = accum_out
        return self.I("act", "activation", out=out, in_=in_, func=func, **kw)

    def tt(self, eng, out, in0, in1, op):
        return self.I(eng, "tensor_tensor", out=out, in0=in0, in1=in1, op=op)

    def ts(self, eng, out, in0, s1, op0, s2=None, op1=None, accum_out=None):
        kw = {}
        if op1 is not None:
            kw["op1"] = op1
        if accum_out is not None:
            kw["accum_out"] = accum_out
        return self.I(eng, "tensor_scalar", out=out, in0=in0, scalar1=s1, scalar2=s2, op0=op0, **kw)

    def stt(self, out, in0, scalar, in1, op0, op1):
        return self.I("dve", "scalar_tensor_tensor", out=out, in0=in0, scalar=scalar, in1=in1, op0=op0, op1=op1)

    def copy(self, eng, out, in_):
        if eng == "act":
            return self.I("act", "activation", out=out, in_=in_, func=AF.Copy)
        return self.I(eng, "tensor_copy", out=out, in_=in_)

    def memset(self, eng, ap, val):
        return self.I(eng, "memset", ap=ap, constant=val)

    def sb(self, name, shape, dtype):
        return V(self.nc.alloc_sbuf_tensor(name, list(shape), dtype)[:])


def _cols_layout():
    m = {}
    n = 0

    def add(name, k):
        nonlocal n
        m[name] = n
        n += k
    add("ffn1_norm", 8)
    add("mix_norm", 8)
    add("ffn2_norm", 8)
    add("final_norm", 8)
    add("lru_conv_w", 8)
    add("lru_conv_b", 2)
    add("lru_ga_b", 2)
    add("lru_gx_b", 2)
    add("lru_lam", 2)
    add("lru_out_g", 2)
    add("rw_mu", 11)
    add("rw_wbias", 3)
    add("rw_abias", 3)
    add("rw_kk", 3)
    add("rw_ka", 3)
    add("rw_rk", 3)
    add("rw_lng", 3)
    add("rw_lnb", 3)
    add("rw_vresb", 3)
    add("gd_conv_w", 36)
    m["_n"] = n
    return m


PCOL = _cols_layout()


C0 = float(np.exp(-0.5))
GELU_C = 0.7978845608028654
CST = {"ident": 0, "ones": 128, "mu_s": 256, "mu_i": 384, "ml_s": 512, "bd": 640, "rst": 768, "sel": 1280}
CST_N = 1280 + 384
PROW = {"gd_dt": 0, "gd_alog": 6, "gd_norm": 12}
PROW_N = 12 + 128


class Bump:
    def __init__(self, carve, base, end):
        self.carve, self.off, self.end = carve, base, end

    def _a(self, n, esz, dtype):
        nbytes = (n * esz + 31) // 32 * 32
        v = self.carve(self.off, nbytes, dtype)
        self.off += nbytes
        assert self.off <= self.end, ("arena overflow", self.off, self.end)
        return v[:, 0:n]

    def f32(self, n):
        return self._a(n, 4, F32)

    def b16(self, n):
        return self._a(n, 2, BF16)


def build(nc, dbg=None):
    dbg = dbg or {}
    stages = dbg.get("stages", "all")
    Tn = dbg.get("T", T)
    NBn = Tn // TB
    Ln = dbg.get("L", L)
    kb = KB(nc)

    def din(name, shape):
        return V(nc.dram_tensor(name, list(shape), F32, kind="ExternalInput").ap())

    x_d = din("x", [Tn, D])
    need_ffn = stages in ("all", "ffn1")
    need_mix = stages in ("all", "lru", "rwkv", "gdn", "mix")
    if need_ffn:
        wi_d = [din("ffn1_wi", [L, D, 2 * DFF]), din("ffn2_wi", [L, D, 2 * DFF])]
        wo_d = [din("ffn1_wo", [L, DFF, D]), din("ffn2_wo", [L, DFF, D])]
    if need_mix:
        w_in_d = din("w_in", [L, D, DIN])
        w_out_d = din("w_out", [L, D, D])
        lru_g_d = din("lru_g", [L, 128, 4, 128])
        rw_wa_d = din("rw_wa", [L, 128, 384])
        rw_gup_d = din("rwkv_g_up", [L, 128, 384])
        vw1_d = din("rwkv_vres_w1", [L - 1, 384, 32])
        vw2_d = din("rwkv_vres_w2", [L - 1, 32, 384])
        prow_d = din("prow", [L, 128, PROW_N])
    pcol_d = din("pcol", [L, 128, PCOL["_n"]])
    cst_d = din("cst", [128, CST_N])
    y_d = V(nc.dram_tensor("y", [Tn, D], F32, kind="ExternalOutput").ap())

    XT = [[kb.sb("xt%d_%d" % (c, b), [128, TB], F32) for b in range(NBn)] for c in range(KC)]
    HT = [kb.sb("ht%d" % b, [128, KC, TB], BF16) for b in range(NBn)]
    VF = [[kb.sb("vf%d_%d" % (hp, b), [128, TB], BF16) for b in range(NBn)] for hp in range(3)]
    cst = kb.sb("cst_sb", [128, CST_N], F32)
    ident = cst[:, 0:128]
    ones_f = cst[:, 128:256]
    ones_b = kb.sb("ones_b", [128, 128], BF16)
    ident_b = kb.sb("ident_b", [128, 128], BF16)
    bd_b = kb.sb("bd_b", [128, 128], BF16)
    pcol = [kb.sb("pcol%d" % l, [128, PCOL["_n"]], F32) for l in range(L)]
    if need_mix:
        prow = [kb.sb("prow%d" % l, [128, PROW_N], F32) for l in range(L)]
    ps = [V(nc.alloc_psum_tensor("ps%d" % i, [128, 512], F32)[:], Dep(excl=True)) for i in range(8)]

    ARENA = 88 * 1024
    arena_h = nc.alloc_sbuf_tensor("arena", [128, ARENA // 4], F32)

    def carve(off, nbytes, dtype, dep=None):
        assert off % 4 == 0 and nbytes % 4 == 0 and off + nbytes <= ARENA, (off, nbytes)
        v = V(arena_h[:][:, off // 4:(off + nbytes) // 4], dep)
        return v.bc(dtype) if dtype != F32 else v

    kb.dma("sp", cst, cst_d)
    kb.copy("dve", ones_b, ones_f)
    kb.copy("dve", ident_b, ident)
    kb.copy("dve", bd_b, cst[:, CST["bd"]:CST["bd"] + 128])
    for l in range(L):
        kb.dma("sp", pcol[l], pcol_d[l])
        if need_mix:
            kb.dma("sp", prow[l], prow_d[l])

    def pc(l, name, j=0):
        c = PCOL[name] + j
        return pcol[l][:, c:c + 1]

    NFMAX = 5
    off = 0
    WI = [carve(off + i * 20480, 20480, BF16).re("p (k n) -> p k n", k=KC) for i in range(2)]
    off += 2 * 20480
    WO = [carve(off + i * 10240, 10240, BF16).re("p (f n) -> p f n", f=NFMAX) for i in range(2)]
    off += 2 * 10240
    ATb = [carve(off + i * 5120, 5120, BF16).re("p (f n) -> p f n", f=NFMAX) for i in range(2)]
    off += 2 * 5120
    SG = [carve(off + i * 2048, 2048, F32) for i in range(2)]
    off += 2 * 2048
    SQ = carve(off, 8192, BF16).re("p (k n) -> p k n", k=KC)
    off += 8192
    RS = [carve(off + i * 2048, 2048, F32) for i in range(2)]
    off += 2 * 2048
    assert off <= ARENA
    UPO = 640
    XS = [carve(i * 4096, 4096, F32) for i in range(4)]

    psi = [0]
    ps_b16 = [p.bc(BF16) for p in ps]

    pool_sel = ["all"]
    pool_ctr = {"all": 0, "S": 0, "P": 0}

    def _ps_index():
        p = pool_sel[0]
        if p == "all":
            i = psi[0] % 8
            psi[0] += 1
            return i
        i = pool_ctr[p] % 4 + (0 if p == "S" else 4)
        pool_ctr[p] += 1
        return i

    def next_ps():
        return ps[_ps_index()]

    def next_ps_b():
        return ps_b16[_ps_index()]

    def drain(gen, pool="all"):
        pool_sel[0] = pool
        for _ in gen:
            pass
        pool_sel[0] = "all"

    def interleave(gS, gP, ratio=2):
        doneS = gS is None
        doneP = gP is None
        while not (doneS and doneP):
            if not doneS:
                pool_sel[0] = "S"
                try:
                    next(gS)
                except StopIteration:
                    doneS = True
            for _ in range(ratio):
                if not doneP:
                    pool_sel[0] = "P"
                    try:
                        next(gP)
                    except StopIteration:
                        doneP = True
        pool_sel[0] = "all"

    for tt_ in range(Tn // 128):
        xs = XS[tt_ % 4]
        kb.dma("sp", xs, x_d[tt_ * 128:(tt_ + 1) * 128, :])
        b, o = divmod(tt_ * 128, TB)
        for half in range(2):
            p = next_ps()
            for j in range(4):
                c = half * 4 + j
                kb.tr(p[:, j * 128:(j + 1) * 128], xs[:, c * 128:(c + 1) * 128], ident)
            for j in range(4):
                c = half * 4 + j
                kb.copy("act" if half else "dve", XT[c][b][:, o:o + 128], p[:, j * 128:(j + 1) * 128])
    kb.barrier()

    def rmsnorm(l, gname, out_fn, SQ=SQ, RS=RS):
        for b in range(NBn):
            for c in range(KC):
                kb.act(SQ[:, c, :], XT[c][b], AF.Square)
            p = next_ps()
            for c in range(KC):
                kb.mm(p, ones_b, SQ[:, c, :], start=(c == 0), stop=(c == KC - 1))
            rs = RS[b % 2]
            kb.act(rs, p, AF.Sqrt, bias=EPS, scale=1.0 / D)
            kb.I("dve", "reciprocal", out=rs, in_=rs)
            for c in range(KC):
                out_fn(b, c, rs, pc(l, gname, c))

    def h_out(b, c, rs, g):
        kb.stt(HT[b][:, c, :], XT[c][b], g, rs, ALU.mult, ALU.mult)

    groups = [(0, 5), (5, 5), (10, 4), (14, 4), (18, 4)]

    def ffn_load(l, which, gi):
        f0, nf = groups[gi]
        buf = gi % 2
        wi = wi_d[which]
        wo = wo_d[which]
        for k in range(KC):
            kb.dma("pool", WI[buf][:, k, 0:nf * 128], wi[l, k * 128:(k + 1) * 128, f0 * 128:(f0 + nf) * 128])
            kb.dma("pool", WI[buf][:, k, UPO:UPO + nf * 128],
                   wi[l, k * 128:(k + 1) * 128, DFF + f0 * 128:DFF + (f0 + nf) * 128])
        for f in range(nf):
            kb.dma("pool", WO[buf][:, f, :], wo[l, (f0 + f) * 128:(f0 + f + 1) * 128, :])

    def ffn(l, which):
        ffn_load(l, which, 0)
        rmsnorm(l, "ffn1_norm" if which == 0 else "ffn2_norm", h_out)
        it = 0
        for gi, (f0, nf) in enumerate(groups):
            buf = gi % 2
            if gi + 1 < len(groups):
                ffn_load(l, which, gi + 1)
            for b in range(NBn):
                at = ATb[it % 2]
                for f in range(nf):
                    pg = next_ps()
                    pu = next_ps()
                    for k in range(KC):
                        kb.mm(pg, WI[buf][:, k, f * 128:(f + 1) * 128], HT[b][:, k, :], start=(k == 0), stop=(k == KC - 1))
                    for k in range(KC):
                        kb.mm(pu, WI[buf][:, k, UPO + f * 128:UPO + (f + 1) * 128], HT[b][:, k, :],
                              start=(k == 0), stop=(k == KC - 1))
                    sg = SG[f % 2]
                    kb.act(sg, pg, AF.Silu)
                    kb.tt("dve", at[:, f, :], sg, pu, ALU.mult)
                for c in range(KC):
                    po = next_ps()
                    for f in range(nf):
                        kb.mm(po, WO[buf][:, f, c * 128:(c + 1) * 128], at[:, f, :], start=(f == 0), stop=(f == nf - 1))
                    kb.stt(XT[c][b], po, 0.5, XT[c][b], ALU.mult, ALU.add)
                it += 1

    MIX_BASE = 0
    if need_mix:
        mo = 0
        WIN = [carve(mo + k * 3104, 3104, BF16) for k in range(KC)]
        mo += KC * 3104
        WOUT = carve(mo, 6144, BF16).re("p (c n) -> p c n", c=3)
        mo += 6144
        MIX_BASE = mo

    def proj(b, col0, ncols, out_ps):
        for k in range(KC):
            kb.mm(out_ps, WIN[k][:, col0:col0 + ncols], HT[b][:, k, :], start=(k == 0), stop=(k == KC - 1))

    def wout_apply(b, nchunks, ysrc):
        for d in range(KC):
            po = next_ps()
            for c in range(nchunks):
                kb.mm(po, WOUT[:, c, d * 128:(d + 1) * 128], ysrc(c), start=(c == 0), stop=(c == nchunks - 1))
            kb.tt("dve", XT[d][b], po, XT[d][b], ALU.add)

    def load_win(l, c0, n):
        for k in range(KC):
            kb.dma("pool", WIN[k][:, 0:n], w_in_d[l, k * 128:(k + 1) * 128, c0:c0 + n])

    def mixer_lru(l):
        al = Bump(carve, MIX_BASE, ARENA)
        LG = al.b16(512).re("p (g n) -> p g n", g=4)
        load_win(l, 0, 512)
        kb.dma("pool", LG, lru_g_d[l])
        for c in range(2):
            kb.dma("pool", WOUT[:, c, :], w_out_d[l, c * 128:(c + 1) * 128, :])
        cols = al.f32(8)
        PXB = [al.f32(516) for _ in range(2)]
        PY = [al.f32(512) for _ in range(2)]
        XC = [al.f32(512) for _ in range(2)]
        XCb = [al.b16(512) for _ in range(2)]
        Rg = [al.f32(512) for _ in range(2)]
        Ig = [al.f32(512) for _ in range(2)]
        Ag = [al.f32(512) for _ in range(2)]
        A2 = [al.f32(512) for _ in range(2)]
        Hh = [al.f32(512) for _ in range(2)]
        Ut = [al.f32(512) for _ in range(2)]
        SQl = [al.b16(512) for _ in range(2)]
        YL = [al.b16(512) for _ in range(2)]
        RSl = al.f32(512)
        if dbg.get('mem'):
            print('LRU arena used', al.off, 'of', ARENA)
        for c in range(2):
            t = cols[:, c:c + 1]
            kb.act(t, pc(l, "lru_lam", c), AF.Exp, scale=-1.0)
            kb.act(t, t, AF.Ln, bias=1.0)
            kb.ts("dve", cols[:, 2 + c:3 + c], t, -16.0, ALU.mult)
            kb.ts("dve", t, t, -8.0, ALU.mult)
            kb.memset("dve", cols[:, 4 + c:5 + c], 0.0)
            kb.memset("dve", PXB[c][:, 0:3], 0.0)
        for b in range(NBn):
            for c in range(2):
                px = next_ps()
                proj(b, c * 128, 128, px)
                kb.copy("act", PXB[c][:, 3:515], px)
                py = next_ps()
                proj(b, 256 + c * 128, 128, py)
                kb.copy("act", PY[c], py)
                xc = XC[c]
                kb.ts("dve", xc, PXB[c][:, 3:515], pc(l, "lru_conv_w", 3 * 2 + c), ALU.mult,
                      s2=pc(l, "lru_conv_b", c), op1=ALU.add)
                for k in range(3):
                    kb.stt(xc, PXB[c][:, k:k + 512], pc(l, "lru_conv_w", k * 2 + c), xc, ALU.mult, ALU.add)
                kb.copy("dve", PXB[c][:, 0:3], PXB[c][:, 512:515])
                kb.copy("act", XCb[c], xc)
                pr = next_ps()
                kb.mm(pr, LG[:, c, :], XCb[c])
                kb.act(Rg[c], pr, AF.Sigmoid, bias=pc(l, "lru_ga_b", c))
                pi_ = next_ps()
                kb.mm(pi_, LG[:, 2 + c, :], XCb[c])
                kb.act(Ig[c], pi_, AF.Sigmoid, bias=pc(l, "lru_gx_b", c))
                kb.act(Ag[c], Rg[c], AF.Exp, scale=cols[:, c:c + 1])
                kb.act(A2[c], Rg[c], AF.Exp, scale=cols[:, 2 + c:3 + c])
                kb.act(A2[c], A2[c], AF.Sqrt, scale=-1.0, bias=1.0)
                if b == 0:
                    kb.memset("dve", A2[c][:, 0:1], 1.0)
                kb.tt("dve", Ig[c], Ig[c], xc, ALU.mult)
                kb.tt("dve", Ig[c], Ig[c], A2[c], ALU.mult)
                kb.I("dve", "tensor_tensor_scan", out=Hh[c], data0=Ag[c], data1=Ig[c],
                     initial=cols[:, 4 + c:5 + c], op0=ALU.mult, op1=ALU.add)
                kb.copy("act", cols[:, 4 + c:5 + c], Hh[c][:, 511:512])
                u = Ut[c]
                kb.act(u, PY[c], AF.Square)
                kb.ts("dve", u, u, 0.044715, ALU.mult, s2=1.0, op1=ALU.add)
                kb.tt("dve", u, u, PY[c], ALU.mult)
                kb.act(u, u, AF.Sigmoid, scale=2.0 * GELU_C)
                kb.tt("dve", u, u, PY[c], ALU.mult)
                kb.tt("dve", Hh[c], Hh[c], u, ALU.mult)
                kb.act(SQl[c], Hh[c], AF.Square)
            pss = next_ps()
            for c in range(2):
                kb.mm(pss, ones_b, SQl[c], start=(c == 0), stop=(c == 1))
            kb.act(RSl, pss, AF.Sqrt, bias=EPS, scale=1.0 / 256)
            kb.I("dve", "reciprocal", out=RSl, in_=RSl)
            for c in range(2):
                kb.stt(YL[c], Hh[c], pc(l, "lru_out_g", c), RSl, ALU.mult, ALU.mult)
            wout_apply(b, 2, lambda c: YL[c])


    def tchain_gen(Y0s, A1s, bufs, f32=False):
        nch = len(Y0s)
        W = nch * 128
        Sf, Sb = bufs["Sf"], bufs.get("Sb")

        def sl(v, j):
            return v[:, j * 128:(j + 1) * 128]
        for j in range(nch):
            kb.tt("dve", sl(Sf, j), Y0s[j], ident, ALU.add)
        if not f32:
            kb.copy("act", Sb[:, 0:W], Sf[:, 0:W])
        Yk = list(Y0s)
        Ak = list(A1s)
        yield
        for lev in range(6):
            An, Yn = bufs["Ap"][lev % 2], bufs["Yp"][lev % 2]
            pa = next_ps()
            for j in range(nch):
                kb.mm(sl(pa, j), Yk[j], Ak[j])
            kb.copy("act", An[:, 0:W], pa[:, 0:W])
            yield
            if lev < 5:
                py = next_ps()
                for j in range(nch):
                    kb.mm(sl(py, j), Ak[j], Yk[j])
                kb.copy("dve", Yn[:, 0:W], py[:, 0:W])
                Yk = [sl(Yn, j) for j in range(nch)]
                yield
            Ak = [sl(An, j) for j in range(nch)]
            psn = next_ps()
            for j in range(nch):
                kb.mm(sl(psn, j), Ak[j], sl(Sf, j) if f32 else sl(Sb, j))
            kb.tt("dve", Sf[:, 0:W], psn[:, 0:W], Sf[:, 0:W], ALU.add)
            if not f32:
                kb.copy("act", Sb[:, 0:W], Sf[:, 0:W])
            yield

    def mixer_gdn(l):
        al = Bump(carve, MIX_BASE, ARENA)
        load_win(l, 1920, 1548)
        for c in range(3):
            kb.dma("pool", WOUT[:, c, :], w_out_d[l, 640 + c * 128:640 + (c + 1) * 128, :])
        NEA = al.f32(8)
        kb.act(NEA[:, 0:6], prow[l][:, 6:12], AF.Exp)
        kb.ts("dve", NEA[:, 0:6], NEA[:, 0:6], -1.0, ALU.mult)
        NMU = al.f32(128)
        kb.ts("dve", NMU, cst[:, CST["mu_s"]:CST["mu_s"] + 128], -1.0, ALU.mult)
        mu_i = cst[:, CST["mu_i"]:CST["mu_i"] + 128]
        ml_s = cst[:, CST["ml_s"]:CST["ml_s"] + 128]
        GNR = prow[l][:, 12:140]
        Mf = [al.f32(64) for _ in range(3)]
        Mb = [al.b16(64) for _ in range(3)]
        TQ = al.f32(27).re("p (j k) -> p j k", j=9)
        for hp in range(3):
            kb.memset("dve", Mf[hp], 0.0)
            kb.memset("dve", Mb[hp], 0.0)
        kb.memset("dve", TQ, 0.0)
        def t46():
            return al.f32(24)
        G4, BT4, GC4, GAM, NGB, DK, GCE, TA = [t46() for _ in range(8)]

        def nh(v, n, h0, h1=None):
            h1 = h0 + 1 if h1 is None else h1
            return v[:, n * 6 + h0:n * 6 + h1]
        BTT = al.f32(512)
        PQ = [al.f32(516) for _ in range(2)]
        QKV = [al.f32(512) for _ in range(3)]
        TMP = [al.f32(512)] * 2
        SQb = al.b16(512)
        KQ = al.b16(1024).re("p (n j t) -> p n j t", n=4, j=2)
        KNb = al.b16(512)
        Vb = al.b16(512)
        OF = al.f32(512)
        YG = [al.b16(512) for _ in range(3)]
        KD = [al.b16(128) for _ in range(4)]
        VB = [al.b16(128) for _ in range(4)]
        QKT = [[al.b16(128) for _ in range(2)] for _ in range(4)]
        TTp = [al.f32(512) for _ in range(2)]
        TTf = [[TTp[n // 2][:, ((n % 2) * 2 + h) * 128:((n % 2) * 2 + h + 1) * 128] for h in range(2)] for n in range(4)]
        SL = [[dict(Y0=al.f32(128), A1=al.f32(128)) for _ in range(2)] for _ in range(2)]
        CBUF = dict(Yp=[al.f32(512), al.f32(512)], Ap=[al.f32(512), al.f32(512)])
        GMh = [al.f32(128) for _ in range(2)]
        EXD = [al.f32(128) for _ in range(2)]
        EM = [al.f32(256) for _ in range(2)]
        EL = [al.f32(128) for _ in range(2)]
        Rb = [al.f32(64) for _ in range(2)]
        VN = [al.b16(64) for _ in range(2)]
        O1 = [al.f32(64) for _ in range(2)]
        Ot = [al.f32(128) for _ in range(4)]
        SS = al.f32(8)
        qi = [0]
        if dbg.get('mem'):
            print('GDN arena used', al.off, 'of', ARENA)

        for b in range(NBn):
            pab = next_ps()
            for n in range(4):
                for k in range(KC):
                    kb.mm(pab[:, n * 12:(n + 1) * 12], HT[b][:, k, n * 128:(n + 1) * 128], WIN[k][:, 1536:1548],
                          start=(k == 0), stop=(k == KC - 1))
            for n in range(4):
                kb.tt("dve", nh(TA, n, 0, 6), pab[:, n * 12:n * 12 + 6], prow[l][:, 0:6], ALU.add)
                kb.act(nh(BT4, n, 0, 6), pab[:, n * 12 + 6:n * 12 + 12], AF.Sigmoid)
            kb.act(TA, TA, AF.Exp)
            kb.act(TA, TA, AF.Ln, bias=1.0)
            for n in range(4):
                kb.tt("dve", nh(G4, n, 0, 6), nh(TA, n, 0, 6), NEA[:, 0:6], ALU.mult)
            pgc = next_ps()
            for n in range(4):
                kb.mm(pgc[:, n * 6:(n + 1) * 6], mu_i, nh(G4, n, 0, 6))
            pgl = next_ps()
            for n in range(4):
                kb.mm(pgl[:, n * 6:(n + 1) * 6], ones_f, nh(G4, n, 0, 6))
            kb.copy("act", GC4, pgc[:, 0:24])
            kb.act(GAM, pgc[:, 0:24], AF.Exp)
            kb.stt(NGB, GAM, -1.0, BT4, ALU.mult, ALU.mult)
            kb.tt("dve", DK, pgl[:, 0:24], GC4, ALU.subtract)
            kb.act(DK, DK, AF.Exp)
            kb.act(GCE, pgl[:, 0:24], AF.Exp)
            pbt = next_ps()
            for k in range(KC):
                kb.mm(pbt[0:6, :], WIN[k][:, 1542:1548], HT[b][:, k, :], start=(k == 0), stop=(k == KC - 1))
            kb.act(BTT[0:6, :], pbt[0:6, :], AF.Sigmoid)
            if os.environ.get("GDN_CUT") == "1":
                return

            for hp in range(3):
                for wi_, j in enumerate((hp, 3 + hp, 6 + hp)):
                    pq = PQ[qi[0] % 2]
                    qi[0] += 1
                    pp = next_ps()
                    proj(b, j * 128, 128, pp)
                    kb.copy("act", pq[:, 3:515], pp)
                    kb.copy("dve", pq[:, 0:3], TQ[:, j, :])
                    t = QKV[wi_]
                    kb.ts("dve", t, pq[:, 3:515], pc(l, "gd_conv_w", 3 * 9 + j), ALU.mult)
                    for k in range(3):
                        kb.stt(t, pq[:, k:k + 512], pc(l, "gd_conv_w", k * 9 + j), t, ALU.mult, ALU.add)
                    kb.copy("dve", TQ[:, j, :], pq[:, 512:515])
                    kb.act(t, t, AF.Silu)
                q, k_, v = QKV
                kb.act(SQb, q, AF.Square)
                pss = next_ps()
                kb.mm(pss, bd_b, SQb)
                kb.act(TMP[0], pss, AF.Sqrt, bias=1e-6)
                kb.I("dve", "reciprocal", out=TMP[0], in_=TMP[0])
                kb.stt(KQ[:, :, 1, :], q.re("p (n t) -> p n t", n=4), 0.125, TMP[0].re("p (n t) -> p n t", n=4), ALU.mult, ALU.mult)
                kb.act(SQb, k_, AF.Square)
                pss = next_ps()
                kb.mm(pss, bd_b, SQb)
                kb.act(TMP[1], pss, AF.Sqrt, bias=1e-6)
                kb.I("dve", "reciprocal", out=TMP[1], in_=TMP[1])
                kb.tt("dve", k_, k_, TMP[1], ALU.mult)
                kb.copy("act", KNb, k_)
                pbb = next_ps()
                kb.mm(pbb, cst[0:6, CST["sel"] + hp * 128:CST["sel"] + (hp + 1) * 128], BTT[0:6, :])
                kb.tt("dve", KQ[:, :, 0, :], k_.re("p (n t) -> p n t", n=4), pbb.re("p (n t) -> p n t", n=4), ALU.mult)
                kb.copy("act", Vb, v)
                if os.environ.get("GDN_CUT") == "2":
                    return
                for n2 in range(2):
                    items = []
                    for n in (2 * n2, 2 * n2 + 1):
                        cs = slice(n * 128, (n + 1) * 128)
                        pk = next_ps_b()
                        kb.tr(pk[:, 0:128], KNb[:, cs], ident_b)
                        pv = next_ps_b()
                        kb.tr(pv[:, 0:128], Vb[:, cs], ident_b)
                        for h in range(2):
                            hh = 2 * hp + h
                            fs = slice(h * 64, (h + 1) * 64)
                            kb.ts("dve", KD[n][:, fs], pk[:, fs], nh(DK, n, hh), ALU.mult)
                            kb.ts("dve", VB[n][:, fs], pv[:, fs], nh(BT4, n, hh), ALU.mult)
                        for h in range(2):
                            hh = 2 * hp + h
                            hs = slice(h * 64, (h + 1) * 64)
                            gm, exd, em, el = GMh[h], EXD[h], EM[h], EL[h]
                            sl = SL[n % 2][h]
                            kb.ts("dve", gm, ml_s, nh(G4, n, hh), ALU.mult)
                            pD = next_ps()
                            kb.mm(pD[:, 0:128], gm, mu_i)
                            kb.act(exd, pD[:, 0:128], AF.Exp)
                            kb.tt("dve", em[:, 0:128], exd, NMU, ALU.mult)
                            kb.tt("dve", em[:, 128:256], exd, mu_i, ALU.mult)
                            pEL = next_ps()
                            kb.tr(pEL[:, 0:128], em[:, 0:128], ident)
                            kb.copy("act", el, pEL[:, 0:128])
                            pGR = next_ps()
                            kb.mm(pGR[:, 0:256], KNb[hs, cs], KQ[hs, n, :, :].re("p j t -> p (j t)"))
                            kb.tt("dve", sl["Y0"], pGR[:, 0:128], em[:, 0:128], ALU.mult)
                            kb.tt("dve", QKT[n][h], pGR[:, 128:256], em[:, 128:256], ALU.mult)
                            pGL = next_ps()
                            kb.mm(pGL[:, 0:128], KQ[hs, n, 0, :], KNb[hs, cs])
                            kb.tt("dve", sl["A1"], pGL[:, 0:128], el, ALU.mult)
                            items.append(sl)
                    drain(tchain_gen([it["Y0"] for it in items], [it["A1"] for it in items],
                                     dict(CBUF, Sf=TTp[n2]), f32=True))
                for n in range(int(os.environ.get("GDN_N", "4"))):
                    cs = slice(n * 128, (n + 1) * 128)
                    if os.environ.get("GDN_CUT") == "4b" and n == int(os.environ.get("GDN_CUTN", "0")):
                        return
                    pKMh = [next_ps(), next_ps()]
                    pO1h = [next_ps(), next_ps()]
                    for h in range(2):
                        hs = slice(h * 64, (h + 1) * 64)
                        kb.mm(pKMh[h][:, 0:64], KNb[hs, cs], Mb[hp][hs, :])
                        kb.mm(pO1h[h][:, 0:64], KQ[hs, n, 1, :], Mb[hp][hs, :])
                    if os.environ.get("GDN_CUT") == "4c" and n == int(os.environ.get("GDN_CUTN", "0")):
                        return
                    pVN = next_ps()
                    for h in range(2):
                        hh = 2 * hp + h
                        fs = slice(h * 64, (h + 1) * 64)
                        kb.stt(Rb[h], pKMh[h][:, 0:64], nh(NGB, n, hh), VB[n][:, fs], ALU.mult, ALU.add)
                        if os.environ.get("GDN_CUT") == "4d" and n == int(os.environ.get("GDN_CUTN", "0")):
                            return
                        kb.mm(pVN[:, fs], TTf[n][h], Rb[h])
                        if os.environ.get("GDN_CUT") == "4e" and n == int(os.environ.get("GDN_CUTN", "0")):
                            return
                        kb.act(O1[h], pO1h[h][:, 0:64], AF.Identity, scale=nh(GAM, n, hh))
                    if os.environ.get("GDN_CUT") == "5" and n == int(os.environ.get("GDN_CUTN", "0")):
                        return
                    pO2 = next_ps()
                    pM = next_ps()
                    for h in range(2):
                        fs = slice(h * 64, (h + 1) * 64)
                        kb.copy("act", VN[h], pVN[:, fs])
                        kb.mm(pO2[:, fs], QKT[n][h], VN[h])
                        kb.mm(pM[fs, 0:64], KD[n][:, fs], VN[h])
                    if os.environ.get("GDN_CUT") == "6" and n == int(os.environ.get("GDN_CUTN", "0")):
                        return
                    ot = Ot[n]
                    for h in range(2):
                        hh = 2 * hp + h
                        fs = slice(h * 64, (h + 1) * 64)
                        kb.stt(Mf[hp][fs, :], Mf[hp][fs, :], nh(GCE, n, hh)[fs, :], pM[fs, 0:64], ALU.mult, ALU.add)
                        kb.tt("dve", ot[:, fs], O1[h], pO2[:, fs], ALU.add)
                    if not os.environ.get("GDN_NOMB"):
                        kb.copy("act", Mb[hp], Mf[hp])
                    if os.environ.get("GDN_CUT") == "7" and n == int(os.environ.get("GDN_CUTN", "0")):
                        return
                    if os.environ.get("GDN_NONORM"):
                        continue
                for n in range(4):
                    cs = slice(n * 128, (n + 1) * 128)
                    ot = Ot[n]
                    for h in range(2):
                        fs = slice(h * 64, (h + 1) * 64)
                        kb.act(O1[h], ot[:, fs], AF.Square, accum_out=SS[:, h:h + 1])
                    kb.act(SS[:, 0:2], SS[:, 0:2], AF.Sqrt, bias=EPS, scale=1.0 / 64)
                    kb.I("dve", "reciprocal", out=SS[:, 0:2], in_=SS[:, 0:2])
                    for h in range(2):
                        fs = slice(h * 64, (h + 1) * 64)
                        kb.stt(ot[:, fs], ot[:, fs], SS[:, h:h + 1], GNR[:, fs], ALU.mult, ALU.mult)
                    pT = next_ps()
                    kb.tr(pT[:, 0:128], ot, ident)
                    kb.copy("act", OF[:, cs], pT[:, 0:128])
                pz = next_ps()
                proj(b, 1152 + hp * 128, 128, pz)
                kb.act(TMP[0], pz, AF.Silu)
                kb.tt("dve", YG[hp], OF, TMP[0], ALU.mult)
                if os.environ.get("GDN_CUT") == "10":
                    return
            wout_apply(b, 3, lambda c: YG[c])


    def mixer_rwkv(l):
        SW = 256
        NCH = SW // 128
        al = Bump(carve, MIX_BASE, ARENA)
        load_win(l, 512, 1408)
        for c in range(3):
            kb.dma("pool", WOUT[:, c, :], w_out_d[l, 256 + c * 128:256 + (c + 1) * 128, :])
        WA = al.b16(384)
        GUP = al.b16(384)
        kb.dma("pool", WA, rw_wa_d[l])
        kb.dma("pool", GUP, rw_gup_d[l])
        if l >= 1:
            VW1 = al.b16(96).re("p (c n) -> p c n", c=3)
            VW2 = al.b16(384)
            kb.dma("pool", VW1, vw1_d[l - 1].re("(c p) n -> p c n", p=128))
            kb.dma("pool", VW2[0:32, :], vw2_d[l - 1])
        mu_s = cst[:, CST["mu_s"]:CST["mu_s"] + 128]
        mu_i = cst[:, CST["mu_i"]:CST["mu_i"] + 128]
        ml_s = cst[:, CST["ml_s"]:CST["ml_s"] + 128]
        bd_f = cst[:, CST["bd"]:CST["bd"] + 128]
        rst = cst[:, CST["rst"]:CST["rst"] + SW]
        MSK1 = al.f32(256)
        MSK2 = al.f32(256)
        NML = al.f32(128)
        kb.ts("dve", MSK1[:, 0:128], mu_s, -1.0, ALU.mult)
        kb.copy("dve", MSK1[:, 128:256], mu_i)
        kb.copy("dve", MSK2[:, 0:128], mu_s)
        kb.copy("dve", MSK2[:, 128:256], mu_i)
        kb.ts("dve", NML, ml_s, -1.0, ALU.mult)
        OMM = al.f32(11)
        OMKA = al.f32(3)
        kb.ts("dve", OMM, pcol[l][:, PCOL["rw_mu"]:PCOL["rw_mu"] + 11], -1.0, ALU.mult, s2=1.0, op1=ALU.add)
        kb.ts("dve", OMKA, pcol[l][:, PCOL["rw_ka"]:PCOL["rw_ka"] + 3], -1.0, ALU.mult, s2=1.0, op1=ALU.add)
        TR = al.f32(11)
        kb.memset("dve", TR, 0.0)
        Mf = [al.f32(64) for _ in range(3)]
        Mb = [al.b16(64) for _ in range(3)]
        for hp in range(3):
            kb.memset("dve", Mf[hp], 0.0)
            kb.memset("dve", Mb[hp], 0.0)
        PR = [al.f32(SW + 4) for _ in range(2)]
        XWAb = al.b16(SW)
        SXGb = al.b16(SW)
        V3 = [al.f32(SW) for _ in range(3)]
        V3b = [al.b16(SW) for _ in range(3)]
        V1b = al.b16(SW)
        Rf = al.f32(SW)
        Kf = al.f32(SW)
        SGW, GS, E1, Aa, KKf, KPf, TMP = [al.f32(SW) for _ in range(7)]
        SQb = al.b16(SW)
        BIGb = al.b16(SW)
        KIGb = al.b16(SW)
        Vb = al.b16(SW)
        YNF = al.f32(SW)
        YR = [al.b16(SW) for _ in range(3)]
        A1 = [[al.b16(128) for _ in range(2)] for _ in range(NCH)]
        CBUF = dict(Yp=[al.b16(512), al.b16(512)], Ap=[al.b16(512), al.b16(512)], Sf=al.f32(512))
        HO = []
        for _ in range(2):
            HO.append(dict(
                KR=al.b16(2 * SW).re("p (n j t) -> p n j t", n=NCH, j=2),
                BT=[al.b16(128) for _ in range(NCH)], KT=[al.b16(128) for _ in range(NCH)],
                VT=[al.b16(128) for _ in range(NCH)],
                ARm=[[al.b16(256) for _ in range(2)] for _ in range(NCH)],
                BRm=[[al.b16(256) for _ in range(2)] for _ in range(NCH)],
                SbA=al.b16(512),
                BON=al.f32(SW), Gg=al.f32(SW), GCOL=al.f32(NCH)))
        Zn = [al.b16(64) for _ in range(2)]
        Ub = [al.b16(64) for _ in range(2)]
        YS = [al.f32(128) for _ in range(2)]
        SCR = al.f32(64)
        ST = al.f32(8)
        if dbg.get('mem'):
            print('RWKV arena used', al.off, 'of', ARENA)
        pri = [0]

        def lerp(b, t0, j, out):
            pr = PR[pri[0] % 2]
            pri[0] += 1
            pp = next_ps()
            for k in range(KC):
                kb.mm(pp[:, 0:SW], WIN[k][:, j * 128:(j + 1) * 128], HT[b][:, k, t0:t0 + SW], start=(k == 0), stop=(k == KC - 1))
            kb.copy("act", pr[:, 1:SW + 1], pp[:, 0:SW])
            kb.copy("dve", pr[:, 0:1], TR[:, j:j + 1])
            kb.ts("dve", out, pr[:, 1:SW + 1], OMM[:, j:j + 1], ALU.mult)
            kb.stt(out, pr[:, 0:SW], pc(l, "rw_mu", j), out, ALU.mult, ALU.add)
            kb.copy("dve", TR[:, j:j + 1], pr[:, SW:SW + 1])

        def P(sb, hp, ho):
            b, t0 = divmod(sb * SW, TB)
            ts_ = slice(t0, t0 + SW)
            KR, BON, Gg = ho["KR"], ho["BON"], ho["Gg"]
            if hp == 0:
                lerp(b, t0, 9, TMP)
                kb.act(XWAb[0:64, :], TMP[0:64, :], AF.Tanh)
                kb.copy("act", XWAb[64:128, :], TMP[64:128, :])
                yield
                lerp(b, t0, 10, TMP)
                kb.act(SXGb, TMP, AF.Sigmoid)
                yield
                for h3 in range(3):
                    lerp(b, t0, 6 + h3, V3[h3])
                    if l == 0:
                        kb.copy("act", VF[h3][b][:, ts_], V3[h3])
                    else:
                        kb.copy("act", V3b[h3], V3[h3])
                    yield
                if l >= 1:
                    pv1 = next_ps()
                    for h3 in range(3):
                        kb.mm(pv1[0:32, 0:SW], VW1[:, h3, :], V3b[h3], start=(h3 == 0), stop=(h3 == 2))
                    kb.copy("act", V1b[0:32, :], pv1[0:32, 0:SW])
                    yield
            v = V3[hp]
            hc = slice(hp * 128, (hp + 1) * 128)
            if l >= 1:
                pv2 = next_ps()
                kb.mm(pv2[:, 0:SW], VW2[0:32, hc], V1b[0:32, :])
                kb.act(TMP, pv2[:, 0:SW], AF.Sigmoid, bias=pc(l, "rw_vresb", hp))
                kb.tt("dve", KPf, VF[hp][b][:, ts_], v, ALU.subtract)
                kb.tt("dve", KPf, KPf, TMP, ALU.mult)
                kb.tt("dve", v, v, KPf, ALU.add)
                yield
            lerp(b, t0, hp, Rf)
            yield
            lerp(b, t0, 3 + hp, Kf)
            yield
            pw = next_ps()
            kb.mm(pw[:, 0:SW], WA[0:64, hc], XWAb[0:64, :])
            kb.act(SGW, pw[:, 0:SW], AF.Sigmoid, bias=pc(l, "rw_wbias", hp))
            pa = next_ps()
            kb.mm(pa[:, 0:SW], WA[64:128, hc], XWAb[64:128, :])
            kb.act(Aa, pa[:, 0:SW], AF.Sigmoid, bias=pc(l, "rw_abias", hp))
            pg = next_ps()
            kb.mm(pg[:, 0:SW], GUP[:, hc], SXGb)
            kb.copy("act", Gg, pg[:, 0:SW])
            yield
            kb.ts("dve", KKf, Kf, pc(l, "rw_kk", hp), ALU.mult)
            kb.act(SQb, KKf, AF.Square)
            pss = next_ps()
            kb.mm(pss[:, 0:SW], bd_b, SQb)
            kb.act(TMP, pss[:, 0:SW], AF.Sqrt, bias=1e-6)
            kb.I("dve", "reciprocal", out=TMP, in_=TMP)
            kb.tt("dve", KKf, KKf, TMP, ALU.mult)
            yield
            kb.ts("dve", TMP, Aa, pc(l, "rw_ka", hp), ALU.mult, s2=OMKA[:, hp:hp + 1], op1=ALU.add)
            kb.tt("dve", KPf, Kf, TMP, ALU.mult)
            kb.stt(BON, Rf, pc(l, "rw_rk", hp), KPf, ALU.mult, ALU.mult)
            pbs = next_ps()
            kb.mm(pbs[:, 0:SW], bd_f, BON)
            kb.tt("dve", BON, pbs[:, 0:SW], v, ALU.mult)
            yield
            kb.tt("dve", Aa, KKf, Aa, ALU.mult)
            kb.I("dve", "tensor_tensor_scan", out=GS, data0=rst, data1=SGW, initial=0.0, op0=ALU.mult, op1=ALU.add)
            kb.tt("dve", SGW, GS, SGW, ALU.subtract)
            kb.act(SGW, SGW, AF.Exp, scale=-C0)
            kb.act(E1, GS, AF.Exp, scale=-C0)
            kb.act(GS, GS, AF.Exp, scale=C0)
            yield
            kb.tt("dve", KR[:, :, 0, :], KKf.re("p (n t) -> p n t", n=NCH), SGW.re("p (n t) -> p n t", n=NCH), ALU.mult)
            kb.tt("dve", KR[:, :, 1, :], Rf.re("p (n t) -> p n t", n=NCH), E1.re("p (n t) -> p n t", n=NCH), ALU.mult)
            for n in range(NCH):
                kb.copy("act", ho["GCOL"][:, n:n + 1], E1[:, n * 128 + 127:n * 128 + 128])
            yield
            kb.tt("dve", BIGb, Aa, GS, ALU.mult)
            kb.tt("dve", KIGb, KPf, GS, ALU.mult)
            kb.copy("act", Vb, v)
            yield
            items = []
            for n in range(NCH):
                cs = slice(n * 128, (n + 1) * 128)
                for src, dst in ((BIGb, ho["BT"][n]), (KIGb, ho["KT"][n]), (Vb, ho["VT"][n])):
                    pt = next_ps_b()
                    kb.tr(pt[:, 0:128], src[:, cs], ident_b)
                    kb.copy("act", dst, pt[:, 0:128])
                yield
                for h in range(2):
                    hs = slice(h * 64, (h + 1) * 64)
                    krh = KR[hs, n, :, :].re("p j t -> p (j t)")
                    pAR = next_ps()
                    kb.mm(pAR[:, 0:256], BIGb[hs, cs], krh)
                    kb.tt("dve", ho["ARm"][n][h], pAR[:, 0:256], MSK1, ALU.mult)
                    pBR = next_ps()
                    kb.mm(pBR[:, 0:256], KIGb[hs, cs], krh)
                    kb.tt("dve", ho["BRm"][n][h], pBR[:, 0:256], MSK2, ALU.mult)
                    pA = next_ps()
                    kb.mm(pA[:, 0:128], KR[hs, n, 0, :], BIGb[hs, cs])
                    kb.tt("dve", A1[n][h], pA[:, 0:128], NML, ALU.mult)
                    items.append((ho["ARm"][n][h][:, 0:128], A1[n][h]))
                    yield
            yield from tchain_gen([it[0] for it in items], [it[1] for it in items], dict(CBUF, Sb=ho["SbA"]))

        def S(sb, hp, ho):
            b, t0 = divmod(sb * SW, TB)
            ts_ = slice(t0, t0 + SW)
            KR = ho["KR"]
            for n in range(NCH):
                cs = slice(n * 128, (n + 1) * 128)
                pZh = [next_ps(), next_ps()]
                for h in range(2):
                    hs = slice(h * 64, (h + 1) * 64)
                    kb.mm(pZh[h][:, 0:64], KR[hs, n, 0, :], Mb[hp][hs, :], start=True, stop=False)
                    kb.mm(pZh[h][:, 0:64], ho["BRm"][n][h][:, 0:128], ho["VT"][n][:, hs], start=False, stop=True)
                yield
                pU = next_ps()
                for h in range(2):
                    hs = slice(h * 64, (h + 1) * 64)
                    kb.act(Zn[h], pZh[h][:, 0:64], AF.Copy, scale=-1.0)
                    kb.mm(pU[:, hs], ho["SbA"][:, (n * 2 + h) * 128:(n * 2 + h + 1) * 128], Zn[h])
                yield
                pYh = [next_ps(), next_ps()]
                pM = next_ps()
                for h in range(2):
                    hs = slice(h * 64, (h + 1) * 64)
                    kb.copy("act" if h else "dve", Ub[h], pU[:, hs])
                yield
                for h in range(2):
                    hs = slice(h * 64, (h + 1) * 64)
                    kb.mm(pYh[h][:, 0:64], KR[hs, n, 1, :], Mb[hp][hs, :], start=True, stop=False)
                    kb.mm(pYh[h][:, 0:64], ho["ARm"][n][h][:, 128:256], Ub[h], start=False, stop=False)
                    kb.mm(pYh[h][:, 0:64], ho["BRm"][n][h][:, 128:256], ho["VT"][n][:, hs], start=False, stop=True)
                for h in range(2):
                    hs = slice(h * 64, (h + 1) * 64)
                    kb.mm(pM[hs, 0:64], ho["BT"][n][:, hs], Ub[h], start=True, stop=False)
                    kb.mm(pM[hs, 0:64], ho["KT"][n][:, hs], ho["VT"][n][:, hs], start=False, stop=True)
                yield
                kb.tt("dve", Mf[hp], Mf[hp], pM[:, 0:64], ALU.add)
                kb.ts("dve", Mf[hp], Mf[hp], ho["GCOL"][:, n:n + 1], ALU.mult)
                kb.copy("act", Mb[hp], Mf[hp])
                yield
                ys = YS[n % 2]
                for h in range(2):
                    hs = slice(h * 64, (h + 1) * 64)
                    kb.act(ys[:, hs], pYh[h][:, 0:64], AF.Copy, accum_out=ST[:, h:h + 1])
                kb.ts("dve", ST[:, 2:4], ST[:, 0:2], -1.0 / 64, ALU.mult)
                yield
                for h in range(2):
                    hs = slice(h * 64, (h + 1) * 64)
                    kb.ts("dve", ys[:, hs], ys[:, hs], ST[:, 2 + h:3 + h], ALU.add)
                    kb.act(SCR, ys[:, hs], AF.Square, accum_out=ST[:, 4 + h:5 + h])
                yield
                kb.act(ST[:, 4:6], ST[:, 4:6], AF.Sqrt, bias=64e-5, scale=1.0 / 64)
                kb.I("dve", "reciprocal", out=ST[:, 4:6], in_=ST[:, 4:6])
                for h in range(2):
                    hs = slice(h * 64, (h + 1) * 64)
                    kb.ts("dve", ys[:, hs], ys[:, hs], ST[:, 4 + h:5 + h], ALU.mult)
                yield
                pT = next_ps()
                kb.tr(pT[:, 0:128], ys, ident)
                kb.act(YNF[:, cs], pT[:, 0:128], AF.Identity, scale=pc(l, "rw_lng", hp), bias=pc(l, "rw_lnb", hp))
                yield
            kb.tt("dve", YNF, YNF, ho["BON"], ALU.add)
            kb.tt("dve", YR[hp], YNF, ho["Gg"], ALU.mult)
            yield
            if hp == 2:
                for d in range(KC):
                    po = next_ps()
                    for c in range(3):
                        kb.mm(po[:, 0:SW], WOUT[:, c, d * 128:(d + 1) * 128], YR[c], start=(c == 0), stop=(c == 2))
                    kb.tt("dve", XT[d][b][:, ts_], po[:, 0:SW], XT[d][b][:, ts_], ALU.add)
                    yield

        iters = [(sb, hp) for sb in range(Tn // SW) for hp in range(3)]
        drain(P(iters[0][0], iters[0][1], HO[0]))
        for i, (sb, hp) in enumerate(iters):
            gS = S(sb, hp, HO[i % 2])
            gP = P(iters[i + 1][0], iters[i + 1][1], HO[(i + 1) % 2]) if i + 1 < len(iters) else None
            interleave(gS, gP)

    def mixer(l):
        rmsnorm(l, "mix_norm", h_out, SQ=MSQ, RS=MRS)
        kb.barrier()
        sel = os.environ.get("MIXERS", "lru,rwkv,gdn").split(",")
        if stages in ("all", "mix", "lru") and (stages == "lru" or "lru" in sel):
            mixer_lru(l)
            kb.barrier()
        if stages in ("all", "mix", "rwkv") and (stages == "rwkv" or "rwkv" in sel):
            mixer_rwkv(l)
            kb.barrier()
        if stages in ("all", "mix", "gdn") and (stages == "gdn" or "gdn" in sel):
            mixer_gdn(l)
            kb.barrier()

    if need_mix:
        MSQ = carve(ARENA - 8192 - 4096, 8192, BF16).re("p (k n) -> p k n", k=KC)
        MRS = [carve(ARENA - 4096 + i * 2048, 2048, F32) for i in range(2)]

    for l in range(Ln):
        if stages in ("io", "io0"):
            break
        if stages == "norm":
            rmsnorm(l, "ffn1_norm", h_out)
            break
        if need_ffn:
            ffn(l, 0)
            if stages == "ffn1":
                break
            kb.barrier()
        if need_mix:
            mixer(l)
            kb.barrier()
        if need_ffn:
            ffn(l, 1)
            kb.barrier()

    kb.barrier()
    YT = [carve(16384 + c * 2048 * NBn, 2048 * NBn, F32).re("p (b n) -> p b n", b=NBn) for c in range(KC)]
    OS = [carve(i * 4096, 4096, F32) for i in range(4)]
    FSQ = carve(16384 + 8 * 2048 * NBn, 8192, BF16).re("p (k n) -> p k n", k=KC) if NBn < 4 else None
    if NBn == 4:
        FSQ = carve(16384 + 65536, 8192, BF16).re("p (k n) -> p k n", k=KC)
    FRS = [carve(3 * 4096 + i * 2048, 2048, F32) for i in range(2)]

    def y_out(b, c, rs, g):
        kb.stt(YT[c][:, b, :], XT[c][b], g, rs, ALU.mult, ALU.mult)

    if stages == "io0":
        for b in range(NBn):
            for c in range(KC):
                kb.copy("dve", YT[c][:, b, :], XT[c][b])
    else:
        rmsnorm(Ln - 1, "final_norm", y_out, SQ=FSQ, RS=FRS)
    kb.barrier()
    out_toks = []
    for tt_ in range(Tn // 128):
        b, o = divmod(tt_ * 128, TB)
        osb = OS[tt_ % 4]
        for half in range(2):
            p = next_ps()
            for j in range(4):
                c = half * 4 + j
                kb.tr(p[:, j * 128:(j + 1) * 128], YT[c][:, b, o:o + 128], ident)
            kb.copy("act" if half else "dve", osb[:, half * 512:(half + 1) * 512], p)
        out_toks.append(kb.dma("sp", y_d[tt_ * 128:(tt_ + 1) * 128, :], osb))
    for t in out_toks:
        kb._wait("sp", t)
    return nc


def _prep_shared(inp, need_ffn=True, need_mix=True):
    f = lambda a: np.asarray(a, dtype=np.float32)
    pc = np.zeros((L, 128, PCOL["_n"]), np.float32)

    def put(l, name, vec, n):
        pc[l, :, PCOL[name]:PCOL[name] + n] = f(vec).reshape(n, 128).T

    for l in range(L):
        put(l, "ffn1_norm", inp["ffn1_norm"][l], 8)
        put(l, "mix_norm", inp["mix_norm"][l], 8)
        put(l, "ffn2_norm", inp["ffn2_norm"][l], 8)
        put(l, "final_norm", inp["final_norm"], 8)
        cw = f(inp["lru_conv_w"][l])
        for k in range(4):
            pc[l, :, PCOL["lru_conv_w"] + 2 * k:PCOL["lru_conv_w"] + 2 * k + 2] = cw[k].reshape(2, 128).T
        put(l, "lru_conv_b", inp["lru_conv_b"][l], 2)
        put(l, "lru_ga_b", f(inp["lru_gate_a_b"][l]).reshape(-1), 2)
        put(l, "lru_gx_b", f(inp["lru_gate_x_b"][l]).reshape(-1), 2)
        put(l, "lru_lam", inp["lru_lambda"][l], 2)
        put(l, "lru_out_g", inp["lru_out_norm"][l], 2)
        put(l, "rw_mu", inp["rwkv_mu"][l], 11)
        put(l, "rw_wbias", inp["rwkv_w_bias"][l], 3)
        put(l, "rw_abias", inp["rwkv_a_bias"][l], 3)
        put(l, "rw_kk", inp["rwkv_k_k"][l], 3)
        put(l, "rw_ka", inp["rwkv_k_a"][l], 3)
        put(l, "rw_rk", f(inp["rwkv_r_k"][l]).reshape(-1), 3)
        put(l, "rw_lng", inp["rwkv_ln_g"][l], 3)
        put(l, "rw_lnb", inp["rwkv_ln_b"][l], 3)
        if l >= 1:
            put(l, "rw_vresb", inp["rwkv_vres_b"][l - 1], 3)
        gw = f(inp["gdn_conv_w"][l])
        for k in range(4):
            pc[l, :, PCOL["gd_conv_w"] + 9 * k:PCOL["gd_conv_w"] + 9 * k + 9] = gw[k].reshape(9, 128).T
    cst = np.zeros((128, CST_N), np.float32)
    i = np.arange(128)
    cst[:, 0:128] = np.eye(128)
    cst[:, 128:256] = 1.0
    cst[:, 256:384] = (i[:, None] < i[None, :])
    cst[:, 384:512] = (i[:, None] <= i[None, :])
    cst[:, 512:640] = (i[:, None] > i[None, :])
    cst[:, 640:768] = (i[:, None] // 64 == i[None, :] // 64)
    cst[:, 768:1280] = (np.arange(512) % 128 != 0)[None, :]
    for h in range(6):
        cst[h, 1280 + h * 64:1280 + (h + 1) * 64] = 1.0
    sh = {"pcol": pc, "cst": cst}
    if need_ffn:
        for k in ("ffn1_wi", "ffn2_wi", "ffn1_wo", "ffn2_wo"):
            sh[k] = np.ascontiguousarray(f(inp[k]))
    if need_mix:
        pr = np.zeros((L, 128, PROW_N), np.float32)
        lg = np.zeros((L, 128, 4, 128), np.float32)
        wa = np.zeros((L, 128, 384), np.float32)
        for l in range(L):
            pr[l, :, 0:6] = f(inp["gdn_dt_bias"][l])[None, :]
            pr[l, :, 6:12] = f(inp["gdn_a_log"][l])[None, :]
            pr[l, :, 12:76] = f(inp["gdn_norm"][l])[None, :]
            pr[l, :, 76:140] = f(inp["gdn_norm"][l])[None, :]
            for gi, nm in enumerate(("lru_gate_a_w", "lru_gate_x_w")):
                w = f(inp[nm][l])
                for c in range(2):
                    for j in range(2):
                        lg[l, j * 64:(j + 1) * 64, gi * 2 + c, j * 64:(j + 1) * 64] = w[2 * c + j]
            wa[l, 0:64] = f(inp["rwkv_w_up"][l])
            wa[l, 64:128] = f(inp["rwkv_a_up"][l])
        sh.update({"prow": pr, "lru_g": lg, "rw_wa": wa})
        for k in ("w_in", "w_out", "rwkv_g_up", "rwkv_vres_w1", "rwkv_vres_w2"):
            sh[k] = np.ascontiguousarray(f(inp[k]))
    return sh


def kernel(**inputs):
    x = np.ascontiguousarray(inputs["x"], dtype=np.float32)
    B = x.shape[0]
    sh = _prep_shared(inputs)
    nc = bass.Bass("TRN2", target_bir_lowering=False)
    build(nc)
    in_maps = []
    for b in range(B):
        m = dict(sh)
        m["x"] = x[b]
        in_maps.append(m)
    res = run_bass_kernel_spmd(nc, in_maps, core_ids=list(range(B)))
    return np.stack([np.asarray(r["y"]) for r in res.results], axis=0).astype(np.float32)
```

```python
import numpy as np
import concourse.bass as bass
import concourse.mybir as mybir
from concourse.bass_utils import run_bass_kernel_spmd

F32 = mybir.dt.float32
BF16 = mybir.dt.bfloat16
F32R = mybir.dt.float32r
USE_F32R = False
AF = mybir.ActivationFunctionType
ALU = mybir.AluOpType

D = 1024
T = 2048
L = 2
DFF = 2816
NFC = DFF // 128
KC = D // 128
TB = 512
NB = T // TB
EPS = 1e-6
DIN = 3468

WRITE_KEYS = ("out", "accum_out", "ap")
ATTACH_WAITS = True


class Dep:
    __slots__ = ("w", "r", "excl")

    def __init__(self, excl=False):
        self.w = None
        self.r = {}
        self.excl = excl


class V:
    __slots__ = ("ap", "dep")

    def __init__(self, ap, dep=None):
        self.ap = ap
        self.dep = dep if dep is not None else Dep()

    def __getitem__(self, idx):
        return V(self.ap[idx], self.dep)

    def re(self, s, **kw):
        return V(self.ap.rearrange(s, **kw), self.dep)

    def bc(self, dt):
        return V(self.ap.bitcast(dt), self.dep)


class KB:
    def __init__(self, nc):
        self.nc = nc
        self.E = {"pe": nc.tensor, "act": nc.scalar, "dve": nc.vector, "pool": nc.gpsimd, "sp": nc.sync}
        self.sems = {}
        self.ecnt = {}
        self.waited = {e: {} for e in self.E}
        for e in self.E:
            self.sems[("e", e)] = nc.alloc_semaphore("s_" + e)
            self.ecnt[e] = 0
        self.dma_n = {"sp": 24, "pool": 12, "act": 4}
        self.dma_rr = {q: 0 for q in self.dma_n}
        self.dma_val = {}
        for q, n in self.dma_n.items():
            for i in range(n):
                self.sems[("d", q, i)] = nc.alloc_semaphore("d_%s%d" % (q, i))
                self.dma_val[("d", q, i)] = 0
        self.all_dma_toks = []
        self.sb_off = 0

    def _wait(self, eng, tok):
        if tok is None:
            return
        key, val = tok
        if val <= 0 or self.waited[eng].get(key, 0) >= val:
            return
        self.E[eng].wait_ge(self.sems[key], val)
        self.waited[eng][key] = val

    def _need(self, eng, reads, writes):
        mykey = ("e", eng)
        skip_self = eng == "pe"
        need = {}

        def add(tok):
            key, val = tok
            if val <= 0 or self.waited[eng].get(key, 0) >= val:
                return
            if need.get(key, 0) < val:
                need[key] = val
        for v in reads:
            w = v.dep.w
            if w is not None and not (skip_self and w[0] == mykey):
                add(w)
            if v.dep.excl:
                for key, val in v.dep.r.items():
                    if key != mykey:
                        add((key, val))
        for v in writes:
            d = v.dep
            if d.w is not None and not (skip_self and d.w[0] == mykey):
                add(d.w)
            for key, val in d.r.items():
                if skip_self and key == mykey:
                    continue
                add((key, val))
        return need

    def _deps(self, eng, reads, writes):
        for key, val in self._need(eng, reads, writes).items():
            self._wait(eng, (key, val))

    def _mark(self, tok, reads, writes):
        key, val = tok
        for v in reads:
            if v.dep.r.get(key, 0) < val:
                v.dep.r[key] = val
        for v in writes:
            v.dep.w = tok
            v.dep.r = {}

    def I(self, eng, fname, **kw):
        reads, writes, real = [], [], {}
        for k, v in kw.items():
            if isinstance(v, V):
                (writes if k in WRITE_KEYS else reads).append(v)
                real[k] = v.ap
            else:
                real[k] = v
        need = list(self._need(eng, reads, writes).items())
        fuse = None
        if need and ATTACH_WAITS and "accum_out" not in kw:
            fuse = need.pop()
        for key, val in need:
            self._wait(eng, (key, val))
        ins = getattr(self.E[eng], fname)(**real)
        if fuse is not None:
            ins._wait_ge(self.sems[fuse[0]], fuse[1])
            self.waited[eng][fuse[0]] = fuse[1]
        self.ecnt[eng] += 1
        ins.then_inc(self.sems[("e", eng)], 1)
        self._mark((("e", eng), self.ecnt[eng]), reads, writes)
        return ins

    def dma(self, q, out, in_, **kw):
        self._deps(q, [in_], [out])
        idx = self.dma_rr[q] % self.dma_n[q]
        self.dma_rr[q] += 1
        key = ("d", q, idx)
        prev = self.dma_val[key]
        self._wait(q, (key, prev))
        ins = self.E[q].dma_start(out=out.ap, in_=in_.ap, **kw)
        ins.then_inc(self.sems[key], 16)
        self.dma_val[key] = prev + 16
        tok = (key, prev + 16)
        self._mark(tok, [in_], [out])
        return tok

    def barrier(self, engines=("pe", "act", "dve", "pool", "sp")):
        toks = [(("e", e), self.ecnt[e]) for e in self.E]
        toks += [(k, v) for k, v in self.dma_val.items()]
        for e in engines:
            for t in toks:
                if t[0] == ("e", e):
                    continue
                self._wait(e, t)

    def mm(self, out, lhsT, rhs, start=True, stop=True):
        return self.I("pe", "matmul", out=out, lhsT=lhsT, rhs=rhs, start=start, stop=stop)

    def tr(self, out, in_, ident):
        return self.I("pe", "transpose", out=out, in_=in_, identity=ident)

    def act(self, out, in_, func, bias=None, scale=None, accum_out=None):
        kw = {}
        if bias is not None:
            kw["bias"] = bias
        if scale is not None:
            kw["scale"] = scale
        if accum_out is not None:
            kw["accum_out"] = accum_out
        return self.I("act", "activation", out=out, in_=in_, func=func, **kw)

    def tt(self, eng, out, in0, in1, op):
        return self.I(eng, "tensor_tensor", out=out, in0=in0, in1=in1, op=op)

    def ts(self, eng, out, in0, s1, op0, s2=None, op1=None, accum_out=None):
        kw = {}
        if op1 is not None:
            kw["op1"] = op1
        if accum_out is not None:
            kw["accum_out"] = accum_out
        return self.I(eng, "tensor_scalar", out=out, in0=in0, scalar1=s1, scalar2=s2, op0=op0, **kw)

    def stt(self, out, in0, scalar, in1, op0, op1):
        return self.I("dve", "scalar_tensor_tensor", out=out, in0=in0, scalar=scalar, in1=in1, op0=op0, op1=op1)

    def copy(self, eng, out, in_):
        if eng == "act":
            return self.I("act", "activation", out=out, in_=in_, func=AF.Copy)
        return self.I(eng, "tensor_copy", out=out, in_=in_)

    def memset(self, eng, ap, val):
        return self.I(eng, "memset", ap=ap, constant=val)

    def sb(self, name, shape, dtype):
        return V(self.nc.alloc_sbuf_tensor(name, list(shape), dtype)[:])


def _cols_layout():
    m = {}
    n = 0

    def add(name, k):
        nonlocal n
        m[name] = n
        n += k
    add("ffn1_norm", 8)
    add("mix_norm", 8)
    add("ffn2_norm", 8)
    add("final_norm", 8)
    add("lru_conv_w", 8)
    add("lru_conv_b", 2)
    add("lru_ga_b", 2)
    add("lru_gx_b", 2)
    add("lru_lam", 2)
    add("lru_out_g", 2)
    add("rw_mu", 11)
    add("rw_wbias", 3)
    add("rw_abias", 3)
    add("rw_kk", 3)
    add("rw_ka", 3)
    add("rw_rk", 3)
    add("rw_lng", 3)
    add("rw_lnb", 3)
    add("rw_vresb", 3)
    add("gd_conv_w", 36)
    m["_n"] = n
    return m


PCOL = _cols_layout()


C0 = float(np.exp(-0.5))
GELU_C = 0.7978845608028654
CST = {"ident": 0, "ones": 128, "mu_s": 256, "mu_i": 384, "ml_s": 512, "bd": 640, "rst": 768, "sel": 1280}
CST_N = 1280 + 384
PROW = {"gd_dt": 0, "gd_alog": 6, "gd_norm": 12}
PROW_N = 12 + 128


class Bump:
    def __init__(self, carve, base, end):
        self.carve, self.off, self.end = carve, base, end

    def _a(self, n, esz, dtype):
        nbytes = (n * esz + 31) // 32 * 32
        v = self.carve(self.off, nbytes, dtype)
        self.off += nbytes
        assert self.off <= self.end, ("arena overflow", self.off, self.end)
        return v[:, 0:n]

    def f32(self, n):
        return self._a(n, 4, F32)

    def b16(self, n):
        return self._a(n, 2, BF16)


def build(nc, dbg=None):
    dbg = dbg or {}
    stages = dbg.get("stages", "all")
    Tn = dbg.get("T", T)
    NBn = Tn // TB
    Ln = dbg.get("L", L)
    kb = KB(nc)

    def din(name, shape):
        return V(nc.dram_tensor(name, list(shape), F32, kind="ExternalInput").ap())

    x_d = din("x", [Tn, D])
    need_ffn = stages in ("all", "ffn1")
    need_mix = stages in ("all", "lru", "rwkv", "gdn", "mix")
    if need_ffn:
        wi_d = [din("ffn1_wi", [L, D, 2 * DFF]), din("ffn2_wi", [L, D, 2 * DFF])]
        wo_d = [din("ffn1_wo", [L, DFF, D]), din("ffn2_wo", [L, DFF, D])]
    if need_mix:
        w_in_d = din("w_in", [L, D, DIN])
        w_out_d = din("w_out", [L, D, D])
        lru_g_d = din("lru_g", [L, 128, 4, 128])
        rw_wa_d = din("rw_wa", [L, 128, 384])
        rw_gup_d = din("rwkv_g_up", [L, 128, 384])
        vw1_d = din("rwkv_vres_w1", [L - 1, 384, 32])
        vw2_d = din("rwkv_vres_w2", [L - 1, 32, 384])
        prow_d = din("prow", [L, 128, PROW_N])
    pcol_d = din("pcol", [L, 128, PCOL["_n"]])
    cst_d = din("cst", [128, CST_N])
    y_d = V(nc.dram_tensor("y", [Tn, D], F32, kind="ExternalOutput").ap())

    XT = [[kb.sb("xt%d_%d" % (c, b), [128, TB], F32) for b in range(NBn)] for c in range(KC)]
    HT = [kb.sb("ht%d" % b, [128, KC, TB], BF16) for b in range(NBn)]
    VF = [[kb.sb("vf%d_%d" % (hp, b), [128, TB], BF16) for b in range(NBn)] for hp in range(3)]
    cst = kb.sb("cst_sb", [128, CST_N], F32)
    ident = cst[:, 0:128]
    ones_f = cst[:, 128:256]
    ones_b = kb.sb("ones_b", [128, 128], BF16)
    ident_b = kb.sb("ident_b", [128, 128], BF16)
    bd_b = kb.sb("bd_b", [128, 128], BF16)
    pcol = [kb.sb("pcol%d" % l, [128, PCOL["_n"]], F32) for l in range(L)]
    if need_mix:
        prow = [kb.sb("prow%d" % l, [128, PROW_N], F32) for l in range(L)]
    ps = [V(nc.alloc_psum_tensor("ps%d" % i, [128, 512], F32)[:], Dep(excl=True)) for i in range(8)]

    ARENA = 88 * 1024
    arena_h = nc.alloc_sbuf_tensor("arena", [128, ARENA // 4], F32)

    def carve(off, nbytes, dtype, dep=None):
        assert off % 4 == 0 and nbytes % 4 == 0 and off + nbytes <= ARENA, (off, nbytes)
        v = V(arena_h[:][:, off // 4:(off + nbytes) // 4], dep)
        return v.bc(dtype) if dtype != F32 else v

    kb.dma("sp", cst, cst_d)
    kb.copy("dve", ones_b, ones_f)
    kb.copy("dve", ident_b, ident)
    kb.copy("dve", bd_b, cst[:, CST["bd"]:CST["bd"] + 128])
    for l in range(L):
        kb.dma("sp", pcol[l], pcol_d[l])
        if need_mix:
            kb.dma("sp", prow[l], prow_d[l])

    def pc(l, name, j=0):
        c = PCOL[name] + j
        return pcol[l][:, c:c + 1]

    NFMAX = 5
    off = 0
    WI = [carve(off + i * 20480, 20480, BF16).re("p (k n) -> p k n", k=KC) for i in range(2)]
    off += 2 * 20480
    WO = [carve(off + i * 10240, 10240, BF16).re("p (f n) -> p f n", f=NFMAX) for i in range(2)]
    off += 2 * 10240
    ATb = [carve(off + i * 5120, 5120, BF16).re("p (f n) -> p f n", f=NFMAX) for i in range(2)]
    off += 2 * 5120
    SG = [carve(off + i * 2048, 2048, F32) for i in range(2)]
    off += 2 * 2048
    SQ = carve(off, 8192, BF16).re("p (k n) -> p k n", k=KC)
    off += 8192
    RS = [carve(off + i * 2048, 2048, F32) for i in range(2)]
    off += 2 * 2048
    assert off <= ARENA
    UPO = 640
    XS = [carve(i * 4096, 4096, F32) for i in range(4)]

    psi = [0]
    ps_b16 = [p.bc(BF16) for p in ps]

    pool_sel = ["all"]
    pool_ctr = {"all": 0, "S": 0, "P": 0}

    def _ps_index():
        p = pool_sel[0]
        if p == "all":
            i = psi[0] % 8
            psi[0] += 1
            return i
        i = pool_ctr[p] % 4 + (0 if p == "S" else 4)
        pool_ctr[p] += 1
        return i

    def next_ps():
        return ps[_ps_index()]

    def next_ps_b():
        return ps_b16[_ps_index()]

    def drain(gen, pool="all"):
        pool_sel[0] = pool
        for _ in gen:
            pass
        pool_sel[0] = "all"

    def interleave(gS, gP, ratio=2):
        doneS = gS is None
        doneP = gP is None
        while not (doneS and doneP):
            if not doneS:
                pool_sel[0] = "S"
                try:
                    next(gS)
                except StopIteration:
                    doneS = True
            for _ in range(ratio):
                if not doneP:
                    pool_sel[0] = "P"
                    try:
                        next(gP)
                    except StopIteration:
                        doneP = True
        pool_sel[0] = "all"

    for tt_ in range(Tn // 128):
        xs = XS[tt_ % 4]
        kb.dma("sp", xs, x_d[tt_ * 128:(tt_ + 1) * 128, :])
        b, o = divmod(tt_ * 128, TB)
        for half in range(2):
            p = next_ps()
            for j in range(4):
                c = half * 4 + j
                kb.tr(p[:, j * 128:(j + 1) * 128], xs[:, c * 128:(c + 1) * 128], ident)
            for j in range(4):
                c = half * 4 + j
                kb.copy("act" if half else "dve", XT[c][b][:, o:o + 128], p[:, j * 128:(j + 1) * 128])
    kb.barrier()

    def rmsnorm(l, gname, out_fn, SQ=SQ, RS=RS):
        for b in range(NBn):
            for c in range(KC):
                kb.act(SQ[:, c, :], XT[c][b], AF.Square)
            p = next_ps()
            for c in range(KC):
                kb.mm(p, ones_b, SQ[:, c, :], start=(c == 0), stop=(c == KC - 1))
            rs = RS[b % 2]
            kb.act(rs, p, AF.Sqrt, bias=EPS, scale=1.0 / D)
            kb.I("dve", "reciprocal", out=rs, in_=rs)
            for c in range(KC):
                out_fn(b, c, rs, pc(l, gname, c))

    def h_out(b, c, rs, g):
        kb.stt(HT[b][:, c, :], XT[c][b], g, rs, ALU.mult, ALU.mult)

    groups = [(0, 5), (5, 5), (10, 4), (14, 4), (18, 4)]

    ffn_calls = [0]

    def ffn_load(l, which, gi, par):
        f0, nf = groups[gi]
        buf = (gi + par) % 2
        wi = wi_d[which]
        wo = wo_d[which]
        for k in range(KC):
            kb.dma("pool", WI[buf][:, k, 0:nf * 128], wi[l, k * 128:(k + 1) * 128, f0 * 128:(f0 + nf) * 128])
            kb.dma("pool", WI[buf][:, k, UPO:UPO + nf * 128],
                   wi[l, k * 128:(k + 1) * 128, DFF + f0 * 128:DFF + (f0 + nf) * 128])
        for f in range(nf):
            kb.dma("pool", WO[buf][:, f, :], wo[l, (f0 + f) * 128:(f0 + f + 1) * 128, :])

    def ffn(l, which):
        par = ffn_calls[0] % 2
        ffn_calls[0] += 1
        ffn_load(l, which, 0, par)
        rmsnorm(l, "ffn1_norm" if which == 0 else "ffn2_norm", h_out)
        it = 0
        for gi, (f0, nf) in enumerate(groups):
            buf = (gi + par) % 2
            if gi + 1 < len(groups):
                ffn_load(l, which, gi + 1, par)
            for b in range(NBn):
                at = ATb[it % 2]
                for f in range(nf):
                    pg = next_ps()
                    pu = next_ps()
                    for k in range(KC):
                        kb.mm(pg, WI[buf][:, k, f * 128:(f + 1) * 128], HT[b][:, k, :], start=(k == 0), stop=(k == KC - 1))
                    for k in range(KC):
                        kb.mm(pu, WI[buf][:, k, UPO + f * 128:UPO + (f + 1) * 128], HT[b][:, k, :],
                              start=(k == 0), stop=(k == KC - 1))
                    sg = SG[f % 2]
                    kb.act(sg, pg, AF.Silu)
                    kb.tt("dve", at[:, f, :], sg, pu, ALU.mult)
                for c in range(KC):
                    po = next_ps()
                    for f in range(nf):
                        kb.mm(po, WO[buf][:, f, c * 128:(c + 1) * 128], at[:, f, :], start=(f == 0), stop=(f == nf - 1))
                    kb.stt(XT[c][b], po, 0.5, XT[c][b], ALU.mult, ALU.add)
                it += 1

    MIX_BASE = 0
    if need_mix:
        mo = 0
        WIN = [carve(mo + k * 3104, 3104, BF16) for k in range(KC)]
        mo += KC * 3104
        WOUT = carve(mo, 6144, BF16).re("p (c n) -> p c n", c=3)
        mo += 6144
        MIX_BASE = mo

    def proj(b, col0, ncols, out_ps):
        for k in range(KC):
            kb.mm(out_ps, WIN[k][:, col0:col0 + ncols], HT[b][:, k, :], start=(k == 0), stop=(k == KC - 1))

    def wout_apply(b, nchunks, ysrc):
        for d in range(KC):
            po = next_ps()
            for c in range(nchunks):
                kb.mm(po, WOUT[:, c, d * 128:(d + 1) * 128], ysrc(c), start=(c == 0), stop=(c == nchunks - 1))
            kb.tt("dve", XT[d][b], po, XT[d][b], ALU.add)

    def load_win(l, c0, n):
        for k in range(KC):
            kb.dma("pool", WIN[k][:, 0:n], w_in_d[l, k * 128:(k + 1) * 128, c0:c0 + n])

    def mixer_lru(l):
        al = Bump(carve, MIX_BASE, ARENA - 12288)
        LG = al.b16(512).re("p (g n) -> p g n", g=4)
        load_win(l, 0, 512)
        kb.dma("pool", LG, lru_g_d[l])
        for c in range(2):
            kb.dma("pool", WOUT[:, c, :], w_out_d[l, c * 128:(c + 1) * 128, :])
        cols = al.f32(8)
        PXB = [al.f32(516) for _ in range(2)]
        PY = [al.f32(512) for _ in range(2)]
        XC = [al.f32(512) for _ in range(2)]
        XCb = [al.b16(512) for _ in range(2)]
        Rg = [al.f32(512) for _ in range(2)]
        Ig = [al.f32(512) for _ in range(2)]
        Ag = [al.f32(512) for _ in range(2)]
        A2 = [al.f32(512) for _ in range(2)]
        Hh = [al.f32(512) for _ in range(2)]
        Ut = [al.f32(512) for _ in range(2)]
        SQl = [al.b16(512) for _ in range(2)]
        YL = [al.b16(512) for _ in range(2)]
        RSl = al.f32(512)
        if dbg.get('mem'):
            print('LRU arena used', al.off, 'of', ARENA)
        for c in range(2):
            t = cols[:, c:c + 1]
            kb.act(t, pc(l, "lru_lam", c), AF.Exp, scale=-1.0)
            kb.act(t, t, AF.Ln, bias=1.0)
            kb.ts("dve", cols[:, 2 + c:3 + c], t, -16.0, ALU.mult)
            kb.ts("dve", t, t, -8.0, ALU.mult)
            kb.memset("dve", cols[:, 4 + c:5 + c], 0.0)
            kb.memset("dve", PXB[c][:, 0:3], 0.0)
        for b in range(NBn):
            for c in range(2):
                px = next_ps()
                proj(b, c * 128, 128, px)
                kb.copy("act", PXB[c][:, 3:515], px)
                py = next_ps()
                proj(b, 256 + c * 128, 128, py)
                kb.copy("act", PY[c], py)
                xc = XC[c]
                kb.ts("dve", xc, PXB[c][:, 3:515], pc(l, "lru_conv_w", 3 * 2 + c), ALU.mult,
                      s2=pc(l, "lru_conv_b", c), op1=ALU.add)
                for k in range(3):
                    kb.stt(xc, PXB[c][:, k:k + 512], pc(l, "lru_conv_w", k * 2 + c), xc, ALU.mult, ALU.add)
                kb.copy("dve", PXB[c][:, 0:3], PXB[c][:, 512:515])
                kb.copy("act", XCb[c], xc)
                pr = next_ps()
                kb.mm(pr, LG[:, c, :], XCb[c])
                kb.act(Rg[c], pr, AF.Sigmoid, bias=pc(l, "lru_ga_b", c))
                pi_ = next_ps()
                kb.mm(pi_, LG[:, 2 + c, :], XCb[c])
                kb.act(Ig[c], pi_, AF.Sigmoid, bias=pc(l, "lru_gx_b", c))
                kb.act(Ag[c], Rg[c], AF.Exp, scale=cols[:, c:c + 1])
                kb.act(A2[c], Rg[c], AF.Exp, scale=cols[:, 2 + c:3 + c])
                kb.act(A2[c], A2[c], AF.Sqrt, scale=-1.0, bias=1.0)
                if b == 0:
                    kb.memset("dve", A2[c][:, 0:1], 1.0)
                kb.tt("dve", Ig[c], Ig[c], xc, ALU.mult)
                kb.tt("dve", Ig[c], Ig[c], A2[c], ALU.mult)
                kb.I("dve", "tensor_tensor_scan", out=Hh[c], data0=Ag[c], data1=Ig[c],
                     initial=cols[:, 4 + c:5 + c], op0=ALU.mult, op1=ALU.add)
                kb.copy("act", cols[:, 4 + c:5 + c], Hh[c][:, 511:512])
                u = Ut[c]
                kb.act(u, PY[c], AF.Square)
                kb.ts("dve", u, u, 0.044715, ALU.mult, s2=1.0, op1=ALU.add)
                kb.tt("dve", u, u, PY[c], ALU.mult)
                kb.act(u, u, AF.Sigmoid, scale=2.0 * GELU_C)
                kb.tt("dve", u, u, PY[c], ALU.mult)
                kb.tt("dve", Hh[c], Hh[c], u, ALU.mult)
                kb.act(SQl[c], Hh[c], AF.Square)
            pss = next_ps()
            for c in range(2):
                kb.mm(pss, ones_b, SQl[c], start=(c == 0), stop=(c == 1))
            kb.act(RSl, pss, AF.Sqrt, bias=EPS, scale=1.0 / 256)
            kb.I("dve", "reciprocal", out=RSl, in_=RSl)
            for c in range(2):
                kb.stt(YL[c], Hh[c], pc(l, "lru_out_g", c), RSl, ALU.mult, ALU.mult)
            wout_apply(b, 2, lambda c: YL[c])


    def tchain_gen(Y0s, A1s, bufs, f32=False):
        nch = len(Y0s)
        W = nch * 128
        Sf, Sb = bufs["Sf"], bufs.get("Sb")

        def sl(v, j):
            return v[:, j * 128:(j + 1) * 128]
        for j in range(nch):
            kb.tt("dve", sl(Sf, j), Y0s[j], ident, ALU.add)
        if not f32:
            kb.copy("act", Sb[:, 0:W], Sf[:, 0:W])
        Yk = list(Y0s)
        Ak = list(A1s)
        yield
        for lev in range(6):
            An, Yn = bufs["Ap"][lev % 2], bufs["Yp"][lev % 2]
            pa = next_ps()
            for j in range(nch):
                kb.mm(sl(pa, j), Yk[j], Ak[j])
            kb.copy("act", An[:, 0:W], pa[:, 0:W])
            yield
            if lev < 5:
                py = next_ps()
                for j in range(nch):
                    kb.mm(sl(py, j), Ak[j], Yk[j])
                kb.copy("dve", Yn[:, 0:W], py[:, 0:W])
                Yk = [sl(Yn, j) for j in range(nch)]
                yield
            Ak = [sl(An, j) for j in range(nch)]
            psn = next_ps()
            for j in range(nch):
                kb.mm(sl(psn, j), Ak[j], sl(Sf, j) if f32 else sl(Sb, j))
            kb.tt("dve", Sf[:, 0:W], psn[:, 0:W], Sf[:, 0:W], ALU.add)
            if not f32:
                kb.copy("act", Sb[:, 0:W], Sf[:, 0:W])
            yield

    def mixer_gdn(l):
        al = Bump(carve, MIX_BASE, ARENA)
        load_win(l, 1920, 1548)
        for c in range(3):
            kb.dma("pool", WOUT[:, c, :], w_out_d[l, 640 + c * 128:640 + (c + 1) * 128, :])
        NEA = al.f32(8)
        kb.act(NEA[:, 0:6], prow[l][:, 6:12], AF.Exp)
        kb.ts("dve", NEA[:, 0:6], NEA[:, 0:6], -1.0, ALU.mult)
        NMU = al.f32(128)
        kb.ts("dve", NMU, cst[:, CST["mu_s"]:CST["mu_s"] + 128], -1.0, ALU.mult)
        mu_i = cst[:, CST["mu_i"]:CST["mu_i"] + 128]
        ml_s = cst[:, CST["ml_s"]:CST["ml_s"] + 128]
        GNR = prow[l][:, 12:140]
        Mf = [al.f32(64) for _ in range(3)]
        Mb = [al.b16(64) for _ in range(3)]
        TQ = al.f32(27).re("p (j k) -> p j k", j=9)
        for hp in range(3):
            kb.memset("dve", Mf[hp], 0.0)
            kb.memset("dve", Mb[hp], 0.0)
        kb.memset("dve", TQ, 0.0)
        def t46():
            return al.f32(24)
        G4, BT4, GC4, GAM, NGB, DK, GCE, TA = [t46() for _ in range(8)]

        def nh(v, n, h0, h1=None):
            h1 = h0 + 1 if h1 is None else h1
            return v[:, n * 6 + h0:n * 6 + h1]
        BTT = al.f32(512)
        PQ = [al.f32(516) for _ in range(2)]
        QKV = [al.f32(512) for _ in range(3)]
        TMP = [al.f32(512)] * 2
        SQb = al.b16(512)
        KQ = al.b16(1024).re("p (n j t) -> p n j t", n=4, j=2)
        KNb = al.b16(512)
        Vb = al.b16(512)
        OF = al.f32(512)
        YG = [al.b16(512) for _ in range(3)]
        KD = [al.b16(128) for _ in range(4)]
        VB = [al.b16(128) for _ in range(4)]
        QKT = [[al.b16(128) for _ in range(2)] for _ in range(4)]
        TTp = [al.f32(512) for _ in range(2)]
        TTf = [[TTp[n // 2][:, ((n % 2) * 2 + h) * 128:((n % 2) * 2 + h + 1) * 128] for h in range(2)] for n in range(4)]
        SL = [[dict(Y0=al.f32(128), A1=al.f32(128)) for _ in range(2)] for _ in range(2)]
        CBUF = dict(Yp=[al.f32(512), al.f32(512)], Ap=[al.f32(512), al.f32(512)])
        GMh = [al.f32(128) for _ in range(2)]
        EXD = [al.f32(128) for _ in range(2)]
        EM = [al.f32(256) for _ in range(2)]
        EL = [al.f32(128) for _ in range(2)]
        Rb = [al.f32(64) for _ in range(2)]
        VN = [al.b16(64) for _ in range(2)]
        O1 = [al.f32(64) for _ in range(2)]
        Ot = [al.f32(128) for _ in range(4)]
        SS = al.f32(8)
        SQS = [al.f32(64) for _ in range(2)]
        qi = [0]
        if dbg.get('mem'):
            print('GDN arena used', al.off, 'of', ARENA)

        def block_scalars(b):
            pab = next_ps()
            for n in range(4):
                for k in range(KC):
                    kb.mm(pab[:, n * 12:(n + 1) * 12], HT[b][:, k, n * 128:(n + 1) * 128], WIN[k][:, 1536:1548],
                          start=(k == 0), stop=(k == KC - 1))
            for n in range(4):
                kb.tt("dve", nh(TA, n, 0, 6), pab[:, n * 12:n * 12 + 6], prow[l][:, 0:6], ALU.add)
                kb.act(nh(BT4, n, 0, 6), pab[:, n * 12 + 6:n * 12 + 12], AF.Sigmoid)
            kb.act(TA, TA, AF.Exp)
            kb.act(TA, TA, AF.Ln, bias=1.0)
            for n in range(4):
                kb.tt("dve", nh(G4, n, 0, 6), nh(TA, n, 0, 6), NEA[:, 0:6], ALU.mult)
            pgc = next_ps()
            for n in range(4):
                kb.mm(pgc[:, n * 6:(n + 1) * 6], mu_i, nh(G4, n, 0, 6))
            pgl = next_ps()
            for n in range(4):
                kb.mm(pgl[:, n * 6:(n + 1) * 6], ones_f, nh(G4, n, 0, 6))
            kb.copy("act", GC4, pgc[:, 0:24])
            kb.act(GAM, pgc[:, 0:24], AF.Exp)
            kb.stt(NGB, GAM, -1.0, BT4, ALU.mult, ALU.mult)
            kb.tt("dve", DK, pgl[:, 0:24], GC4, ALU.subtract)
            kb.act(DK, DK, AF.Exp)
            kb.act(GCE, pgl[:, 0:24], AF.Exp)
            pbt = next_ps()
            for k in range(KC):
                kb.mm(pbt[0:6, :], WIN[k][:, 1542:1548], HT[b][:, k, :], start=(k == 0), stop=(k == KC - 1))
            kb.act(BTT[0:6, :], pbt[0:6, :], AF.Sigmoid)

        def prep1(b, hp):
            for wi_, j in enumerate((hp, 3 + hp, 6 + hp)):
                pq = PQ[qi[0] % 2]
                qi[0] += 1
                pp = next_ps()
                proj(b, j * 128, 128, pp)
                kb.copy("act", pq[:, 3:515], pp)
                kb.copy("dve", pq[:, 0:3], TQ[:, j, :])
                yield
                t = QKV[wi_]
                kb.ts("dve", t, pq[:, 3:515], pc(l, "gd_conv_w", 3 * 9 + j), ALU.mult)
                for k in range(3):
                    kb.stt(t, pq[:, k:k + 512], pc(l, "gd_conv_w", k * 9 + j), t, ALU.mult, ALU.add)
                    yield
                kb.copy("dve", TQ[:, j, :], pq[:, 512:515])
                kb.act(t, t, AF.Silu)
                yield

        def prep2(b, hp):
            q, k_, v = QKV
            kb.act(SQb, q, AF.Square)
            pss = next_ps()
            kb.mm(pss, bd_b, SQb)
            kb.act(TMP[0], pss, AF.Sqrt, bias=1e-6)
            kb.I("dve", "reciprocal", out=TMP[0], in_=TMP[0])
            kb.stt(KQ[:, :, 1, :], q.re("p (n t) -> p n t", n=4), 0.125, TMP[0].re("p (n t) -> p n t", n=4), ALU.mult, ALU.mult)
            kb.act(SQb, k_, AF.Square)
            pss = next_ps()
            kb.mm(pss, bd_b, SQb)
            kb.act(TMP[1], pss, AF.Sqrt, bias=1e-6)
            kb.I("dve", "reciprocal", out=TMP[1], in_=TMP[1])
            kb.tt("dve", k_, k_, TMP[1], ALU.mult)
            kb.copy("act", KNb, k_)
            pbb = next_ps()
            kb.mm(pbb, cst[0:6, CST["sel"] + hp * 128:CST["sel"] + (hp + 1) * 128], BTT[0:6, :])
            kb.tt("dve", KQ[:, :, 0, :], k_.re("p (n t) -> p n t", n=4), pbb.re("p (n t) -> p n t", n=4), ALU.mult)
            kb.copy("act", Vb, v)
            yield

        def mats_chain(b, hp, n2):
            items = []
            for n in (2 * n2, 2 * n2 + 1):
                cs = slice(n * 128, (n + 1) * 128)
                pk = next_ps_b()
                kb.tr(pk[:, 0:128], KNb[:, cs], ident_b)
                for h in range(2):
                    hh = 2 * hp + h
                    fs = slice(h * 64, (h + 1) * 64)
                    kb.ts("dve", KD[n][:, fs], pk[:, fs], nh(DK, n, hh), ALU.mult)
                pv = next_ps_b()
                kb.tr(pv[:, 0:128], Vb[:, cs], ident_b)
                for h in range(2):
                    hh = 2 * hp + h
                    fs = slice(h * 64, (h + 1) * 64)
                    kb.ts("dve", VB[n][:, fs], pv[:, fs], nh(BT4, n, hh), ALU.mult)
                yield
                for h in range(2):
                    hh = 2 * hp + h
                    hs = slice(h * 64, (h + 1) * 64)
                    gm, exd, em, el = GMh[h], EXD[h], EM[h], EL[h]
                    sl = SL[n % 2][h]
                    kb.ts("dve", gm, ml_s, nh(G4, n, hh), ALU.mult)
                    pD = next_ps()
                    kb.mm(pD[:, 0:128], gm, mu_i)
                    kb.act(exd, pD[:, 0:128], AF.Exp)
                    kb.tt("dve", em[:, 0:128], exd, NMU, ALU.mult)
                    kb.tt("dve", em[:, 128:256], exd, mu_i, ALU.mult)
                    yield
                    pEL = next_ps()
                    kb.tr(pEL[:, 0:128], em[:, 0:128], ident)
                    kb.copy("act", el, pEL[:, 0:128])
                    pGR = next_ps()
                    kb.mm(pGR[:, 0:256], KNb[hs, cs], KQ[hs, n, :, :].re("p j t -> p (j t)"))
                    kb.tt("dve", sl["Y0"], pGR[:, 0:128], em[:, 0:128], ALU.mult)
                    kb.tt("dve", QKT[n][h], pGR[:, 128:256], em[:, 128:256], ALU.mult)
                    pGL = next_ps()
                    kb.mm(pGL[:, 0:128], KQ[hs, n, 0, :], KNb[hs, cs])
                    kb.tt("dve", sl["A1"], pGL[:, 0:128], el, ALU.mult)
                    items.append(sl)
                    yield
            yield from tchain_gen([it["Y0"] for it in items], [it["A1"] for it in items],
                                  dict(CBUF, Sf=TTp[n2]), f32=True)

        def state(b, hp, n):
            cs = slice(n * 128, (n + 1) * 128)
            pKMh = [next_ps(), next_ps()]
            pO1h = [next_ps(), next_ps()]
            for h in range(2):
                hs = slice(h * 64, (h + 1) * 64)
                kb.mm(pKMh[h][:, 0:64], KNb[hs, cs], Mb[hp][hs, :])
                kb.mm(pO1h[h][:, 0:64], KQ[hs, n, 1, :], Mb[hp][hs, :])
            yield
            pVN = next_ps()
            for h in range(2):
                hh = 2 * hp + h
                fs = slice(h * 64, (h + 1) * 64)
                kb.stt(Rb[h], pKMh[h][:, 0:64], nh(NGB, n, hh), VB[n][:, fs], ALU.mult, ALU.add)
                kb.mm(pVN[:, fs], TTf[n][h], Rb[h])
                kb.act(O1[h], pO1h[h][:, 0:64], AF.Identity, scale=nh(GAM, n, hh))
            yield
            pO2 = next_ps()
            pM = next_ps()
            for h in range(2):
                fs = slice(h * 64, (h + 1) * 64)
                kb.copy("act", VN[h], pVN[:, fs])
                kb.mm(pO2[:, fs], QKT[n][h], VN[h])
                kb.mm(pM[fs, 0:64], KD[n][:, fs], VN[h])
            yield
            ot = Ot[n]
            for h in range(2):
                hh = 2 * hp + h
                fs = slice(h * 64, (h + 1) * 64)
                kb.stt(Mf[hp][fs, :], Mf[hp][fs, :], nh(GCE, n, hh)[fs, :], pM[fs, 0:64], ALU.mult, ALU.add)
                kb.tt("dve", ot[:, fs], O1[h], pO2[:, fs], ALU.add)
            kb.copy("act", Mb[hp], Mf[hp])
            yield

        def norm(b, hp, n):
            cs = slice(n * 128, (n + 1) * 128)
            ot = Ot[n]
            for h in range(2):
                fs = slice(h * 64, (h + 1) * 64)
                kb.act(SQS[h], ot[:, fs], AF.Square, accum_out=SS[:, h:h + 1])
            kb.act(SS[:, 0:2], SS[:, 0:2], AF.Sqrt, bias=EPS, scale=1.0 / 64)
            kb.I("dve", "reciprocal", out=SS[:, 0:2], in_=SS[:, 0:2])
            yield
            for h in range(2):
                fs = slice(h * 64, (h + 1) * 64)
                kb.stt(ot[:, fs], ot[:, fs], SS[:, h:h + 1], GNR[:, fs], ALU.mult, ALU.mult)
            pT = next_ps()
            kb.tr(pT[:, 0:128], ot, ident)
            kb.copy("act", OF[:, cs], pT[:, 0:128])
            yield

        def gate(b, hp):
            pz = next_ps()
            proj(b, 1152 + hp * 128, 128, pz)
            kb.act(TMP[0], pz, AF.Silu)
            kb.tt("dve", YG[hp], OF, TMP[0], ALU.mult)
            yield

        def chain(*gens):
            for g_ in gens:
                yield from g_

        iters = [(b, hp) for b in range(NBn) for hp in range(3)]
        pre = prep1(*iters[0])
        for i, (b, hp) in enumerate(iters):
            if hp == 0:
                block_scalars(b)
            drain(pre)
            drain(prep2(b, hp))
            drain(mats_chain(b, hp, 0))
            interleave(chain(state(b, hp, 0), state(b, hp, 1)), mats_chain(b, hp, 1), ratio=3)
            nxt = prep1(*iters[i + 1]) if i + 1 < len(iters) else None
            interleave(chain(state(b, hp, 2), state(b, hp, 3)), chain(norm(b, hp, 0), norm(b, hp, 1), nxt) if nxt else chain(norm(b, hp, 0), norm(b, hp, 1)), ratio=3)
            pre = iter(())
            drain(chain(norm(b, hp, 2), norm(b, hp, 3), gate(b, hp)))
            if hp == 2:
                wout_apply(b, 3, lambda c: YG[c])


    def mixer_rwkv(l):
        SW = 256
        NCH = SW // 128
        al = Bump(carve, MIX_BASE, ARENA)
        load_win(l, 512, 1408)
        for c in range(3):
            kb.dma("pool", WOUT[:, c, :], w_out_d[l, 256 + c * 128:256 + (c + 1) * 128, :])
        WA = al.b16(384)
        GUP = al.b16(384)
        kb.dma("pool", WA, rw_wa_d[l])
        kb.dma("pool", GUP, rw_gup_d[l])
        if l >= 1:
            VW1 = al.b16(96).re("p (c n) -> p c n", c=3)
            VW2 = al.b16(384)
            kb.dma("pool", VW1, vw1_d[l - 1].re("(c p) n -> p c n", p=128))
            kb.dma("pool", VW2[0:32, :], vw2_d[l - 1])
        mu_s = cst[:, CST["mu_s"]:CST["mu_s"] + 128]
        mu_i = cst[:, CST["mu_i"]:CST["mu_i"] + 128]
        ml_s = cst[:, CST["ml_s"]:CST["ml_s"] + 128]
        bd_f = cst[:, CST["bd"]:CST["bd"] + 128]
        rst = cst[:, CST["rst"]:CST["rst"] + SW]
        MSK1 = al.f32(256)
        MSK2 = al.f32(256)
        NML = al.f32(128)
        kb.ts("dve", MSK1[:, 0:128], mu_s, -1.0, ALU.mult)
        kb.copy("dve", MSK1[:, 128:256], mu_i)
        kb.copy("dve", MSK2[:, 0:128], mu_s)
        kb.copy("dve", MSK2[:, 128:256], mu_i)
        kb.ts("dve", NML, ml_s, -1.0, ALU.mult)
        OMM = al.f32(11)
        OMKA = al.f32(3)
        kb.ts("dve", OMM, pcol[l][:, PCOL["rw_mu"]:PCOL["rw_mu"] + 11], -1.0, ALU.mult, s2=1.0, op1=ALU.add)
        kb.ts("dve", OMKA, pcol[l][:, PCOL["rw_ka"]:PCOL["rw_ka"] + 3], -1.0, ALU.mult, s2=1.0, op1=ALU.add)
        TR = al.f32(11)
        kb.memset("dve", TR, 0.0)
        Mf = [al.f32(64) for _ in range(3)]
        Mb = [al.b16(64) for _ in range(3)]
        for hp in range(3):
            kb.memset("dve", Mf[hp], 0.0)
            kb.memset("dve", Mb[hp], 0.0)
        PR = [al.f32(SW + 4) for _ in range(2)]
        XWAb = al.b16(SW)
        SXGb = al.b16(SW)
        V3 = [al.f32(SW) for _ in range(3)]
        V3b = [al.b16(SW) for _ in range(3)]
        V1b = al.b16(SW)
        Rf = al.f32(SW)
        Kf = al.f32(SW)
        SGW, GS, E1, Aa, KKf, KPf, TMP = [al.f32(SW) for _ in range(7)]
        SQb = al.b16(SW)
        BIGb = al.b16(SW)
        KIGb = al.b16(SW)
        Vb = al.b16(SW)
        YNF = al.f32(SW)
        YR = [al.b16(SW) for _ in range(3)]
        A1 = [[al.b16(128) for _ in range(2)] for _ in range(NCH)]
        CBUF = dict(Yp=[al.b16(512), al.b16(512)], Ap=[al.b16(512), al.b16(512)], Sf=al.f32(512))
        HO = []
        for _ in range(2):
            HO.append(dict(
                KR=al.b16(2 * SW).re("p (n j t) -> p n j t", n=NCH, j=2),
                BT=[al.b16(128) for _ in range(NCH)], KT=[al.b16(128) for _ in range(NCH)],
                VT=[al.b16(128) for _ in range(NCH)],
                ARm=[[al.b16(256) for _ in range(2)] for _ in range(NCH)],
                BRm=[[al.b16(256) for _ in range(2)] for _ in range(NCH)],
                SbA=al.b16(512),
                BON=al.f32(SW), Gg=al.f32(SW), GCOL=al.f32(NCH)))
        Zn = [al.b16(64) for _ in range(2)]
        Ub = [al.b16(64) for _ in range(2)]
        YS = [al.f32(128) for _ in range(2)]
        SCR = al.f32(64)
        ST = al.f32(8)
        if dbg.get('mem'):
            print('RWKV arena used', al.off, 'of', ARENA)
        pri = [0]

        def lerp(b, t0, j, out):
            pr = PR[pri[0] % 2]
            pri[0] += 1
            pp = next_ps()
            for k in range(KC):
                kb.mm(pp[:, 0:SW], WIN[k][:, j * 128:(j + 1) * 128], HT[b][:, k, t0:t0 + SW], start=(k == 0), stop=(k == KC - 1))
            kb.copy("act", pr[:, 1:SW + 1], pp[:, 0:SW])
            kb.copy("dve", pr[:, 0:1], TR[:, j:j + 1])
            kb.ts("dve", out, pr[:, 1:SW + 1], OMM[:, j:j + 1], ALU.mult)
            kb.stt(out, pr[:, 0:SW], pc(l, "rw_mu", j), out, ALU.mult, ALU.add)
            kb.copy("dve", TR[:, j:j + 1], pr[:, SW:SW + 1])

        def P(sb, hp, ho):
            b, t0 = divmod(sb * SW, TB)
            ts_ = slice(t0, t0 + SW)
            KR, BON, Gg = ho["KR"], ho["BON"], ho["Gg"]
            if hp == 0:
                lerp(b, t0, 9, TMP)
                kb.act(XWAb[0:64, :], TMP[0:64, :], AF.Tanh)
                kb.copy("act", XWAb[64:128, :], TMP[64:128, :])
                yield
                lerp(b, t0, 10, TMP)
                kb.act(SXGb, TMP, AF.Sigmoid)
                yield
                for h3 in range(3):
                    lerp(b, t0, 6 + h3, V3[h3])
                    if l == 0:
                        kb.copy("act", VF[h3][b][:, ts_], V3[h3])
                    else:
                        kb.copy("act", V3b[h3], V3[h3])
                    yield
                if l >= 1:
                    pv1 = next_ps()
                    for h3 in range(3):
                        kb.mm(pv1[0:32, 0:SW], VW1[:, h3, :], V3b[h3], start=(h3 == 0), stop=(h3 == 2))
                    kb.copy("act", V1b[0:32, :], pv1[0:32, 0:SW])
                    yield
            v = V3[hp]
            hc = slice(hp * 128, (hp + 1) * 128)
            if l >= 1:
                pv2 = next_ps()
                kb.mm(pv2[:, 0:SW], VW2[0:32, hc], V1b[0:32, :])
                kb.act(TMP, pv2[:, 0:SW], AF.Sigmoid, bias=pc(l, "rw_vresb", hp))
                kb.tt("dve", KPf, VF[hp][b][:, ts_], v, ALU.subtract)
                kb.tt("dve", KPf, KPf, TMP, ALU.mult)
                kb.tt("dve", v, v, KPf, ALU.add)
                yield
            lerp(b, t0, hp, Rf)
            yield
            lerp(b, t0, 3 + hp, Kf)
            yield
            pw = next_ps()
            kb.mm(pw[:, 0:SW], WA[0:64, hc], XWAb[0:64, :])
            kb.act(SGW, pw[:, 0:SW], AF.Sigmoid, bias=pc(l, "rw_wbias", hp))
            pa = next_ps()
            kb.mm(pa[:, 0:SW], WA[64:128, hc], XWAb[64:128, :])
            kb.act(Aa, pa[:, 0:SW], AF.Sigmoid, bias=pc(l, "rw_abias", hp))
            pg = next_ps()
            kb.mm(pg[:, 0:SW], GUP[:, hc], SXGb)
            kb.copy("act", Gg, pg[:, 0:SW])
            yield
            kb.ts("dve", KKf, Kf, pc(l, "rw_kk", hp), ALU.mult)
            kb.act(SQb, KKf, AF.Square)
            pss = next_ps()
            kb.mm(pss[:, 0:SW], bd_b, SQb)
            kb.act(TMP, pss[:, 0:SW], AF.Sqrt, bias=1e-6)
            kb.I("dve", "reciprocal", out=TMP, in_=TMP)
            kb.tt("dve", KKf, KKf, TMP, ALU.mult)
            yield
            kb.ts("dve", TMP, Aa, pc(l, "rw_ka", hp), ALU.mult, s2=OMKA[:, hp:hp + 1], op1=ALU.add)
            kb.tt("dve", KPf, Kf, TMP, ALU.mult)
            kb.stt(BON, Rf, pc(l, "rw_rk", hp), KPf, ALU.mult, ALU.mult)
            pbs = next_ps()
            kb.mm(pbs[:, 0:SW], bd_f, BON)
            kb.tt("dve", BON, pbs[:, 0:SW], v, ALU.mult)
            yield
            kb.tt("dve", Aa, KKf, Aa, ALU.mult)
            kb.I("dve", "tensor_tensor_scan", out=GS, data0=rst, data1=SGW, initial=0.0, op0=ALU.mult, op1=ALU.add)
            kb.tt("dve", SGW, GS, SGW, ALU.subtract)
            kb.act(SGW, SGW, AF.Exp, scale=-C0)
            kb.act(E1, GS, AF.Exp, scale=-C0)
            kb.act(GS, GS, AF.Exp, scale=C0)
            yield
            kb.tt("dve", KR[:, :, 0, :], KKf.re("p (n t) -> p n t", n=NCH), SGW.re("p (n t) -> p n t", n=NCH), ALU.mult)
            kb.tt("dve", KR[:, :, 1, :], Rf.re("p (n t) -> p n t", n=NCH), E1.re("p (n t) -> p n t", n=NCH), ALU.mult)
            for n in range(NCH):
                kb.copy("act", ho["GCOL"][:, n:n + 1], E1[:, n * 128 + 127:n * 128 + 128])
            yield
            kb.tt("dve", BIGb, Aa, GS, ALU.mult)
            kb.tt("dve", KIGb, KPf, GS, ALU.mult)
            kb.copy("act", Vb, v)
            yield
            items = []
            for n in range(NCH):
                cs = slice(n * 128, (n + 1) * 128)
                for src, dst in ((BIGb, ho["BT"][n]), (KIGb, ho["KT"][n]), (Vb, ho["VT"][n])):
                    pt = next_ps_b()
                    kb.tr(pt[:, 0:128], src[:, cs], ident_b)
                    kb.copy("act", dst, pt[:, 0:128])
                yield
                for h in range(2):
                    hs = slice(h * 64, (h + 1) * 64)
                    krh = KR[hs, n, :, :].re("p j t -> p (j t)")
                    pAR = next_ps()
                    kb.mm(pAR[:, 0:256], BIGb[hs, cs], krh)
                    kb.tt("dve", ho["ARm"][n][h], pAR[:, 0:256], MSK1, ALU.mult)
                    pBR = next_ps()
                    kb.mm(pBR[:, 0:256], KIGb[hs, cs], krh)
                    kb.tt("dve", ho["BRm"][n][h], pBR[:, 0:256], MSK2, ALU.mult)
                    pA = next_ps()
                    kb.mm(pA[:, 0:128], KR[hs, n, 0, :], BIGb[hs, cs])
                    kb.tt("dve", A1[n][h], pA[:, 0:128], NML, ALU.mult)
                    items.append((ho["ARm"][n][h][:, 0:128], A1[n][h]))
                    yield
            yield from tchain_gen([it[0] for it in items], [it[1] for it in items], dict(CBUF, Sb=ho["SbA"]))

        def S(sb, hp, ho):
            b, t0 = divmod(sb * SW, TB)
            ts_ = slice(t0, t0 + SW)
            KR = ho["KR"]
            for n in range(NCH):
                cs = slice(n * 128, (n + 1) * 128)
                pZh = [next_ps(), next_ps()]
                for h in range(2):
                    hs = slice(h * 64, (h + 1) * 64)
                    kb.mm(pZh[h][:, 0:64], KR[hs, n, 0, :], Mb[hp][hs, :], start=True, stop=False)
                    kb.mm(pZh[h][:, 0:64], ho["BRm"][n][h][:, 0:128], ho["VT"][n][:, hs], start=False, stop=True)
                yield
                pU = next_ps()
                for h in range(2):
                    hs = slice(h * 64, (h + 1) * 64)
                    kb.act(Zn[h], pZh[h][:, 0:64], AF.Copy, scale=-1.0)
                    kb.mm(pU[:, hs], ho["SbA"][:, (n * 2 + h) * 128:(n * 2 + h + 1) * 128], Zn[h])
                yield
                pYh = [next_ps(), next_ps()]
                pM = next_ps()
                for h in range(2):
                    hs = slice(h * 64, (h + 1) * 64)
                    kb.copy("act" if h else "dve", Ub[h], pU[:, hs])
                yield
                for h in range(2):
                    hs = slice(h * 64, (h + 1) * 64)
                    kb.mm(pYh[h][:, 0:64], KR[hs, n, 1, :], Mb[hp][hs, :], start=True, stop=False)
                    kb.mm(pYh[h][:, 0:64], ho["ARm"][n][h][:, 128:256], Ub[h], start=False, stop=False)
                    kb.mm(pYh[h][:, 0:64], ho["BRm"][n][h][:, 128:256], ho["VT"][n][:, hs], start=False, stop=True)
                for h in range(2):
                    hs = slice(h * 64, (h + 1) * 64)
                    kb.mm(pM[hs, 0:64], ho["BT"][n][:, hs], Ub[h], start=True, stop=False)
                    kb.mm(pM[hs, 0:64], ho["KT"][n][:, hs], ho["VT"][n][:, hs], start=False, stop=True)
                yield
                kb.tt("dve", Mf[hp], Mf[hp], pM[:, 0:64], ALU.add)
                kb.ts("dve", Mf[hp], Mf[hp], ho["GCOL"][:, n:n + 1], ALU.mult)
                kb.copy("act", Mb[hp], Mf[hp])
                yield
                ys = YS[n % 2]
                for h in range(2):
                    hs = slice(h * 64, (h + 1) * 64)
                    kb.act(ys[:, hs], pYh[h][:, 0:64], AF.Copy, accum_out=ST[:, h:h + 1])
                kb.ts("dve", ST[:, 2:4], ST[:, 0:2], -1.0 / 64, ALU.mult)
                yield
                for h in range(2):
                    hs = slice(h * 64, (h + 1) * 64)
                    kb.ts("dve", ys[:, hs], ys[:, hs], ST[:, 2 + h:3 + h], ALU.add)
                    kb.act(SCR, ys[:, hs], AF.Square, accum_out=ST[:, 4 + h:5 + h])
                yield
                kb.act(ST[:, 4:6], ST[:, 4:6], AF.Sqrt, bias=64e-5, scale=1.0 / 64)
                kb.I("dve", "reciprocal", out=ST[:, 4:6], in_=ST[:, 4:6])
                for h in range(2):
                    hs = slice(h * 64, (h + 1) * 64)
                    kb.ts("dve", ys[:, hs], ys[:, hs], ST[:, 4 + h:5 + h], ALU.mult)
                yield
                pT = next_ps()
                kb.tr(pT[:, 0:128], ys, ident)
                kb.act(YNF[:, cs], pT[:, 0:128], AF.Identity, scale=pc(l, "rw_lng", hp), bias=pc(l, "rw_lnb", hp))
                yield
            kb.tt("dve", YNF, YNF, ho["BON"], ALU.add)
            kb.tt("dve", YR[hp], YNF, ho["Gg"], ALU.mult)
            yield
            if hp == 2:
                for d in range(KC):
                    po = next_ps()
                    for c in range(3):
                        kb.mm(po[:, 0:SW], WOUT[:, c, d * 128:(d + 1) * 128], YR[c], start=(c == 0), stop=(c == 2))
                    kb.tt("dve", XT[d][b][:, ts_], po[:, 0:SW], XT[d][b][:, ts_], ALU.add)
                    yield

        iters = [(sb, hp) for sb in range(Tn // SW) for hp in range(3)]
        drain(P(iters[0][0], iters[0][1], HO[0]))
        for i, (sb, hp) in enumerate(iters):
            gS = S(sb, hp, HO[i % 2])
            gP = P(iters[i + 1][0], iters[i + 1][1], HO[(i + 1) % 2]) if i + 1 < len(iters) else None
            interleave(gS, gP)

    def mixer(l):
        rmsnorm(l, "mix_norm", h_out, SQ=MSQ, RS=MRS)
        sel = dbg.get("MIXERS", "lru,rwkv,gdn").split(",")
        if not (stages in ("all", "mix", "lru") and (stages == "lru" or "lru" in sel)):
            kb.barrier()
        if stages in ("all", "mix", "lru") and (stages == "lru" or "lru" in sel):
            mixer_lru(l)
            kb.barrier()
        if stages in ("all", "mix", "rwkv") and (stages == "rwkv" or "rwkv" in sel):
            mixer_rwkv(l)
            kb.barrier()
        if stages in ("all", "mix", "gdn") and (stages == "gdn" or "gdn" in sel):
            mixer_gdn(l)
            kb.barrier()

    if need_mix:
        MSQ = carve(ARENA - 8192 - 4096, 8192, BF16).re("p (k n) -> p k n", k=KC)
        MRS = [carve(ARENA - 4096 + i * 2048, 2048, F32) for i in range(2)]

    for l in range(Ln):
        if stages in ("io", "io0"):
            break
        if stages == "norm":
            rmsnorm(l, "ffn1_norm", h_out)
            break
        if need_ffn:
            ffn(l, 0)
            if stages == "ffn1":
                break
            kb.barrier()
        if need_mix:
            mixer(l)
            kb.barrier()
        if need_ffn:
            ffn(l, 1)
            if not need_mix or l == Ln - 1:
                kb.barrier()

    kb.barrier()
    YT = [carve(16384 + c * 2048 * NBn, 2048 * NBn, F32).re("p (b n) -> p b n", b=NBn) for c in range(KC)]
    OS = [carve(i * 4096, 4096, F32) for i in range(4)]
    FSQ = carve(16384 + 8 * 2048 * NBn, 8192, BF16).re("p (k n) -> p k n", k=KC) if NBn < 4 else None
    if NBn == 4:
        FSQ = carve(16384 + 65536, 8192, BF16).re("p (k n) -> p k n", k=KC)
    FRS = [carve(3 * 4096 + i * 2048, 2048, F32) for i in range(2)]

    def y_out(b, c, rs, g):
        kb.stt(YT[c][:, b, :], XT[c][b], g, rs, ALU.mult, ALU.mult)

    if stages == "io0":
        for b in range(NBn):
            for c in range(KC):
                kb.copy("dve", YT[c][:, b, :], XT[c][b])
    else:
        rmsnorm(Ln - 1, "final_norm", y_out, SQ=FSQ, RS=FRS)
    kb.barrier()
    out_toks = []
    for tt_ in range(Tn // 128):
        b, o = divmod(tt_ * 128, TB)
        osb = OS[tt_ % 4]
        for half in range(2):
            p = next_ps()
            for j in range(4):
                c = half * 4 + j
                kb.tr(p[:, j * 128:(j + 1) * 128], YT[c][:, b, o:o + 128], ident)
            kb.copy("act" if half else "dve", osb[:, half * 512:(half + 1) * 512], p)
        out_toks.append(kb.dma("sp", y_d[tt_ * 128:(tt_ + 1) * 128, :], osb))
    for t in out_toks:
        kb._wait("sp", t)
    return nc


def _prep_shared(inp, need_ffn=True, need_mix=True):
    f = lambda a: np.asarray(a, dtype=np.float32)
    pc = np.zeros((L, 128, PCOL["_n"]), np.float32)

    def put(l, name, vec, n):
        pc[l, :, PCOL[name]:PCOL[name] + n] = f(vec).reshape(n, 128).T

    for l in range(L):
        put(l, "ffn1_norm", inp["ffn1_norm"][l], 8)
        put(l, "mix_norm", inp["mix_norm"][l], 8)
        put(l, "ffn2_norm", inp["ffn2_norm"][l], 8)
        put(l, "final_norm", inp["final_norm"], 8)
        cw = f(inp["lru_conv_w"][l])
        for k in range(4):
            pc[l, :, PCOL["lru_conv_w"] + 2 * k:PCOL["lru_conv_w"] + 2 * k + 2] = cw[k].reshape(2, 128).T
        put(l, "lru_conv_b", inp["lru_conv_b"][l], 2)
        put(l, "lru_ga_b", f(inp["lru_gate_a_b"][l]).reshape(-1), 2)
        put(l, "lru_gx_b", f(inp["lru_gate_x_b"][l]).reshape(-1), 2)
        put(l, "lru_lam", inp["lru_lambda"][l], 2)
        put(l, "lru_out_g", inp["lru_out_norm"][l], 2)
        put(l, "rw_mu", inp["rwkv_mu"][l], 11)
        put(l, "rw_wbias", inp["rwkv_w_bias"][l], 3)
        put(l, "rw_abias", inp["rwkv_a_bias"][l], 3)
        put(l, "rw_kk", inp["rwkv_k_k"][l], 3)
        put(l, "rw_ka", inp["rwkv_k_a"][l], 3)
        put(l, "rw_rk", f(inp["rwkv_r_k"][l]).reshape(-1), 3)
        put(l, "rw_lng", inp["rwkv_ln_g"][l], 3)
        put(l, "rw_lnb", inp["rwkv_ln_b"][l], 3)
        if l >= 1:
            put(l, "rw_vresb", inp["rwkv_vres_b"][l - 1], 3)
        gw = f(inp["gdn_conv_w"][l])
        for k in range(4):
            pc[l, :, PCOL["gd_conv_w"] + 9 * k:PCOL["gd_conv_w"] + 9 * k + 9] = gw[k].reshape(9, 128).T
    cst = np.zeros((128, CST_N), np.float32)
    i = np.arange(128)
    cst[:, 0:128] = np.eye(128)
    cst[:, 128:256] = 1.0
    cst[:, 256:384] = (i[:, None] < i[None, :])
    cst[:, 384:512] = (i[:, None] <= i[None, :])
    cst[:, 512:640] = (i[:, None] > i[None, :])
    cst[:, 640:768] = (i[:, None] // 64 == i[None, :] // 64)
    cst[:, 768:1280] = (np.arange(512) % 128 != 0)[None, :]
    for h in range(6):
        cst[h, 1280 + h * 64:1280 + (h + 1) * 64] = 1.0
    sh = {"pcol": pc, "cst": cst}
    if need_ffn:
        for k in ("ffn1_wi", "ffn2_wi", "ffn1_wo", "ffn2_wo"):
            sh[k] = np.ascontiguousarray(f(inp[k]))
    if need_mix:
        pr = np.zeros((L, 128, PROW_N), np.float32)
        lg = np.zeros((L, 128, 4, 128), np.float32)
        wa = np.zeros((L, 128, 384), np.float32)
        for l in range(L):
            pr[l, :, 0:6] = f(inp["gdn_dt_bias"][l])[None, :]
            pr[l, :, 6:12] = f(inp["gdn_a_log"][l])[None, :]
            pr[l, :, 12:76] = f(inp["gdn_norm"][l])[None, :]
            pr[l, :, 76:140] = f(inp["gdn_norm"][l])[None, :]
            for gi, nm in enumerate(("lru_gate_a_w", "lru_gate_x_w")):
                w = f(inp[nm][l])
                for c in range(2):
                    for j in range(2):
                        lg[l, j * 64:(j + 1) * 64, gi * 2 + c, j * 64:(j + 1) * 64] = w[2 * c + j]
            wa[l, 0:64] = f(inp["rwkv_w_up"][l])
            wa[l, 64:128] = f(inp["rwkv_a_up"][l])
        sh.update({"prow": pr, "lru_g": lg, "rw_wa": wa})
        for k in ("w_in", "w_out", "rwkv_g_up", "rwkv_vres_w1", "rwkv_vres_w2"):
            sh[k] = np.ascontiguousarray(f(inp[k]))
    return sh


def kernel(**inputs):
    x = np.ascontiguousarray(inputs["x"], dtype=np.float32)
    B = x.shape[0]
    sh = _prep_shared(inputs)
    nc = bass.Bass("TRN2", target_bir_lowering=False)
    build(nc)
    in_maps = []
    for b in range(B):
        m = dict(sh)
        m["x"] = x[b]
        in_maps.append(m)
    res = run_bass_kernel_spmd(nc, in_maps, core_ids=list(range(B)))
    return np.stack([np.asarray(r["y"]) for r in res.results], axis=0).astype(np.float32)
```

```python
import numpy as np
import concourse.bass as bass
import concourse.mybir as mybir
from concourse.bass_utils import run_bass_kernel_spmd

F32 = mybir.dt.float32
BF16 = mybir.dt.bfloat16
F32R = mybir.dt.float32r
USE_F32R = False
AF = mybir.ActivationFunctionType
ALU = mybir.AluOpType

D = 1024
T = 2048
L = 2
DFF = 2816
NFC = DFF // 128
KC = D // 128
TB = 512
NB = T // TB
EPS = 1e-6
DIN = 3468

WRITE_KEYS = ("out", "accum_out", "ap")
ATTACH_WAITS = True


class Dep:
    __slots__ = ("w", "r", "excl")

    def __init__(self, excl=False):
        self.w = None
        self.r = {}
        self.excl = excl


class V:
    __slots__ = ("ap", "dep")

    def __init__(self, ap, dep=None):
        self.ap = ap
        self.dep = dep if dep is not None else Dep()

    def __getitem__(self, idx):
        return V(self.ap[idx], self.dep)

    def re(self, s, **kw):
        return V(self.ap.rearrange(s, **kw), self.dep)

    def bc(self, dt):
        return V(self.ap.bitcast(dt), self.dep)


class KB:
    def __init__(self, nc):
        self.nc = nc
        self.E = {"pe": nc.tensor, "act": nc.scalar, "dve": nc.vector, "pool": nc.gpsimd, "sp": nc.sync}
        self.sems = {}
        self.ecnt = {}
        self.waited = {e: {} for e in self.E}
        for e in self.E:
            self.sems[("e", e)] = nc.alloc_semaphore("s_" + e)
            self.ecnt[e] = 0
        self.dma_n = {"sp": 24, "pool": 12, "act": 4}
        self.dma_rr = {q: 0 for q in self.dma_n}
        self.dma_val = {}
        for q, n in self.dma_n.items():
            for i in range(n):
                self.sems[("d", q, i)] = nc.alloc_semaphore("d_%s%d" % (q, i))
                self.dma_val[("d", q, i)] = 0
        self.all_dma_toks = []
        self.sb_off = 0

    def _wait(self, eng, tok):
        if tok is None:
            return
        key, val = tok
        if val <= 0 or self.waited[eng].get(key, 0) >= val:
            return
        self.E[eng].wait_ge(self.sems[key], val)
        self.waited[eng][key] = val

    def _need(self, eng, reads, writes):
        mykey = ("e", eng)
        skip_self = eng == "pe"
        need = {}

        def add(tok):
            key, val = tok
            if val <= 0 or self.waited[eng].get(key, 0) >= val:
                return
            if need.get(key, 0) < val:
                need[key] = val
        for v in reads:
            w = v.dep.w
            if w is not None and not (skip_self and w[0] == mykey):
                add(w)
            if v.dep.excl:
                for key, val in v.dep.r.items():
                    if key != mykey:
                        add((key, val))
        for v in writes:
            d = v.dep
            if d.w is not None and not (skip_self and d.w[0] == mykey):
                add(d.w)
            for key, val in d.r.items():
                if skip_self and key == mykey:
                    continue
                add((key, val))
        return need

    def _deps(self, eng, reads, writes):
        for key, val in self._need(eng, reads, writes).items():
            self._wait(eng, (key, val))

    def _mark(self, tok, reads, writes):
        key, val = tok
        for v in reads:
            if v.dep.r.get(key, 0) < val:
                v.dep.r[key] = val
        for v in writes:
            v.dep.w = tok
            v.dep.r = {}

    def I(self, eng, fname, **kw):
        reads, writes, real = [], [], {}
        for k, v in kw.items():
            if isinstance(v, V):
                (writes if k in WRITE_KEYS else reads).append(v)
                real[k] = v.ap
            else:
                real[k] = v
        need = list(self._need(eng, reads, writes).items())
        fuse = None
        if need and ATTACH_WAITS and "accum_out" not in kw:
            fuse = need.pop()
        for key, val in need:
            self._wait(eng, (key, val))
        ins = getattr(self.E[eng], fname)(**real)
        if fuse is not None:
            ins._wait_ge(self.sems[fuse[0]], fuse[1])
            self.waited[eng][fuse[0]] = fuse[1]
        self.ecnt[eng] += 1
        ins.then_inc(self.sems[("e", eng)], 1)
        self._mark((("e", eng), self.ecnt[eng]), reads, writes)
        return ins

    def dma(self, q, out, in_, **kw):
        self._deps(q, [in_], [out])
        idx = self.dma_rr[q] % self.dma_n[q]
        self.dma_rr[q] += 1
        key = ("d", q, idx)
        prev = self.dma_val[key]
        self._wait(q, (key, prev))
        ins = self.E[q].dma_start(out=out.ap, in_=in_.ap, **kw)
        ins.then_inc(self.sems[key], 16)
        self.dma_val[key] = prev + 16
        tok = (key, prev + 16)
        self._mark(tok, [in_], [out])
        return tok

    def barrier(self, engines=("pe", "act", "dve", "pool", "sp")):
        toks = [(("e", e), self.ecnt[e]) for e in self.E]
        toks += [(k, v) for k, v in self.dma_val.items()]
        for e in engines:
            for t in toks:
                if t[0] == ("e", e):
                    continue
                self._wait(e, t)

    def mm(self, out, lhsT, rhs, start=True, stop=True):
        return self.I("pe", "matmul", out=out, lhsT=lhsT, rhs=rhs, start=start, stop=stop)

    def tr(self, out, in_, ident):
        return self.I("pe", "transpose", out=out, in_=in_, identity=ident)

    def act(self, out, in_, func, bias=None, scale=None, accum_out=None):
        kw = {}
        if bias is not None:
            kw["bias"] = bias
        if scale is not None:
            kw["scale"] = scale
        if accum_out is not None:
            kw["accum_out"] = accum_out
        return self.I("act", "activation", out=out, in_=in_, func=func, **kw)

    def tt(self, eng, out, in0, in1, op):
        return self.I(eng, "tensor_tensor", out=out, in0=in0, in1=in1, op=op)

    def ts(self, eng, out, in0, s1, op0, s2=None, op1=None, accum_out=None):
        kw = {}
        if op1 is not None:
            kw["op1"] = op1
        if accum_out is not None:
            kw["accum_out"] = accum_out
        return self.I(eng, "tensor_scalar", out=out, in0=in0, scalar1=s1, scalar2=s2, op0=op0, **kw)

    def stt(self, out, in0, scalar, in1, op0, op1):
        return self.I("dve", "scalar_tensor_tensor", out=out, in0=in0, scalar=scalar, in1=in1, op0=op0, op1=op1)

    def copy(self, eng, out, in_):
        if eng == "act":
            return self.I("act", "activation", out=out, in_=in_, func=AF.Copy)
        return self.I(eng, "tensor_copy", out=out, in_=in_)

    def memset(self, eng, ap, val):
        return self.I(eng, "memset", ap=ap, constant=val)

    def sb(self, name, shape, dtype):
        return V(self.nc.alloc_sbuf_tensor(name, list(shape), dtype)[:])


def _cols_layout():
    m = {}
    n = 0

    def add(name, k):
        nonlocal n
        m[name] = n
        n += k
    add("ffn1_norm", 8)
    add("mix_norm", 8)
    add("ffn2_norm", 8)
    add("final_norm", 8)
    add("lru_conv_w", 8)
    add("lru_conv_b", 2)
    add("lru_ga_b", 2)
    add("lru_gx_b", 2)
    add("lru_lam", 2)
    add("lru_out_g", 2)
    add("rw_mu", 11)
    add("rw_wbias", 3)
    add("rw_abias", 3)
    add("rw_kk", 3)
    add("rw_ka", 3)
    add("rw_rk", 3)
    add("rw_lng", 3)
    add("rw_lnb", 3)
    add("rw_vresb", 3)
    add("gd_conv_w", 36)
    m["_n"] = n
    return m


PCOL = _cols_layout()


C0 = float(np.exp(-0.5))
GELU_C = 0.7978845608028654
CST = {"ident": 0, "ones": 128, "mu_s": 256, "mu_i": 384, "ml_s": 512, "bd": 640, "rst": 768, "sel": 1280}
CST_N = 1280 + 384
PROW = {"gd_dt": 0, "gd_alog": 6, "gd_norm": 12}
PROW_N = 12 + 128


class Bump:
    def __init__(self, carve, base, end):
        self.carve, self.off, self.end = carve, base, end

    def _a(self, n, esz, dtype):
        nbytes = (n * esz + 31) // 32 * 32
        v = self.carve(self.off, nbytes, dtype)
        self.off += nbytes
        assert self.off <= self.end, ("arena overflow", self.off, self.end)
        return v[:, 0:n]

    def f32(self, n):
        return self._a(n, 4, F32)

    def b16(self, n):
        return self._a(n, 2, BF16)


def build(nc, dbg=None):
    dbg = dbg or {}
    stages = dbg.get("stages", "all")
    Tn = dbg.get("T", T)
    NBn = Tn // TB
    Ln = dbg.get("L", L)
    kb = KB(nc)

    def din(name, shape):
        return V(nc.dram_tensor(name, list(shape), F32, kind="ExternalInput").ap())

    x_d = din("x", [Tn, D])
    need_ffn = stages in ("all", "ffn1")
    need_mix = stages in ("all", "lru", "rwkv", "gdn", "mix")
    if need_ffn:
        wi_d = [din("ffn1_wi", [L, D, 2 * DFF]), din("ffn2_wi", [L, D, 2 * DFF])]
        wo_d = [din("ffn1_wo", [L, DFF, D]), din("ffn2_wo", [L, DFF, D])]
    if need_mix:
        w_in_d = din("w_in", [L, D, DIN])
        w_out_d = din("w_out", [L, D, D])
        lru_g_d = din("lru_g", [L, 128, 4, 128])
        rw_wa_d = din("rw_wa", [L, 128, 384])
        rw_gup_d = din("rwkv_g_up", [L, 128, 384])
        vw1_d = din("rwkv_vres_w1", [L - 1, 384, 32])
        vw2_d = din("rwkv_vres_w2", [L - 1, 32, 384])
        prow_d = din("prow", [L, 128, PROW_N])
    pcol_d = din("pcol", [L, 128, PCOL["_n"]])
    cst_d = din("cst", [128, CST_N])
    y_d = V(nc.dram_tensor("y", [Tn, D], F32, kind="ExternalOutput").ap())

    XT = [[kb.sb("xt%d_%d" % (c, b), [128, TB], F32) for b in range(NBn)] for c in range(KC)]
    HT = [kb.sb("ht%d" % b, [128, KC, TB], BF16) for b in range(NBn)]
    VF = [[kb.sb("vf%d_%d" % (hp, b), [128, TB], BF16) for b in range(NBn)] for hp in range(3)]
    cst = kb.sb("cst_sb", [128, CST_N], F32)
    ident = cst[:, 0:128]
    ones_f = cst[:, 128:256]
    ones_b = kb.sb("ones_b", [128, 128], BF16)
    ident_b = kb.sb("ident_b", [128, 128], BF16)
    bd_b = kb.sb("bd_b", [128, 128], BF16)
    pcol = [kb.sb("pcol%d" % l, [128, PCOL["_n"]], F32) for l in range(L)]
    if need_mix:
        prow = [kb.sb("prow%d" % l, [128, PROW_N], F32) for l in range(L)]
    ps = [V(nc.alloc_psum_tensor("ps%d" % i, [128, 512], F32)[:], Dep(excl=True)) for i in range(8)]

    ARENA = 88 * 1024
    arena_h = nc.alloc_sbuf_tensor("arena", [128, ARENA // 4], F32)

    def carve(off, nbytes, dtype, dep=None):
        assert off % 4 == 0 and nbytes % 4 == 0 and off + nbytes <= ARENA, (off, nbytes)
        v = V(arena_h[:][:, off // 4:(off + nbytes) // 4], dep)
        return v.bc(dtype) if dtype != F32 else v

    kb.dma("sp", cst, cst_d)
    kb.copy("dve", ones_b, ones_f)
    kb.copy("dve", ident_b, ident)
    kb.copy("dve", bd_b, cst[:, CST["bd"]:CST["bd"] + 128])
    for l in range(L):
        kb.dma("sp", pcol[l], pcol_d[l])
        if need_mix:
            kb.dma("sp", prow[l], prow_d[l])

    def pc(l, name, j=0):
        c = PCOL[name] + j
        return pcol[l][:, c:c + 1]

    NFMAX = 5
    off = 0
    WI = [carve(off + i * 20480, 20480, BF16).re("p (k n) -> p k n", k=KC) for i in range(2)]
    off += 2 * 20480
    WO = [carve(off + i * 10240, 10240, BF16).re("p (f n) -> p f n", f=NFMAX) for i in range(2)]
    off += 2 * 10240
    ATb = [carve(off + i * 5120, 5120, BF16).re("p (f n) -> p f n", f=NFMAX) for i in range(2)]
    off += 2 * 5120
    SG = [carve(off + i * 2048, 2048, F32) for i in range(2)]
    off += 2 * 2048
    SQ = carve(off, 8192, BF16).re("p (k n) -> p k n", k=KC)
    off += 8192
    RS = [carve(off + i * 2048, 2048, F32) for i in range(2)]
    off += 2 * 2048
    assert off <= ARENA
    UPO = 640
    WIk = [[V(WI[i].ap[:, k, :], Dep()) for k in range(KC)] for i in range(2)]
    WOf = [[V(WO[i].ap[:, f, :], Dep()) for f in range(NFMAX)] for i in range(2)]
    XS = [carve(i * 4096, 4096, F32) for i in range(4)]

    psi = [0]
    ps_b16 = [p.bc(BF16) for p in ps]

    pool_sel = ["all"]
    pool_ctr = {"all": 0, "S": 0, "P": 0}

    def _ps_index():
        p = pool_sel[0]
        if p == "all":
            i = psi[0] % 8
            psi[0] += 1
            return i
        i = pool_ctr[p] % 4 + (0 if p == "S" else 4)
        pool_ctr[p] += 1
        return i

    def next_ps():
        return ps[_ps_index()]

    def next_ps_b():
        return ps_b16[_ps_index()]

    def drain(gen, pool="all"):
        pool_sel[0] = pool
        for _ in gen:
            pass
        pool_sel[0] = "all"

    def interleave(gS, gP, ratio=2):
        doneS = gS is None
        doneP = gP is None
        while not (doneS and doneP):
            if not doneS:
                pool_sel[0] = "S"
                try:
                    next(gS)
                except StopIteration:
                    doneS = True
            for _ in range(ratio):
                if not doneP:
                    pool_sel[0] = "P"
                    try:
                        next(gP)
                    except StopIteration:
                        doneP = True
        pool_sel[0] = "all"

    for tt_ in range(Tn // 128):
        xs = XS[tt_ % 4]
        kb.dma("sp", xs, x_d[tt_ * 128:(tt_ + 1) * 128, :])
        b, o = divmod(tt_ * 128, TB)
        for half in range(2):
            p = next_ps()
            for j in range(4):
                c = half * 4 + j
                kb.tr(p[:, j * 128:(j + 1) * 128], xs[:, c * 128:(c + 1) * 128], ident)
            for j in range(4):
                c = half * 4 + j
                kb.copy("act" if half else "dve", XT[c][b][:, o:o + 128], p[:, j * 128:(j + 1) * 128])
    kb.barrier()

    def rmsnorm(l, gname, out_fn, SQ=SQ, RS=RS):
        for b in range(NBn):
            for c in range(KC):
                kb.act(SQ[:, c, :], XT[c][b], AF.Square)
            p = next_ps()
            for c in range(KC):
                kb.mm(p, ones_b, SQ[:, c, :], start=(c == 0), stop=(c == KC - 1))
            rs = RS[b % 2]
            kb.act(rs, p, AF.Sqrt, bias=EPS, scale=1.0 / D)
            kb.I("dve", "reciprocal", out=rs, in_=rs)
            for c in range(KC):
                out_fn(b, c, rs, pc(l, gname, c))

    def h_out(b, c, rs, g):
        kb.stt(HT[b][:, c, :], XT[c][b], g, rs, ALU.mult, ALU.mult)

    groups = [(0, 5), (5, 5), (10, 4), (14, 4), (18, 4)]

    ffn_calls = [0]

    def ffn_load(l, which, gi, par):
        f0, nf = groups[gi]
        buf = (gi + par) % 2
        wi = wi_d[which]
        wo = wo_d[which]
        for k in range(KC):
            kb.dma("pool", WIk[buf][k][:, 0:nf * 128], wi[l, k * 128:(k + 1) * 128, f0 * 128:(f0 + nf) * 128])
            kb.dma("pool", WIk[buf][k][:, UPO:UPO + nf * 128],
                   wi[l, k * 128:(k + 1) * 128, DFF + f0 * 128:DFF + (f0 + nf) * 128])
        for f in range(nf):
            kb.dma("pool", WOf[buf][f], wo[l, (f0 + f) * 128:(f0 + f + 1) * 128, :])

    def ffn(l, which):
        par = ffn_calls[0] % 2
        ffn_calls[0] += 1
        ffn_load(l, which, 0, par)
        rmsnorm(l, "ffn1_norm" if which == 0 else "ffn2_norm", h_out)
        it = 0
        for gi, (f0, nf) in enumerate(groups):
            buf = (gi + par) % 2
            if gi + 1 < len(groups):
                ffn_load(l, which, gi + 1, par)
            for b in range(NBn):
                at = ATb[it % 2]
                for f in range(nf):
                    pg = next_ps()
                    pu = next_ps()
                    for k in range(KC):
                        kb.mm(pg, WIk[buf][k][:, f * 128:(f + 1) * 128], HT[b][:, k, :], start=(k == 0), stop=(k == KC - 1))
                    for k in range(KC):
                        kb.mm(pu, WIk[buf][k][:, UPO + f * 128:UPO + (f + 1) * 128], HT[b][:, k, :],
                              start=(k == 0), stop=(k == KC - 1))
                    sg = SG[f % 2]
                    kb.act(sg, pg, AF.Silu)
                    kb.tt("dve", at[:, f, :], sg, pu, ALU.mult)
                for c in range(KC):
                    po = next_ps()
                    for f in range(nf):
                        kb.mm(po, WOf[buf][f][:, c * 128:(c + 1) * 128], at[:, f, :], start=(f == 0), stop=(f == nf - 1))
                    kb.stt(XT[c][b], po, 0.5, XT[c][b], ALU.mult, ALU.add)
                it += 1

    MIX_BASE = 0
    if need_mix:
        mo = 0
        WIN = [carve(mo + k * 3104, 3104, BF16) for k in range(KC)]
        mo += KC * 3104
        WOUT = carve(mo, 6144, BF16).re("p (c n) -> p c n", c=3)
        mo += 6144
        MIX_BASE = mo

    def proj(b, col0, ncols, out_ps):
        for k in range(KC):
            kb.mm(out_ps, WIN[k][:, col0:col0 + ncols], HT[b][:, k, :], start=(k == 0), stop=(k == KC - 1))

    def wout_apply(b, nchunks, ysrc):
        for d in range(KC):
            po = next_ps()
            for c in range(nchunks):
                kb.mm(po, WOUT[:, c, d * 128:(d + 1) * 128], ysrc(c), start=(c == 0), stop=(c == nchunks - 1))
            kb.tt("dve", XT[d][b], po, XT[d][b], ALU.add)

    def load_win(l, c0, n):
        for k in range(KC):
            kb.dma("pool", WIN[k][:, 0:n], w_in_d[l, k * 128:(k + 1) * 128, c0:c0 + n])

    def mixer_lru(l):
        al = Bump(carve, MIX_BASE, ARENA - 12288)
        LG = al.b16(512).re("p (g n) -> p g n", g=4)
        load_win(l, 0, 512)
        kb.dma("pool", LG, lru_g_d[l])
        for c in range(2):
            kb.dma("pool", WOUT[:, c, :], w_out_d[l, c * 128:(c + 1) * 128, :])
        cols = al.f32(8)
        PXB = [al.f32(516) for _ in range(2)]
        PY = [al.f32(512) for _ in range(2)]
        XC = [al.f32(512) for _ in range(2)]
        XCb = [al.b16(512) for _ in range(2)]
        Rg = [al.f32(512) for _ in range(2)]
        Ig = [al.f32(512) for _ in range(2)]
        Ag = [al.f32(512) for _ in range(2)]
        A2 = [al.f32(512) for _ in range(2)]
        Hh = [al.f32(512) for _ in range(2)]
        Ut = [al.f32(512) for _ in range(2)]
        SQl = [al.b16(512) for _ in range(2)]
        YL = [al.b16(512) for _ in range(2)]
        RSl = al.f32(512)
        if dbg.get('mem'):
            print('LRU arena used', al.off, 'of', ARENA)
        for c in range(2):
            t = cols[:, c:c + 1]
            kb.act(t, pc(l, "lru_lam", c), AF.Exp, scale=-1.0)
            kb.act(t, t, AF.Ln, bias=1.0)
            kb.ts("dve", cols[:, 2 + c:3 + c], t, -16.0, ALU.mult)
            kb.ts("dve", t, t, -8.0, ALU.mult)
            kb.memset("dve", cols[:, 4 + c:5 + c], 0.0)
            kb.memset("dve", PXB[c][:, 0:3], 0.0)
        for b in range(NBn):
            for c in range(2):
                px = next_ps()
                proj(b, c * 128, 128, px)
                kb.copy("act", PXB[c][:, 3:515], px)
                py = next_ps()
                proj(b, 256 + c * 128, 128, py)
                kb.copy("act", PY[c], py)
                xc = XC[c]
                kb.ts("dve", xc, PXB[c][:, 3:515], pc(l, "lru_conv_w", 3 * 2 + c), ALU.mult,
                      s2=pc(l, "lru_conv_b", c), op1=ALU.add)
                for k in range(3):
                    kb.stt(xc, PXB[c][:, k:k + 512], pc(l, "lru_conv_w", k * 2 + c), xc, ALU.mult, ALU.add)
                kb.copy("dve", PXB[c][:, 0:3], PXB[c][:, 512:515])
                kb.copy("act", XCb[c], xc)
                pr = next_ps()
                kb.mm(pr, LG[:, c, :], XCb[c])
                kb.act(Rg[c], pr, AF.Sigmoid, bias=pc(l, "lru_ga_b", c))
                pi_ = next_ps()
                kb.mm(pi_, LG[:, 2 + c, :], XCb[c])
                kb.act(Ig[c], pi_, AF.Sigmoid, bias=pc(l, "lru_gx_b", c))
                kb.act(Ag[c], Rg[c], AF.Exp, scale=cols[:, c:c + 1])
                kb.act(A2[c], Rg[c], AF.Exp, scale=cols[:, 2 + c:3 + c])
                kb.act(A2[c], A2[c], AF.Sqrt, scale=-1.0, bias=1.0)
                if b == 0:
                    kb.memset("dve", A2[c][:, 0:1], 1.0)
                kb.tt("dve", Ig[c], Ig[c], xc, ALU.mult)
                kb.tt("dve", Ig[c], Ig[c], A2[c], ALU.mult)
                kb.I("dve", "tensor_tensor_scan", out=Hh[c], data0=Ag[c], data1=Ig[c],
                     initial=cols[:, 4 + c:5 + c], op0=ALU.mult, op1=ALU.add)
                kb.copy("act", cols[:, 4 + c:5 + c], Hh[c][:, 511:512])
                u = Ut[c]
                kb.act(u, PY[c], AF.Square)
                kb.ts("dve", u, u, 0.044715, ALU.mult, s2=1.0, op1=ALU.add)
                kb.tt("dve", u, u, PY[c], ALU.mult)
                kb.act(u, u, AF.Sigmoid, scale=2.0 * GELU_C)
                kb.tt("dve", u, u, PY[c], ALU.mult)
                kb.tt("dve", Hh[c], Hh[c], u, ALU.mult)
                kb.act(SQl[c], Hh[c], AF.Square)
            pss = next_ps()
            for c in range(2):
                kb.mm(pss, ones_b, SQl[c], start=(c == 0), stop=(c == 1))
            kb.act(RSl, pss, AF.Sqrt, bias=EPS, scale=1.0 / 256)
            kb.I("dve", "reciprocal", out=RSl, in_=RSl)
            for c in range(2):
                kb.stt(YL[c], Hh[c], pc(l, "lru_out_g", c), RSl, ALU.mult, ALU.mult)
            wout_apply(b, 2, lambda c: YL[c])


    def tchain_gen(Y0s, A1s, bufs, f32=False):
        nch = len(Y0s)
        W = nch * 128
        Sf, Sb = bufs["Sf"], bufs.get("Sb")

        def sl(v, j):
            return v[:, j * 128:(j + 1) * 128]
        for j in range(nch):
            kb.tt("dve", sl(Sf, j), Y0s[j], ident, ALU.add)
        if not f32:
            kb.copy("act", Sb[:, 0:W], Sf[:, 0:W])
        Yk = list(Y0s)
        Ak = list(A1s)
        yield
        for lev in range(6):
            An, Yn = bufs["Ap"][lev % 2], bufs["Yp"][lev % 2]
            pa = next_ps()
            for j in range(nch):
                kb.mm(sl(pa, j), Yk[j], Ak[j])
            kb.copy("act", An[:, 0:W], pa[:, 0:W])
            yield
            if lev < 5:
                py = next_ps()
                for j in range(nch):
                    kb.mm(sl(py, j), Ak[j], Yk[j])
                kb.copy("dve", Yn[:, 0:W], py[:, 0:W])
                Yk = [sl(Yn, j) for j in range(nch)]
                yield
            Ak = [sl(An, j) for j in range(nch)]
            psn = next_ps()
            for j in range(nch):
                kb.mm(sl(psn, j), Ak[j], sl(Sf, j) if f32 else sl(Sb, j))
            kb.tt("dve", Sf[:, 0:W], psn[:, 0:W], Sf[:, 0:W], ALU.add)
            if not f32:
                kb.copy("act", Sb[:, 0:W], Sf[:, 0:W])
            yield

    def mixer_gdn(l):
        al = Bump(carve, MIX_BASE, ARENA)
        load_win(l, 1920, 1548)
        for c in range(3):
            kb.dma("pool", WOUT[:, c, :], w_out_d[l, 640 + c * 128:640 + (c + 1) * 128, :])
        NEA = al.f32(8)
        kb.act(NEA[:, 0:6], prow[l][:, 6:12], AF.Exp)
        kb.ts("dve", NEA[:, 0:6], NEA[:, 0:6], -1.0, ALU.mult)
        NMU = al.f32(128)
        kb.ts("dve", NMU, cst[:, CST["mu_s"]:CST["mu_s"] + 128], -1.0, ALU.mult)
        mu_i = cst[:, CST["mu_i"]:CST["mu_i"] + 128]
        ml_s = cst[:, CST["ml_s"]:CST["ml_s"] + 128]
        GNR = prow[l][:, 12:140]
        Mf = [al.f32(64) for _ in range(3)]
        Mb = [al.b16(64) for _ in range(3)]
        TQ = al.f32(27).re("p (j k) -> p j k", j=9)
        for hp in range(3):
            kb.memset("dve", Mf[hp], 0.0)
            kb.memset("dve", Mb[hp], 0.0)
        kb.memset("dve", TQ, 0.0)
        def t46():
            return al.f32(24)
        G4, BT4, GC4, GAM, NGB, DK, GCE, TA = [t46() for _ in range(8)]

        def nh(v, n, h0, h1=None):
            h1 = h0 + 1 if h1 is None else h1
            return v[:, n * 6 + h0:n * 6 + h1]
        BTT = al.f32(512)
        PQ = [al.f32(516) for _ in range(2)]
        QKV = [al.f32(512) for _ in range(3)]
        TMP = [al.f32(512)] * 2
        SQb = al.b16(512)
        KQ = al.b16(1024).re("p (n j t) -> p n j t", n=4, j=2)
        KNb = al.b16(512)
        Vb = al.b16(512)
        OF = al.f32(512)
        YG = [al.b16(512) for _ in range(3)]
        KD = [al.b16(128) for _ in range(4)]
        VB = [al.b16(128) for _ in range(4)]
        QKT = [[al.b16(128) for _ in range(2)] for _ in range(4)]
        TTp = [al.f32(512) for _ in range(2)]
        TTf = [[TTp[n // 2][:, ((n % 2) * 2 + h) * 128:((n % 2) * 2 + h + 1) * 128] for h in range(2)] for n in range(4)]
        SL = [[dict(Y0=al.f32(128), A1=al.f32(128)) for _ in range(2)] for _ in range(2)]
        CBUF = dict(Yp=[al.f32(512), al.f32(512)], Ap=[al.f32(512), al.f32(512)])
        GMh = [al.f32(128) for _ in range(2)]
        EXD = [al.f32(128) for _ in range(2)]
        EM = [al.f32(256) for _ in range(2)]
        EL = [al.f32(128) for _ in range(2)]
        Rb = [al.f32(64) for _ in range(2)]
        VN = [al.b16(64) for _ in range(2)]
        O1 = [al.f32(64) for _ in range(2)]
        Ot = [al.f32(128) for _ in range(4)]
        SS = al.f32(8)
        SQS = [al.f32(64) for _ in range(2)]
        qi = [0]
        if dbg.get('mem'):
            print('GDN arena used', al.off, 'of', ARENA)

        def block_scalars(b):
            pab = next_ps()
            for n in range(4):
                for k in range(KC):
                    kb.mm(pab[:, n * 12:(n + 1) * 12], HT[b][:, k, n * 128:(n + 1) * 128], WIN[k][:, 1536:1548],
                          start=(k == 0), stop=(k == KC - 1))
            for n in range(4):
                kb.tt("dve", nh(TA, n, 0, 6), pab[:, n * 12:n * 12 + 6], prow[l][:, 0:6], ALU.add)
                kb.act(nh(BT4, n, 0, 6), pab[:, n * 12 + 6:n * 12 + 12], AF.Sigmoid)
            kb.act(TA, TA, AF.Exp)
            kb.act(TA, TA, AF.Ln, bias=1.0)
            for n in range(4):
                kb.tt("dve", nh(G4, n, 0, 6), nh(TA, n, 0, 6), NEA[:, 0:6], ALU.mult)
            pgc = next_ps()
            for n in range(4):
                kb.mm(pgc[:, n * 6:(n + 1) * 6], mu_i, nh(G4, n, 0, 6))
            pgl = next_ps()
            for n in range(4):
                kb.mm(pgl[:, n * 6:(n + 1) * 6], ones_f, nh(G4, n, 0, 6))
            kb.copy("act", GC4, pgc[:, 0:24])
            kb.act(GAM, pgc[:, 0:24], AF.Exp)
            kb.stt(NGB, GAM, -1.0, BT4, ALU.mult, ALU.mult)
            kb.tt("dve", DK, pgl[:, 0:24], GC4, ALU.subtract)
            kb.act(DK, DK, AF.Exp)
            kb.act(GCE, pgl[:, 0:24], AF.Exp)
            pbt = next_ps()
            for k in range(KC):
                kb.mm(pbt[0:6, :], WIN[k][:, 1542:1548], HT[b][:, k, :], start=(k == 0), stop=(k == KC - 1))
            kb.act(BTT[0:6, :], pbt[0:6, :], AF.Sigmoid)

        def prep1(b, hp):
            for wi_, j in enumerate((hp, 3 + hp, 6 + hp)):
                pq = PQ[qi[0] % 2]
                qi[0] += 1
                pp = next_ps()
                proj(b, j * 128, 128, pp)
                kb.copy("act", pq[:, 3:515], pp)
                kb.copy("dve", pq[:, 0:3], TQ[:, j, :])
                yield
                t = QKV[wi_]
                kb.ts("dve", t, pq[:, 3:515], pc(l, "gd_conv_w", 3 * 9 + j), ALU.mult)
                for k in range(3):
                    kb.stt(t, pq[:, k:k + 512], pc(l, "gd_conv_w", k * 9 + j), t, ALU.mult, ALU.add)
                    yield
                kb.copy("dve", TQ[:, j, :], pq[:, 512:515])
                kb.act(t, t, AF.Silu)
                yield

        def prep2(b, hp):
            q, k_, v = QKV
            kb.act(SQb, q, AF.Square)
            pss = next_ps()
            kb.mm(pss, bd_b, SQb)
            kb.act(TMP[0], pss, AF.Sqrt, bias=1e-6)
            kb.I("dve", "reciprocal", out=TMP[0], in_=TMP[0])
            kb.stt(KQ[:, :, 1, :], q.re("p (n t) -> p n t", n=4), 0.125, TMP[0].re("p (n t) -> p n t", n=4), ALU.mult, ALU.mult)
            kb.act(SQb, k_, AF.Square)
            pss = next_ps()
            kb.mm(pss, bd_b, SQb)
            kb.act(TMP[1], pss, AF.Sqrt, bias=1e-6)
            kb.I("dve", "reciprocal", out=TMP[1], in_=TMP[1])
            kb.tt("dve", k_, k_, TMP[1], ALU.mult)
            kb.copy("act", KNb, k_)
            pbb = next_ps()
            kb.mm(pbb, cst[0:6, CST["sel"] + hp * 128:CST["sel"] + (hp + 1) * 128], BTT[0:6, :])
            kb.tt("dve", KQ[:, :, 0, :], k_.re("p (n t) -> p n t", n=4), pbb.re("p (n t) -> p n t", n=4), ALU.mult)
            kb.copy("act", Vb, v)
            yield

        def mats_chain(b, hp, n2):
            items = []
            for n in (2 * n2, 2 * n2 + 1):
                cs = slice(n * 128, (n + 1) * 128)
                pk = next_ps_b()
                kb.tr(pk[:, 0:128], KNb[:, cs], ident_b)
                for h in range(2):
                    hh = 2 * hp + h
                    fs = slice(h * 64, (h + 1) * 64)
                    kb.ts("dve", KD[n][:, fs], pk[:, fs], nh(DK, n, hh), ALU.mult)
                pv = next_ps_b()
                kb.tr(pv[:, 0:128], Vb[:, cs], ident_b)
                for h in range(2):
                    hh = 2 * hp + h
                    fs = slice(h * 64, (h + 1) * 64)
                    kb.ts("dve", VB[n][:, fs], pv[:, fs], nh(BT4, n, hh), ALU.mult)
                yield
                for h in range(2):
                    hh = 2 * hp + h
                    hs = slice(h * 64, (h + 1) * 64)
                    gm, exd, em, el = GMh[h], EXD[h], EM[h], EL[h]
                    sl = SL[n % 2][h]
                    kb.ts("dve", gm, ml_s, nh(G4, n, hh), ALU.mult)
                    pD = next_ps()
                    kb.mm(pD[:, 0:128], gm, mu_i)
                    kb.act(exd, pD[:, 0:128], AF.Exp)
                    kb.tt("dve", em[:, 0:128], exd, NMU, ALU.mult)
                    kb.tt("dve", em[:, 128:256], exd, mu_i, ALU.mult)
                    yield
                    pEL = next_ps()
                    kb.tr(pEL[:, 0:128], em[:, 0:128], ident)
                    kb.copy("act", el, pEL[:, 0:128])
                    pGR = next_ps()
                    kb.mm(pGR[:, 0:256], KNb[hs, cs], KQ[hs, n, :, :].re("p j t -> p (j t)"))
                    kb.tt("dve", sl["Y0"], pGR[:, 0:128], em[:, 0:128], ALU.mult)
                    kb.tt("dve", QKT[n][h], pGR[:, 128:256], em[:, 128:256], ALU.mult)
                    pGL = next_ps()
                    kb.mm(pGL[:, 0:128], KQ[hs, n, 0, :], KNb[hs, cs])
                    kb.tt("dve", sl["A1"], pGL[:, 0:128], el, ALU.mult)
                    items.append(sl)
                    yield
            yield from tchain_gen([it["Y0"] for it in items], [it["A1"] for it in items],
                                  dict(CBUF, Sf=TTp[n2]), f32=True)

        def state(b, hp, n):
            cs = slice(n * 128, (n + 1) * 128)
            pKMh = [next_ps(), next_ps()]
            pO1h = [next_ps(), next_ps()]
            for h in range(2):
                hs = slice(h * 64, (h + 1) * 64)
                kb.mm(pKMh[h][:, 0:64], KNb[hs, cs], Mb[hp][hs, :])
                kb.mm(pO1h[h][:, 0:64], KQ[hs, n, 1, :], Mb[hp][hs, :])
            yield
            pVN = next_ps()
            for h in range(2):
                hh = 2 * hp + h
                fs = slice(h * 64, (h + 1) * 64)
                kb.stt(Rb[h], pKMh[h][:, 0:64], nh(NGB, n, hh), VB[n][:, fs], ALU.mult, ALU.add)
                kb.mm(pVN[:, fs], TTf[n][h], Rb[h])
                kb.act(O1[h], pO1h[h][:, 0:64], AF.Identity, scale=nh(GAM, n, hh))
            yield
            pO2 = next_ps()
            pM = next_ps()
            for h in range(2):
                fs = slice(h * 64, (h + 1) * 64)
                kb.copy("act", VN[h], pVN[:, fs])
                kb.mm(pO2[:, fs], QKT[n][h], VN[h])
                kb.mm(pM[fs, 0:64], KD[n][:, fs], VN[h])
            yield
            ot = Ot[n]
            for h in range(2):
                hh = 2 * hp + h
                fs = slice(h * 64, (h + 1) * 64)
                kb.stt(Mf[hp][fs, :], Mf[hp][fs, :], nh(GCE, n, hh)[fs, :], pM[fs, 0:64], ALU.mult, ALU.add)
                kb.tt("dve", ot[:, fs], O1[h], pO2[:, fs], ALU.add)
            kb.copy("act", Mb[hp], Mf[hp])
            yield

        def norm(b, hp, n):
            cs = slice(n * 128, (n + 1) * 128)
            ot = Ot[n]
            for h in range(2):
                fs = slice(h * 64, (h + 1) * 64)
                kb.act(SQS[h], ot[:, fs], AF.Square, accum_out=SS[:, h:h + 1])
            kb.act(SS[:, 0:2], SS[:, 0:2], AF.Sqrt, bias=EPS, scale=1.0 / 64)
            kb.I("dve", "reciprocal", out=SS[:, 0:2], in_=SS[:, 0:2])
            yield
            for h in range(2):
                fs = slice(h * 64, (h + 1) * 64)
                kb.stt(ot[:, fs], ot[:, fs], SS[:, h:h + 1], GNR[:, fs], ALU.mult, ALU.mult)
            pT = next_ps()
            kb.tr(pT[:, 0:128], ot, ident)
            kb.copy("act", OF[:, cs], pT[:, 0:128])
            yield

        def gate(b, hp):
            pz = next_ps()
            proj(b, 1152 + hp * 128, 128, pz)
            kb.act(TMP[0], pz, AF.Silu)
            kb.tt("dve", YG[hp], OF, TMP[0], ALU.mult)
            yield

        def chain(*gens):
            for g_ in gens:
                yield from g_

        iters = [(b, hp) for b in range(NBn) for hp in range(3)]
        pre = prep1(*iters[0])
        for i, (b, hp) in enumerate(iters):
            if hp == 0:
                block_scalars(b)
            drain(pre)
            drain(prep2(b, hp))
            drain(mats_chain(b, hp, 0))
            interleave(chain(state(b, hp, 0), state(b, hp, 1)), mats_chain(b, hp, 1), ratio=3)
            nxt = prep1(*iters[i + 1]) if i + 1 < len(iters) else None
            interleave(chain(state(b, hp, 2), state(b, hp, 3)), chain(norm(b, hp, 0), norm(b, hp, 1), nxt) if nxt else chain(norm(b, hp, 0), norm(b, hp, 1)), ratio=3)
            pre = iter(())
            drain(chain(norm(b, hp, 2), norm(b, hp, 3), gate(b, hp)))
            if hp == 2:
                wout_apply(b, 3, lambda c: YG[c])


    def mixer_rwkv(l):
        SW = 256
        NCH = SW // 128
        al = Bump(carve, MIX_BASE, ARENA)
        load_win(l, 512, 1408)
        for c in range(3):
            kb.dma("pool", WOUT[:, c, :], w_out_d[l, 256 + c * 128:256 + (c + 1) * 128, :])
        WA = al.b16(384)
        GUP = al.b16(384)
        kb.dma("pool", WA, rw_wa_d[l])
        kb.dma("pool", GUP, rw_gup_d[l])
        if l >= 1:
            VW1 = al.b16(96).re("p (c n) -> p c n", c=3)
            VW2 = al.b16(384)
            kb.dma("pool", VW1, vw1_d[l - 1].re("(c p) n -> p c n", p=128))
            kb.dma("pool", VW2[0:32, :], vw2_d[l - 1])
        mu_s = cst[:, CST["mu_s"]:CST["mu_s"] + 128]
        mu_i = cst[:, CST["mu_i"]:CST["mu_i"] + 128]
        ml_s = cst[:, CST["ml_s"]:CST["ml_s"] + 128]
        bd_f = cst[:, CST["bd"]:CST["bd"] + 128]
        rst = cst[:, CST["rst"]:CST["rst"] + SW]
        MSK1 = al.f32(256)
        MSK2 = al.f32(256)
        NML = al.f32(128)
        kb.ts("dve", MSK1[:, 0:128], mu_s, -1.0, ALU.mult)
        kb.copy("dve", MSK1[:, 128:256], mu_i)
        kb.copy("dve", MSK2[:, 0:128], mu_s)
        kb.copy("dve", MSK2[:, 128:256], mu_i)
        kb.ts("dve", NML, ml_s, -1.0, ALU.mult)
        OMM = al.f32(11)
        OMKA = al.f32(3)
        kb.ts("dve", OMM, pcol[l][:, PCOL["rw_mu"]:PCOL["rw_mu"] + 11], -1.0, ALU.mult, s2=1.0, op1=ALU.add)
        kb.ts("dve", OMKA, pcol[l][:, PCOL["rw_ka"]:PCOL["rw_ka"] + 3], -1.0, ALU.mult, s2=1.0, op1=ALU.add)
        TR = al.f32(11)
        kb.memset("dve", TR, 0.0)
        Mf = [al.f32(64) for _ in range(3)]
        Mb = [al.b16(64) for _ in range(3)]
        for hp in range(3):
            kb.memset("dve", Mf[hp], 0.0)
            kb.memset("dve", Mb[hp], 0.0)
        PR = [al.f32(SW + 4) for _ in range(2)]
        XWAb = al.b16(SW)
        SXGb = al.b16(SW)
        V3 = [al.f32(SW) for _ in range(3)]
        V3b = [al.b16(SW) for _ in range(3)]
        V1b = al.b16(SW)
        Rf = al.f32(SW)
        Kf = al.f32(SW)
        SGW, GS, E1, Aa, KKf, KPf, TMP = [al.f32(SW) for _ in range(7)]
        SQb = al.b16(SW)
        BIGb = al.b16(SW)
        KIGb = al.b16(SW)
        Vb = al.b16(SW)
        YNF = al.f32(SW)
        YR = [al.b16(SW) for _ in range(3)]
        A1 = [[al.b16(128) for _ in range(2)] for _ in range(NCH)]
        CBUF = dict(Yp=[al.b16(512), al.b16(512)], Ap=[al.b16(512), al.b16(512)], Sf=al.f32(512))
        HO = []
        for _ in range(2):
            HO.append(dict(
                KR=al.b16(2 * SW).re("p (n j t) -> p n j t", n=NCH, j=2),
                BT=[al.b16(128) for _ in range(NCH)], KT=[al.b16(128) for _ in range(NCH)],
                VT=[al.b16(128) for _ in range(NCH)],
                ARm=[[al.b16(256) for _ in range(2)] for _ in range(NCH)],
                BRm=[[al.b16(256) for _ in range(2)] for _ in range(NCH)],
                SbA=al.b16(512),
                BON=al.f32(SW), Gg=al.f32(SW), GCOL=al.f32(NCH)))
        Zn = [al.b16(64) for _ in range(2)]
        Ub = [al.b16(64) for _ in range(2)]
        YS = [al.f32(128) for _ in range(2)]
        SCR = al.f32(64)
        ST = al.f32(8)
        if dbg.get('mem'):
            print('RWKV arena used', al.off, 'of', ARENA)
        pri = [0]

        def lerp(b, t0, j, out):
            pr = PR[pri[0] % 2]
            pri[0] += 1
            pp = next_ps()
            for k in range(KC):
                kb.mm(pp[:, 0:SW], WIN[k][:, j * 128:(j + 1) * 128], HT[b][:, k, t0:t0 + SW], start=(k == 0), stop=(k == KC - 1))
            kb.copy("act", pr[:, 1:SW + 1], pp[:, 0:SW])
            kb.copy("dve", pr[:, 0:1], TR[:, j:j + 1])
            kb.ts("dve", out, pr[:, 1:SW + 1], OMM[:, j:j + 1], ALU.mult)
            kb.stt(out, pr[:, 0:SW], pc(l, "rw_mu", j), out, ALU.mult, ALU.add)
            kb.copy("dve", TR[:, j:j + 1], pr[:, SW:SW + 1])

        def P(sb, hp, ho):
            b, t0 = divmod(sb * SW, TB)
            ts_ = slice(t0, t0 + SW)
            KR, BON, Gg = ho["KR"], ho["BON"], ho["Gg"]
            if hp == 0:
                lerp(b, t0, 9, TMP)
                kb.act(XWAb[0:64, :], TMP[0:64, :], AF.Tanh)
                kb.copy("act", XWAb[64:128, :], TMP[64:128, :])
                yield
                lerp(b, t0, 10, TMP)
                kb.act(SXGb, TMP, AF.Sigmoid)
                yield
                for h3 in range(3):
                    lerp(b, t0, 6 + h3, V3[h3])
                    if l == 0:
                        kb.copy("act", VF[h3][b][:, ts_], V3[h3])
                    else:
                        kb.copy("act", V3b[h3], V3[h3])
                    yield
                if l >= 1:
                    pv1 = next_ps()
                    for h3 in range(3):
                        kb.mm(pv1[0:32, 0:SW], VW1[:, h3, :], V3b[h3], start=(h3 == 0), stop=(h3 == 2))
                    kb.copy("act", V1b[0:32, :], pv1[0:32, 0:SW])
                    yield
            v = V3[hp]
            hc = slice(hp * 128, (hp + 1) * 128)
            if l >= 1:
                pv2 = next_ps()
                kb.mm(pv2[:, 0:SW], VW2[0:32, hc], V1b[0:32, :])
                kb.act(TMP, pv2[:, 0:SW], AF.Sigmoid, bias=pc(l, "rw_vresb", hp))
                kb.tt("dve", KPf, VF[hp][b][:, ts_], v, ALU.subtract)
                kb.tt("dve", KPf, KPf, TMP, ALU.mult)
                kb.tt("dve", v, v, KPf, ALU.add)
                yield
            lerp(b, t0, hp, Rf)
            yield
            lerp(b, t0, 3 + hp, Kf)
            yield
            pw = next_ps()
            kb.mm(pw[:, 0:SW], WA[0:64, hc], XWAb[0:64, :])
            kb.act(SGW, pw[:, 0:SW], AF.Sigmoid, bias=pc(l, "rw_wbias", hp))
            pa = next_ps()
            kb.mm(pa[:, 0:SW], WA[64:128, hc], XWAb[64:128, :])
            kb.act(Aa, pa[:, 0:SW], AF.Sigmoid, bias=pc(l, "rw_abias", hp))
            pg = next_ps()
            kb.mm(pg[:, 0:SW], GUP[:, hc], SXGb)
            kb.copy("act", Gg, pg[:, 0:SW])
            yield
            kb.ts("dve", KKf, Kf, pc(l, "rw_kk", hp), ALU.mult)
            kb.act(SQb, KKf, AF.Square)
            pss = next_ps()
            kb.mm(pss[:, 0:SW], bd_b, SQb)
            kb.act(TMP, pss[:, 0:SW], AF.Sqrt, bias=1e-6)
            kb.I("dve", "reciprocal", out=TMP, in_=TMP)
            kb.tt("dve", KKf, KKf, TMP, ALU.mult)
            yield
            kb.ts("dve", TMP, Aa, pc(l, "rw_ka", hp), ALU.mult, s2=OMKA[:, hp:hp + 1], op1=ALU.add)
            kb.tt("dve", KPf, Kf, TMP, ALU.mult)
            kb.stt(BON, Rf, pc(l, "rw_rk", hp), KPf, ALU.mult, ALU.mult)
            pbs = next_ps()
            kb.mm(pbs[:, 0:SW], bd_f, BON)
            kb.tt("dve", BON, pbs[:, 0:SW], v, ALU.mult)
            yield
            kb.tt("dve", Aa, KKf, Aa, ALU.mult)
            kb.I("dve", "tensor_tensor_scan", out=GS, data0=rst, data1=SGW, initial=0.0, op0=ALU.mult, op1=ALU.add)
            kb.tt("dve", SGW, GS, SGW, ALU.subtract)
            kb.act(SGW, SGW, AF.Exp, scale=-C0)
            kb.act(E1, GS, AF.Exp, scale=-C0)
            kb.act(GS, GS, AF.Exp, scale=C0)
            yield
            kb.tt("dve", KR[:, :, 0, :], KKf.re("p (n t) -> p n t", n=NCH), SGW.re("p (n t) -> p n t", n=NCH), ALU.mult)
            kb.tt("dve", KR[:, :, 1, :], Rf.re("p (n t) -> p n t", n=NCH), E1.re("p (n t) -> p n t", n=NCH), ALU.mult)
            for n in range(NCH):
                kb.copy("act", ho["GCOL"][:, n:n + 1], E1[:, n * 128 + 127:n * 128 + 128])
            yield
            kb.tt("dve", BIGb, Aa, GS, ALU.mult)
            kb.tt("dve", KIGb, KPf, GS, ALU.mult)
            kb.copy("act", Vb, v)
            yield
            items = []
            for n in range(NCH):
                cs = slice(n * 128, (n + 1) * 128)
                for src, dst in ((BIGb, ho["BT"][n]), (KIGb, ho["KT"][n]), (Vb, ho["VT"][n])):
                    pt = next_ps_b()
                    kb.tr(pt[:, 0:128], src[:, cs], ident_b)
                    kb.copy("act", dst, pt[:, 0:128])
                yield
                for h in range(2):
                    hs = slice(h * 64, (h + 1) * 64)
                    krh = KR[hs, n, :, :].re("p j t -> p (j t)")
                    pAR = next_ps()
                    kb.mm(pAR[:, 0:256], BIGb[hs, cs], krh)
                    kb.tt("dve", ho["ARm"][n][h], pAR[:, 0:256], MSK1, ALU.mult)
                    pBR = next_ps()
                    kb.mm(pBR[:, 0:256], KIGb[hs, cs], krh)
                    kb.tt("dve", ho["BRm"][n][h], pBR[:, 0:256], MSK2, ALU.mult)
                    pA = next_ps()
                    kb.mm(pA[:, 0:128], KR[hs, n, 0, :], BIGb[hs, cs])
                    kb.tt("dve", A1[n][h], pA[:, 0:128], NML, ALU.mult)
                    items.append((ho["ARm"][n][h][:, 0:128], A1[n][h]))
                    yield
            yield from tchain_gen([it[0] for it in items], [it[1] for it in items], dict(CBUF, Sb=ho["SbA"]))

        def S(sb, hp, ho):
            b, t0 = divmod(sb * SW, TB)
            ts_ = slice(t0, t0 + SW)
            KR = ho["KR"]
            for n in range(NCH):
                cs = slice(n * 128, (n + 1) * 128)
                pZh = [next_ps(), next_ps()]
                for h in range(2):
                    hs = slice(h * 64, (h + 1) * 64)
                    kb.mm(pZh[h][:, 0:64], KR[hs, n, 0, :], Mb[hp][hs, :], start=True, stop=False)
                    kb.mm(pZh[h][:, 0:64], ho["BRm"][n][h][:, 0:128], ho["VT"][n][:, hs], start=False, stop=True)
                yield
                pU = next_ps()
                for h in range(2):
                    hs = slice(h * 64, (h + 1) * 64)
                    kb.act(Zn[h], pZh[h][:, 0:64], AF.Copy, scale=-1.0)
                    kb.mm(pU[:, hs], ho["SbA"][:, (n * 2 + h) * 128:(n * 2 + h + 1) * 128], Zn[h])
                yield
                pYh = [next_ps(), next_ps()]
                pM = next_ps()
                for h in range(2):
                    hs = slice(h * 64, (h + 1) * 64)
                    kb.copy("act" if h else "dve", Ub[h], pU[:, hs])
                yield
                for h in range(2):
                    hs = slice(h * 64, (h + 1) * 64)
                    kb.mm(pYh[h][:, 0:64], KR[hs, n, 1, :], Mb[hp][hs, :], start=True, stop=False)
                    kb.mm(pYh[h][:, 0:64], ho["ARm"][n][h][:, 128:256], Ub[h], start=False, stop=False)
                    kb.mm(pYh[h][:, 0:64], ho["BRm"][n][h][:, 128:256], ho["VT"][n][:, hs], start=False, stop=True)
                for h in range(2):
                    hs = slice(h * 64, (h + 1) * 64)
                    kb.mm(pM[hs, 0:64], ho["BT"][n][:, hs], Ub[h], start=True, stop=False)
                    kb.mm(pM[hs, 0:64], ho["KT"][n][:, hs], ho["VT"][n][:, hs], start=False, stop=True)
                yield
                kb.tt("dve", Mf[hp], Mf[hp], pM[:, 0:64], ALU.add)
                kb.ts("dve", Mf[hp], Mf[hp], ho["GCOL"][:, n:n + 1], ALU.mult)
                kb.copy("act", Mb[hp], Mf[hp])
                yield
                ys = YS[n % 2]
                for h in range(2):
                    hs = slice(h * 64, (h + 1) * 64)
                    kb.act(ys[:, hs], pYh[h][:, 0:64], AF.Copy, accum_out=ST[:, h:h + 1])
                kb.ts("dve", ST[:, 2:4], ST[:, 0:2], -1.0 / 64, ALU.mult)
                yield
                for h in range(2):
                    hs = slice(h * 64, (h + 1) * 64)
                    kb.ts("dve", ys[:, hs], ys[:, hs], ST[:, 2 + h:3 + h], ALU.add)
                    kb.act(SCR, ys[:, hs], AF.Square, accum_out=ST[:, 4 + h:5 + h])
                yield
                kb.act(ST[:, 4:6], ST[:, 4:6], AF.Sqrt, bias=64e-5, scale=1.0 / 64)
                kb.I("dve", "reciprocal", out=ST[:, 4:6], in_=ST[:, 4:6])
                for h in range(2):
                    hs = slice(h * 64, (h + 1) * 64)
                    kb.ts("dve", ys[:, hs], ys[:, hs], ST[:, 4 + h:5 + h], ALU.mult)
                yield
                pT = next_ps()
                kb.tr(pT[:, 0:128], ys, ident)
                kb.act(YNF[:, cs], pT[:, 0:128], AF.Identity, scale=pc(l, "rw_lng", hp), bias=pc(l, "rw_lnb", hp))
                yield
            kb.tt("dve", YNF, YNF, ho["BON"], ALU.add)
            kb.tt("dve", YR[hp], YNF, ho["Gg"], ALU.mult)
            yield
            if hp == 2:
                for d in range(KC):
                    po = next_ps()
                    for c in range(3):
                        kb.mm(po[:, 0:SW], WOUT[:, c, d * 128:(d + 1) * 128], YR[c], start=(c == 0), stop=(c == 2))
                    kb.tt("dve", XT[d][b][:, ts_], po[:, 0:SW], XT[d][b][:, ts_], ALU.add)
                    yield

        iters = [(sb, hp) for sb in range(Tn // SW) for hp in range(3)]
        drain(P(iters[0][0], iters[0][1], HO[0]))
        for i, (sb, hp) in enumerate(iters):
            gS = S(sb, hp, HO[i % 2])
            gP = P(iters[i + 1][0], iters[i + 1][1], HO[(i + 1) % 2]) if i + 1 < len(iters) else None
            interleave(gS, gP)

    def mixer(l):
        rmsnorm(l, "mix_norm", h_out, SQ=MSQ, RS=MRS)
        sel = dbg.get("MIXERS", "lru,rwkv,gdn").split(",")
        if not (stages in ("all", "mix", "lru") and (stages == "lru" or "lru" in sel)):
            kb.barrier()
        if stages in ("all", "mix", "lru") and (stages == "lru" or "lru" in sel):
            mixer_lru(l)
            kb.barrier()
        if stages in ("all", "mix", "rwkv") and (stages == "rwkv" or "rwkv" in sel):
            mixer_rwkv(l)
            kb.barrier()
        if stages in ("all", "mix", "gdn") and (stages == "gdn" or "gdn" in sel):
            mixer_gdn(l)
            kb.barrier()

    if need_mix:
        MSQ = carve(ARENA - 8192 - 4096, 8192, BF16).re("p (k n) -> p k n", k=KC)
        MRS = [carve(ARENA - 4096 + i * 2048, 2048, F32) for i in range(2)]

    for l in range(Ln):
        if stages in ("io", "io0"):
            break
        if stages == "norm":
            rmsnorm(l, "ffn1_norm", h_out)
            break
        if need_ffn:
            ffn(l, 0)
            if stages == "ffn1":
                break
            kb.barrier()
        if need_mix:
            mixer(l)
            kb.barrier()
        if need_ffn:
            ffn(l, 1)
            if not need_mix or l == Ln - 1:
                kb.barrier()

    kb.barrier()
    YT = [carve(16384 + c * 2048 * NBn, 2048 * NBn, F32).re("p (b n) -> p b n", b=NBn) for c in range(KC)]
    OS = [carve(i * 4096, 4096, F32) for i in range(4)]
    FSQ = carve(16384 + 8 * 2048 * NBn, 8192, BF16).re("p (k n) -> p k n", k=KC) if NBn < 4 else None
    if NBn == 4:
        FSQ = carve(16384 + 65536, 8192, BF16).re("p (k n) -> p k n", k=KC)
    FRS = [carve(3 * 4096 + i * 2048, 2048, F32) for i in range(2)]

    def y_out(b, c, rs, g):
        kb.stt(YT[c][:, b, :], XT[c][b], g, rs, ALU.mult, ALU.mult)

    if stages == "io0":
        for b in range(NBn):
            for c in range(KC):
                kb.copy("dve", YT[c][:, b, :], XT[c][b])
    else:
        rmsnorm(Ln - 1, "final_norm", y_out, SQ=FSQ, RS=FRS)
    kb.barrier()
    out_toks = []
    for tt_ in range(Tn // 128):
        b, o = divmod(tt_ * 128, TB)
        osb = OS[tt_ % 4]
        for half in range(2):
            p = next_ps()
            for j in range(4):
                c = half * 4 + j
                kb.tr(p[:, j * 128:(j + 1) * 128], YT[c][:, b, o:o + 128], ident)
            kb.copy("act" if half else "dve", osb[:, half * 512:(half + 1) * 512], p)
        out_toks.append(kb.dma("sp", y_d[tt_ * 128:(tt_ + 1) * 128, :], osb))
    for t in out_toks:
        kb._wait("sp", t)
    return nc


def _prep_shared(inp, need_ffn=True, need_mix=True):
    f = lambda a: np.asarray(a, dtype=np.float32)
    pc = np.zeros((L, 128, PCOL["_n"]), np.float32)

    def put(l, name, vec, n):
        pc[l, :, PCOL[name]:PCOL[name] + n] = f(vec).reshape(n, 128).T

    for l in range(L):
        put(l, "ffn1_norm", inp["ffn1_norm"][l], 8)
        put(l, "mix_norm", inp["mix_norm"][l], 8)
        put(l, "ffn2_norm", inp["ffn2_norm"][l], 8)
        put(l, "final_norm", inp["final_norm"], 8)
        cw = f(inp["lru_conv_w"][l])
        for k in range(4):
            pc[l, :, PCOL["lru_conv_w"] + 2 * k:PCOL["lru_conv_w"] + 2 * k + 2] = cw[k].reshape(2, 128).T
        put(l, "lru_conv_b", inp["lru_conv_b"][l], 2)
        put(l, "lru_ga_b", f(inp["lru_gate_a_b"][l]).reshape(-1), 2)
        put(l, "lru_gx_b", f(inp["lru_gate_x_b"][l]).reshape(-1), 2)
        put(l, "lru_lam", inp["lru_lambda"][l], 2)
        put(l, "lru_out_g", inp["lru_out_norm"][l], 2)
        put(l, "rw_mu", inp["rwkv_mu"][l], 11)
        put(l, "rw_wbias", inp["rwkv_w_bias"][l], 3)
        put(l, "rw_abias", inp["rwkv_a_bias"][l], 3)
        put(l, "rw_kk", inp["rwkv_k_k"][l], 3)
        put(l, "rw_ka", inp["rwkv_k_a"][l], 3)
        put(l, "rw_rk", f(inp["rwkv_r_k"][l]).reshape(-1), 3)
        put(l, "rw_lng", inp["rwkv_ln_g"][l], 3)
        put(l, "rw_lnb", inp["rwkv_ln_b"][l], 3)
        if l >= 1:
            put(l, "rw_vresb", inp["rwkv_vres_b"][l - 1], 3)
        gw = f(inp["gdn_conv_w"][l])
        for k in range(4):
            pc[l, :, PCOL["gd_conv_w"] + 9 * k:PCOL["gd_conv_w"] + 9 * k + 9] = gw[k].reshape(9, 128).T
    cst = np.zeros((128, CST_N), np.float32)
    i = np.arange(128)
    cst[:, 0:128] = np.eye(128)
    cst[:, 128:256] = 1.0
    cst[:, 256:384] = (i[:, None] < i[None, :])
    cst[:, 384:512] = (i[:, None] <= i[None, :])
    cst[:, 512:640] = (i[:, None] > i[None, :])
    cst[:, 640:768] = (i[:, None] // 64 == i[None, :] // 64)
    cst[:, 768:1280] = (np.arange(512) % 128 != 0)[None, :]
    for h in range(6):
        cst[h, 1280 + h * 64:1280 + (h + 1) * 64] = 1.0
    sh = {"pcol": pc, "cst": cst}
    if need_ffn:
        for k in ("ffn1_wi", "ffn2_wi", "ffn1_wo", "ffn2_wo"):
            sh[k] = np.ascontiguousarray(f(inp[k]))
    if need_mix:
        pr = np.zeros((L, 128, PROW_N), np.float32)
        lg = np.zeros((L, 128, 4, 128), np.float32)
        wa = np.zeros((L, 128, 384), np.float32)
        for l in range(L):
            pr[l, :, 0:6] = f(inp["gdn_dt_bias"][l])[None, :]
            pr[l, :, 6:12] = f(inp["gdn_a_log"][l])[None, :]
            pr[l, :, 12:76] = f(inp["gdn_norm"][l])[None, :]
            pr[l, :, 76:140] = f(inp["gdn_norm"][l])[None, :]
            for gi, nm in enumerate(("lru_gate_a_w", "lru_gate_x_w")):
                w = f(inp[nm][l])
                for c in range(2):
                    for j in range(2):
                        lg[l, j * 64:(j + 1) * 64, gi * 2 + c, j * 64:(j + 1) * 64] = w[2 * c + j]
            wa[l, 0:64] = f(inp["rwkv_w_up"][l])
            wa[l, 64:128] = f(inp["rwkv_a_up"][l])
        sh.update({"prow": pr, "lru_g": lg, "rw_wa": wa})
        for k in ("w_in", "w_out", "rwkv_g_up", "rwkv_vres_w1", "rwkv_vres_w2"):
            sh[k] = np.ascontiguousarray(f(inp[k]))
    return sh


def kernel(**inputs):
    x = np.ascontiguousarray(inputs["x"], dtype=np.float32)
    B = x.shape[0]
    sh = _prep_shared(inputs)
    nc = bass.Bass("TRN2", target_bir_lowering=False)
    build(nc)
    in_maps = []
    for b in range(B):
        m = dict(sh)
        m["x"] = x[b]
        in_maps.append(m)
    res = run_bass_kernel_spmd(nc, in_maps, core_ids=list(range(B)))
    return np.stack([np.asarray(r["y"]) for r in res.results], axis=0).astype(np.float32)
```
